# Optimizing a Trainium2 kernel written in Bass

```python
import math
import numpy as np
import jax
import jax.numpy as jnp
from jax import lax


D_MODEL = 1024
BATCH = 8
SEQ = 4096
DEPTH = 2

PLE_DIM = 256
GRID_W = 64
BRANCH_W = 512
N_BRANCH = 4
N_IN = 10 * BRANCH_W + N_BRANCH * D_MODEL
NORM_EPS = 1e-6
LRU_BLOCKS = 8
LRU_BLOCK_DIM = BRANCH_W // LRU_BLOCKS
LRU_CONV_W = 4
LRU_C = 8.0
NA_HEADS = 8
NA_HEAD_DIM = BRANCH_W // NA_HEADS
NA_ROWS_MAX = 8
NA_COLS = 16
NA_COL_BLOCKS = GRID_W // NA_COLS
NA_BAND = 2 * NA_COLS
SSM_GROUP = 16
SSM_GROUPS = BRANCH_W // SSM_GROUP
SSM_STATE = 64
POOL_WINDOWS = (2, 4, 8, 16)
POOL_GROUP = BRANCH_W // 4

kernel_name = 'hybrid_lru_natten_s5_pool_encoder'


def rms_norm(x, g):
    x32 = x.astype(jnp.float32)
    y = x32 * lax.rsqrt(jnp.mean(x32 * x32, axis=-1, keepdims=True) + NORM_EPS)
    return (y * g.astype(jnp.float32)).astype(x.dtype)


def _linear_combine(left, right):
    a1, b1 = left
    a2, b2 = right
    return a1 * a2, a2 * b1 + b2


def _complex_combine(left, right):
    a1r, a1i, b1r, b1i = left
    a2r, a2i, b2r, b2i = right
    return (a1r * a2r - a1i * a2i,
            a1r * a2i + a1i * a2r,
            a2r * b1r - a2i * b1i + b2r,
            a2r * b1i + a2i * b1r + b2i)


def _rglru_direction(xc, w_r, b_r, w_i, b_i, lam, reverse):
    bsz, s, w = xc.shape
    xb = xc.reshape(bsz, s, LRU_BLOCKS, LRU_BLOCK_DIM)
    r = jax.nn.sigmoid(jnp.einsum('bsnc,ncd->bsnd', xb, w_r).reshape(bsz, s, w) + b_r)
    gi = jax.nn.sigmoid(jnp.einsum('bsnc,ncd->bsnd', xb, w_i).reshape(bsz, s, w) + b_i)
    log_a = -LRU_C * r * jax.nn.softplus(-lam)
    a = jnp.exp(log_a)
    b = jnp.sqrt(-jnp.expm1(2.0 * log_a)) * (gi * xc)
    _, h = lax.associative_scan(_linear_combine, (a, b), axis=1, reverse=reverse)
    return h


def rglru_mixer(u, conv_w, conv_b, w_r, b_r, w_i, b_i, lam):
    f32 = jnp.float32
    w = u.shape[-1]
    pad_l = LRU_CONV_W // 2
    xc = lax.conv_general_dilated(u.astype(f32), conv_w.astype(f32)[:, None, :], window_strides=(1,),
                                  padding=[(pad_l, LRU_CONV_W - 1 - pad_l)],
                                  dimension_numbers=('NWC', 'WIO', 'NWC'),
                                  feature_group_count=w) + conv_b.astype(f32)
    h_f = _rglru_direction(xc, w_r[0].astype(f32), b_r[0].astype(f32), w_i[0].astype(f32),
                           b_i[0].astype(f32), lam[0].astype(f32), False)
    h_b = _rglru_direction(xc, w_r[1].astype(f32), b_r[1].astype(f32), w_i[1].astype(f32),
                           b_i[1].astype(f32), lam[1].astype(f32), True)
    return h_f + h_b


def neighbourhood_attention(q, k, v, q_gain, k_gain, rpb):
    f32 = jnp.float32
    bsz, s, w = q.shape
    rows = s // GRID_W
    kr = min(NA_ROWS_MAX, rows)
    h, dh = NA_HEADS, NA_HEAD_DIM
    qn = rms_norm(q.reshape(bsz, s, h, dh), q_gain).astype(f32) * (dh ** -0.5)
    kn = rms_norm(k.reshape(bsz, s, h, dh), k_gain).astype(f32)
    qg = qn.reshape(bsz, rows, NA_COL_BLOCKS, NA_COLS, h, dh)
    kg = kn.reshape(bsz, rows, GRID_W, h, dh)
    vg = v.astype(f32).reshape(bsz, rows, GRID_W, h, dh)
    qcol = np.arange(GRID_W).reshape(NA_COL_BLOCKS, NA_COLS)
    band_start = np.clip(qcol[:, 0] - NA_COLS // 2, 0, GRID_W - NA_BAND)
    kcol = band_start[:, None] + np.arange(NA_BAND)
    win_start = np.clip(qcol - NA_COLS // 2, 0, GRID_W - NA_COLS)
    col_mask = ((kcol[:, None, :] >= win_start[:, :, None]) &
                (kcol[:, None, :] < win_start[:, :, None] + NA_COLS))
    dc_idx = np.clip(kcol[:, None, :] - qcol[:, :, None], -(NA_COLS - 1), NA_COLS - 1) + NA_COLS - 1
    rpb32 = rpb.astype(f32)
    mask = jnp.asarray(col_mask)[:, :, None, :]

    def one_row(r):
        rs = jnp.clip(r - kr // 2, 0, rows - kr)
        k_blk = lax.dynamic_slice_in_dim(kg, rs, kr, axis=1)[:, :, kcol]
        v_blk = lax.dynamic_slice_in_dim(vg, rs, kr, axis=1)[:, :, kcol]
        q_r = lax.dynamic_index_in_dim(qg, r, axis=1, keepdims=False)
        sc = jnp.einsum('bjqhd,brjkhd->bhjqrk', q_r, k_blk)
        dr_idx = rs + jnp.arange(kr) - r + NA_ROWS_MAX - 1
        bias = rpb32[:, dr_idx][:, :, dc_idx].transpose(0, 2, 3, 1, 4)
        sc = jnp.where(mask, sc + bias[None], -1e30)
        pr = jax.nn.softmax(sc.reshape(sc.shape[:4] + (kr * NA_BAND,)), axis=-1).reshape(sc.shape)
        return jnp.einsum('bhjqrk,brjkhd->bjqhd', pr, v_blk)

    out = lax.map(one_row, jnp.arange(rows))
    return jnp.moveaxis(out, 0, 1).reshape(bsz, s, w)


def s5_mixer(u, a_re, a_im, log_dt, b_re, b_im, c_re, c_im, d_skip, glu_w, glu_b):
    f32 = jnp.float32
    bsz, s, w = u.shape
    u32 = u.astype(f32)
    ug = u32.reshape(bsz, s, SSM_GROUPS, SSM_GROUP)
    y = u32 * d_skip.astype(f32)
    for d in range(2):
        lr = jnp.minimum(a_re[d].astype(f32), -1e-4)
        li = a_im[d].astype(f32)
        dt = jnp.exp(log_dt[d].astype(f32))[:, None]
        mag = jnp.exp(lr * dt)
        abar_r = mag * jnp.cos(li * dt)
        abar_i = mag * jnp.sin(li * dt)
        nr = abar_r - 1.0
        den = lr * lr + li * li
        fr = ((nr * lr + abar_i * li) / den)[..., None]
        fi = ((abar_i * lr - nr * li) / den)[..., None]
        br_, bi_ = b_re[d].astype(f32), b_im[d].astype(f32)
        bbar_r = fr * br_ - fi * bi_
        bbar_i = fr * bi_ + fi * br_
        bu_r = jnp.einsum('bsgc,gpc->bsgp', ug, bbar_r)
        bu_i = jnp.einsum('bsgc,gpc->bsgp', ug, bbar_i)
        a_r = jnp.broadcast_to(abar_r, (1, s, SSM_GROUPS, SSM_STATE))
        a_i = jnp.broadcast_to(abar_i, (1, s, SSM_GROUPS, SSM_STATE))
        _, _, sr, si = lax.associative_scan(_complex_combine, (a_r, a_i, bu_r, bu_i),
                                            axis=1, reverse=(d == 1))
        yd = (jnp.einsum('bsgp,gcp->bsgc', sr, c_re[d].astype(f32)) -
              jnp.einsum('bsgp,gcp->bsgc', si, c_im[d].astype(f32)))
        y = y + yd.reshape(bsz, s, w)
    yg = jax.nn.gelu(y)
    return yg * jax.nn.sigmoid(yg @ glu_w.astype(f32) + glu_b.astype(f32))


def pool_mixer(u, w_pool, scale):
    f32 = jnp.float32
    bsz, s, w = u.shape
    ug = u.astype(f32).reshape(bsz, s, len(POOL_WINDOWS), POOL_GROUP)
    cs = jnp.pad(jnp.cumsum(ug, axis=1), ((0, 0), (1, 0), (0, 0), (0, 0)))
    t = jnp.arange(s)
    outs = []
    for g, win in enumerate(POOL_WINDOWS):
        lo = jnp.clip(t - win // 2, 0, s)
        hi = jnp.clip(t - win // 2 + win, 0, s)
        cnt = (hi - lo).astype(f32)[None, :, None]
        outs.append((cs[:, hi, g] - cs[:, lo, g]) / cnt - ug[:, :, g])
    pooled = jnp.stack(outs, axis=2)
    mixed = jnp.einsum('bsgc,gcd->bsgd', pooled, w_pool.astype(f32)).reshape(bsz, s, w)
    return mixed * scale.astype(f32)


def setup_inputs(seed: int = 0) -> dict:
    key = jax.random.key(seed)
    k = jax.random.split(key, 32)
    f32 = jnp.float32
    W, G, P, C = BRANCH_W, SSM_GROUPS, SSM_STATE, SSM_GROUP

    def nrm(i, shape, scale):
        return scale * jax.random.normal(k[i], shape, f32)

    u = jax.random.uniform(k[10], (DEPTH, 2, W), f32, minval=0.9, maxval=0.999)
    a0 = u ** (1.0 / LRU_C)
    lam = jnp.log(a0) - jnp.log1p(-a0)
    a_im = jnp.pi * jnp.arange(P, dtype=f32) + nrm(15, (DEPTH, 2, G, P), 0.01)
    log_dt = jax.random.uniform(k[16], (DEPTH, 2, G), f32, minval=math.log(1e-3), maxval=math.log(1e-1))
    return {
        'x': nrm(0, (BATCH, SEQ, D_MODEL), 1.0),
        'p': nrm(1, (DEPTH, BATCH, SEQ, PLE_DIM), 1.0),
        'norm_scale': 1.0 + nrm(2, (DEPTH, D_MODEL), 0.02),
        'w_in': nrm(3, (DEPTH, D_MODEL, N_IN), D_MODEL ** -0.5),
        'lru_conv_w': nrm(4, (DEPTH, LRU_CONV_W, W), LRU_CONV_W ** -0.5),
        'lru_conv_b': nrm(5, (DEPTH, W), 0.01),
        'lru_w_r': nrm(6, (DEPTH, 2, LRU_BLOCKS, LRU_BLOCK_DIM, LRU_BLOCK_DIM), LRU_BLOCK_DIM ** -0.5),
        'lru_b_r': nrm(7, (DEPTH, 2, W), 0.01),
        'lru_w_i': nrm(8, (DEPTH, 2, LRU_BLOCKS, LRU_BLOCK_DIM, LRU_BLOCK_DIM), LRU_BLOCK_DIM ** -0.5),
        'lru_b_i': nrm(9, (DEPTH, 2, W), 0.01),
        'lru_lambda': lam,
        'na_q_gain': 1.0 + nrm(11, (DEPTH, NA_HEAD_DIM), 0.02),
        'na_k_gain': 1.0 + nrm(12, (DEPTH, NA_HEAD_DIM), 0.02),
        'na_rel_bias': nrm(13, (DEPTH, NA_HEADS, 2 * NA_ROWS_MAX - 1, 2 * NA_COLS - 1), 0.02),
        'ssm_a_re': -0.5 + nrm(14, (DEPTH, 2, G, P), 0.01),
        'ssm_a_im': a_im,
        'ssm_log_dt': log_dt,
        'ssm_b_re': nrm(17, (DEPTH, 2, G, P, C), (2 * C) ** -0.5),
        'ssm_b_im': nrm(18, (DEPTH, 2, G, P, C), (2 * C) ** -0.5),
        'ssm_c_re': nrm(19, (DEPTH, 2, G, C, P), P ** -0.5),
        'ssm_c_im': nrm(20, (DEPTH, 2, G, C, P), P ** -0.5),
        'ssm_d': nrm(21, (DEPTH, W), 1.0),
        'ssm_glu_w': nrm(22, (DEPTH, W, W), W ** -0.5),
        'ssm_glu_b': nrm(23, (DEPTH, W), 0.01),
        'pool_w': nrm(24, (DEPTH, len(POOL_WINDOWS), POOL_GROUP, POOL_GROUP), POOL_GROUP ** -0.5),
        'pool_scale': 1.0 + nrm(25, (DEPTH, W), 0.02),
        'w_branch': nrm(26, (DEPTH, N_BRANCH, W, D_MODEL), W ** -0.5),
        'w_out': nrm(27, (DEPTH, D_MODEL, D_MODEL), D_MODEL ** -0.5),
        'ple_proj': nrm(28, (DEPTH, PLE_DIM, D_MODEL), PLE_DIM ** -0.5),
        'ple_gate': nrm(29, (DEPTH, D_MODEL, D_MODEL), D_MODEL ** -0.5),
    }


def reference(x, p, norm_scale, w_in, lru_conv_w, lru_conv_b, lru_w_r, lru_b_r, lru_w_i, lru_b_i,
              lru_lambda, na_q_gain, na_k_gain, na_rel_bias, ssm_a_re, ssm_a_im, ssm_log_dt,
              ssm_b_re, ssm_b_im, ssm_c_re, ssm_c_im, ssm_d, ssm_glu_w, ssm_glu_b, pool_w,
              pool_scale, w_branch, w_out, ple_proj, ple_gate):
    bsz, s, _ = x.shape
    split_points = [BRANCH_W * j for j in range(1, 11)]
    for i in range(DEPTH):
        hn = rms_norm(x, norm_scale[i])
        z = hn @ w_in[i]
        a_x, a_g, q, k, v, b_g, c_x, c_g, d_x, d_g, merge = jnp.split(z, split_points, axis=-1)
        y_a = rglru_mixer(a_x, lru_conv_w[i], lru_conv_b[i], lru_w_r[i], lru_b_r[i],
                          lru_w_i[i], lru_b_i[i], lru_lambda[i]) * jax.nn.silu(a_g)
        y_b = neighbourhood_attention(q, k, v, na_q_gain[i], na_k_gain[i], na_rel_bias[i]) * jax.nn.silu(b_g)
        y_c = s5_mixer(c_x, ssm_a_re[i], ssm_a_im[i], ssm_log_dt[i], ssm_b_re[i], ssm_b_im[i],
                       ssm_c_re[i], ssm_c_im[i], ssm_d[i], ssm_glu_w[i], ssm_glu_b[i]) * jax.nn.silu(c_g)
        y_d = pool_mixer(d_x, pool_w[i], pool_scale[i]) * jax.nn.silu(d_g)
        ys = jnp.stack([y_a, y_b, y_c, y_d], axis=2).astype(x.dtype)
        gates = jax.nn.sigmoid(merge.reshape(bsz, s, N_BRANCH, D_MODEL))
        merged = jnp.sum(jnp.einsum('bsnw,nwd->bsnd', ys, w_branch[i]) * gates, axis=2)
        x = x + (merged @ w_out[i]).astype(x.dtype)
        x = x + jax.nn.sigmoid(x @ ple_gate[i]) * (p[i] @ ple_proj[i])
    return x
```

```python
import numpy as np
from contextlib import ExitStack
import concourse.bass as bass
import concourse.mybir as mybir
from concourse.bass_utils import run_bass_kernel_spmd

F32 = mybir.dt.float32
BF16 = mybir.dt.bfloat16
AF = mybir.ActivationFunctionType
ALU = mybir.AluOpType

T = 4096
D = 1024
W = 512
NIN = 9216
NL = 2
V_CONVB, V_BR0, V_BR1, V_BI0, V_BI1, V_LAM0, V_LAM1, V_SSMD, V_GLUB, V_PSCALE, V_CW0 = range(11)
NV = 14


class Sem:
    def __init__(self, handle, is_dma):
        self.h = handle
        self.is_dma = is_dma
        self.total = 0


class Buf:
    def __init__(self, name, t=None):
        self.name = name
        self.t = t
        self.last_w = {}
        self.readers = {}
        self.dsems = {}
        self.is_psum = False

    def __getitem__(self, k):
        return self.t[k]


class Eng:
    def __init__(self, name, e, sem):
        self.name = name
        self.e = e
        self.sem = sem
        self.waited = {}


class Ctx:
    def __init__(self, nc, stack):
        self.nc = nc
        self.stack = stack
        self.nsem = 0
        self.sems = []
        self.engs = {}
        for name, e in (("pe", nc.tensor), ("act", nc.scalar), ("dve", nc.vector),
                        ("pool", nc.gpsimd), ("sp", nc.sync)):
            self.engs[name] = Eng(name, e, self.new_sem(name, False))
        self.ninstr = 0
        self.free_dma_sems = {"hw": [], "sw": []}

    def new_sem(self, name, is_dma):
        self.nsem += 1
        h = self.stack.enter_context(self.nc.semaphore("s%d_%s" % (self.nsem, name)))
        s = Sem(h, is_dma)
        self.sems.append(s)
        return s

    def dma_sem(self, kind):
        if self.free_dma_sems[kind]:
            return self.free_dma_sems[kind].pop()
        return self.new_sem("dma" + kind, True)

    def sbuf(self, name, shape, dtype, stack=None):
        self.ninstr += 1
        name = "sb%d_%s" % (self.ninstr, name)
        t = (stack or self.stack).enter_context(self.nc.sbuf_tensor(name, list(shape), dtype))
        return Buf(name, t)

    def psum(self, name, shape, dtype):
        t = self.stack.enter_context(self.nc.psum_tensor(name, list(shape), dtype))
        b = Buf(name, t)
        b.is_psum = True
        return b

    def _wait(self, eng, deps):
        for sem, val in deps.items():
            if sem.is_dma:
                val = sem.total
            if sem is eng.sem and eng.name == "pe":
                continue
            if eng.waited.get(sem, 0) >= val:
                continue
            eng.e.wait_ge(sem.h, val)
            eng.waited[sem] = val

    @staticmethod
    def _merge(d, sem, val):
        if d.get(sem, 0) < val:
            d[sem] = val

    def _deps(self, reads, writes):
        deps = {}
        for b in reads:
            for s, v in b.last_w.items():
                self._merge(deps, s, v)
            if b.is_psum:
                for s, v in b.readers.items():
                    self._merge(deps, s, v)
        for b in writes:
            for s, v in b.last_w.items():
                self._merge(deps, s, v)
            for s, v in b.readers.items():
                self._merge(deps, s, v)
        return deps

    def op(self, engname, fn, reads=(), writes=()):
        eng = self.engs[engname]
        self._wait(eng, self._deps(reads, writes))
        ins = fn(eng.e)
        eng.sem.total += 1
        ins.then_inc(eng.sem.h, 1)
        for b in writes:
            b.last_w = {eng.sem: eng.sem.total}
            b.readers = {}
        for b in reads:
            if b not in writes:
                self._merge(b.readers, eng.sem, eng.sem.total)
        self.ninstr += 1
        return ins

    def dma(self, out, in_, reads=(), writes=(), sem_buf=None, store=False, q="sp"):
        eng = self.engs[q]
        kind = "sw" if q == "pool" else "hw"
        key = ("st" if store else "ld", kind)
        if key not in sem_buf.dsems:
            sem_buf.dsems[key] = self.dma_sem(kind)
        sem = sem_buf.dsems[key]
        self._wait(eng, self._deps(reads, writes))
        ins = eng.e.dma_start(out=out, in_=in_)
        sem.total += 16
        ins.then_inc(sem.h, 16)
        for b in writes:
            b.last_w = {sem: sem.total}
            b.readers = {}
        for b in reads:
            if b not in writes:
                self._merge(b.readers, sem, sem.total)
        self.ninstr += 1
        return ins

    def barrier(self):
        allv = {s: s.total for s in self.sems if s.total > 0}
        for eng in self.engs.values():
            self._wait(eng, dict(allv))

    def release(self, bufs):
        for b in bufs:
            for (_, kind), sm in b.dsems.items():
                self.free_dma_sems[kind].append(sm)
            b.dsems = {}


def build(dbg=False, stop_after=None, skip=(), bstage=None):
    nc = bass.Bass("TRN2", target_bir_lowering=False)

    def din(name, shape, dt=F32):
        return nc.dram_tensor(name, list(shape), dt, kind="ExternalInput").ap()

    def dscr(name, shape, dt=F32):
        return nc.dram_tensor(name, list(shape), dt).ap()

    xT_d = din("xT", [D, T])
    pT_d = din("pT", [NL, 256, T])
    g_d = din("gl", [NL, 128, 8])
    win_d = din("w_in", [NL, D, NIN])
    cwbc_d = din("cwbc", [NL, 128, 4, W])
    chp_d = din("chp", [NL, 128, NV, 4])
    bd_d = din("bd", [NL, 2, 2, 4, 128, 128])
    qkg_d = din("qkg", [NL, 128, 2])
    bm_d = din("bm", [NL, 4, 128, 8, 2, 256])
    sp_d = din("ssmp", [NL, 2, 128, 3, 16])
    bpad_d = din("bpad", [NL, 2, 16, 2, 128, 128])
    cpad_d = din("cpad", [NL, 2, 16, 2, 128, 128])
    bpadT_d = din("bpadT", [NL, 2, 16, 2, 128, 128])
    ident_d = din("ident", [128, 128])
    gluw_d = din("gluw", [NL, W, W])
    poolw_d = din("poolw", [NL, 4, 128, 128])
    pedge_d = din("pedge", [128, 4, 16])
    wbr_d = din("wbr", [NL, 4, W, D])
    wout_d = din("wout", [NL, D, D])
    pproj_d = din("pproj", [NL, 256, D])
    pgate_d = din("pgate", [NL, D, D])
    outT_d = nc.dram_tensor("outT", [D, T], F32, kind="ExternalOutput").ap()
    x1T_d = nc.dram_tensor("x1T", [D, T], F32, kind="ExternalOutput").ap() if dbg else dscr("x1T", [D, T])
    ysT_d = dscr("ysT", [4, W, T], BF16)
    hfT_d = dscr("hfT", [W, T])
    ygT_d = dscr("ygT", [W, T])
    ygb_d = dscr("ygb", [W, T], BF16)
    mgT_d = dscr("mgT", [D, T], BF16)
    if dbg:
        dbg_d = nc.dram_tensor("dbg", [4, W, T], F32, kind="ExternalOutput").ap()

    with ExitStack() as st:
        c = Ctx(nc, st)
        pb = [c.psum("pb%d" % i, [128, 512], F32) for i in range(8)]
        state = {"bank": 0, "cast": 0, "nb": 8}

        def bank():
            b = pb[state["bank"] % state["nb"]]
            state["bank"] += 1
            return b

        hn = c.sbuf("hn", [128, 8, T + 4], BF16)
        gsb = c.sbuf("gsb", [128, 8], F32)
        chp = c.sbuf("chp", [128, NV, 4], F32)
        onesf = c.sbuf("onesf", [128, 128], F32)
        onesb = c.sbuf("onesb", [128, 128], BF16)
        blk1 = c.sbuf("blk1", [128, 128], BF16)
        dram_x1 = Buf("x1T_dram")
        dram_ys = Buf("ys_dram")
        dram_out = Buf("out_dram")
        dram_misc = Buf("misc_dram")
        dram_mg = Buf("mg_dram")

        c.op("pool", lambda e: e.memset(onesf[:], 1.0), writes=[onesf])
        c.op("pool", lambda e: e.memset(onesb[:], 1.0), writes=[onesb])
        c.op("pool", lambda e: e.memset(blk1[:], 0.0), writes=[blk1])
        c.op("pool", lambda e: e.memset(blk1[0:64, 0:64], 1.0), writes=[blk1])
        c.op("pool", lambda e: e.memset(blk1[64:128, 64:128], 1.0), writes=[blk1])
        c.op("pool", lambda e: e.memset(hn[:, :, 0:2], 0.0), writes=[hn])
        c.op("pool", lambda e: e.memset(hn[:, :, T + 2:T + 4], 0.0), writes=[hn])

        def cast_eng():
            state["cast"] += 1
            return ("dve", "pool")[state["cast"] % 2]

        def load_w(dst, dst_ap, src_ap, nkt, ncols, fold_g=False, eng=None):
            c.dma(dst_ap, src_ap, writes=[dst], sem_buf=dst, q="pool")

        def win_cols(L, c0, n):
            return win_d[L, :, c0:c0 + n].rearrange("(kt p) n -> p kt n", p=128)

        def mm_chunks(wlist, t0, ntok, evac):
            for a in range(t0, t0 + ntok, 512):
                n = min(512, t0 + ntok - a)
                bk = bank()
                tot = sum(w_[2] for w_ in wlist)
                i = 0
                for (wbuf, lf, nkt, rf, rbuf) in wlist:
                    for kt in range(nkt):
                        c.op("pe", lambda e, lf=lf, rf=rf, kt=kt, i=i: e.matmul(
                            bk[:, 0:n], lhsT=lf(kt), rhs=rf(kt, a, n), start=(i == 0), stop=(i == tot - 1)),
                            reads=[wbuf, rbuf], writes=[bk])
                        i += 1
                evac(bk, a, n)

        def hn_rhs(shift=0):
            return lambda kt, a, n: hn[:, kt, 2 + a + shift:2 + a + shift + n]

        for L in range(NL):
            xsrc_d = xT_d if L == 0 else x1T_d
            xdst_d = x1T_d if L == 0 else outT_d
            c.dma(gsb[:], g_d[L], writes=[gsb], sem_buf=gsb)
            c.dma(chp[:], chp_d[L], writes=[chp], sem_buf=chp)
            with ExitStack() as ph:
                xk = [c.sbuf("xk%d" % i, [128, T], F32, ph) for i in range(2)]
                sq = [c.sbuf("sq%d" % i, [128, T], F32, ph) for i in range(2)]
                rstd = c.sbuf("rstd", [128, T], F32, ph)
                for kt in range(8):
                    xb = xk[kt % 2]
                    sb_ = sq[kt % 2]
                    c.dma(xb[:], xsrc_d[kt * 128:(kt + 1) * 128, :], reads=([dram_x1] if L > 0 else []), writes=[xb], sem_buf=xb,
                          q=("sp", "pool")[kt % 2])
                    c.op("act", lambda e, xb=xb, sb_=sb_: e.activation(out=sb_[:], in_=xb[:], func=AF.Square),
                         reads=[xb], writes=[sb_])
                    for ch in range(8):
                        c.op("pe", lambda e, ch=ch, sb_=sb_, kt=kt: e.matmul(
                            pb[ch][:], lhsT=onesf[:], rhs=sb_[:, ch * 512:(ch + 1) * 512], start=(kt == 0), stop=(kt == 7)),
                            reads=[onesf, sb_], writes=[pb[ch]])
                epsb = c.sbuf("epsb", [128, 1], F32, ph)
                c.op("pool", lambda e: e.memset(epsb[:], 1e-6), writes=[epsb])
                for ch in range(8):
                    sl = slice(ch * 512, (ch + 1) * 512)
                    c.op("act", lambda e, ch=ch, sl=sl: e.activation(out=rstd[:, sl], in_=pb[ch][:], func=AF.Ln,
                                                                     scale=1.0 / D, bias=epsb[:, 0:1]),
                         reads=[pb[ch], epsb], writes=[rstd])
                c.op("act", lambda e: e.activation(out=rstd[:], in_=rstd[:], func=AF.Exp, scale=-0.5), reads=[rstd], writes=[rstd])
                for kt in range(8):
                    xb = xk[kt % 2]
                    c.dma(xb[:], xsrc_d[kt * 128:(kt + 1) * 128, :], reads=([dram_x1] if L > 0 else []), writes=[xb], sem_buf=xb,
                          q=("sp", "pool")[kt % 2])
                    c.op("dve", lambda e, xb=xb, kt=kt: e.scalar_tensor_tensor(
                        out=hn[:, kt, 2:2 + T], in0=xb[:], scalar=gsb[:, kt:kt + 1], in1=rstd[:], op0=ALU.mult, op1=ALU.mult),
                        reads=[xb, rstd, gsb], writes=[hn])
                c.barrier()
                c.release(xk + sq + [rstd, epsb])

            TS = 1024
            with ExitStack() as ph:
                wzs = [c.sbuf("wz%d" % i, [128, 8, 128], BF16, ph) for i in range(2)]
                wgs = [c.sbuf("wg%d" % i, [128, 8, 128], BF16, ph) for i in range(2)]
                bdws = [c.sbuf("bdw%d" % i, [128, 4, 128], BF16, ph) for i in range(2)]

                def loadA(ct_):
                    load_w(wzs[ct_ % 2], wzs[ct_ % 2][:], win_cols(L, 0 * W + ct_ * 128, 128), 8, 128)
                    load_w(wgs[ct_ % 2], wgs[ct_ % 2][:], win_cols(L, 1 * W + ct_ * 128, 128), 8, 128)
                    for d__ in range(2):
                        for gt_ in range(2):
                            c.dma(bdws[ct_ % 2][:, d__ * 2 + gt_, :], bd_d[L, d__, gt_, ct_], writes=[bdws[ct_ % 2]],
                                  sem_buf=bdws[ct_ % 2], q="pool")
                if 'A' not in skip:
                    loadA(0)
                cp = c.sbuf("cp", [128, 4], F32, ph)
                XCF = c.sbuf("XCF", [128, T], F32, ph)
                XCBF = c.sbuf("XCBF", [128, T], BF16, ph)
                R = c.sbuf("R", [128, T], F32, ph)
                GI = c.sbuf("GI", [128, T], F32, ph)
                A2 = c.sbuf("A2", [128, T + 4], F32, ph)
                HF = c.sbuf("HF", [128, T], F32, ph)
                H = c.sbuf("H", [128, TS], F32, ph)
                SG = c.sbuf("SG", [128, TS], F32, ph)
                YS = c.sbuf("YS", [128, TS], BF16, ph)
                carry = c.sbuf("carry", [128, 1], F32, ph)
                for ct in (range(4) if 'A' not in skip else []):
                    wz, wg, bdw = wzs[ct % 2], wgs[ct % 2], bdws[ct % 2]
                    if ct + 1 < 4:
                        loadA(ct + 1)
                    for d_ in range(2):
                        c.op("act", lambda e, d_=d_: e.activation(out=cp[:, 2 * d_:2 * d_ + 1], in_=chp[:, V_LAM0 + d_, ct:ct + 1],
                                                                  func=AF.Exp, scale=-1.0), reads=[chp], writes=[cp])
                        c.op("act", lambda e, d_=d_: e.activation(out=cp[:, 2 * d_:2 * d_ + 1], in_=cp[:, 2 * d_:2 * d_ + 1],
                                                                  func=AF.Ln, bias=1.0, scale=1.0), reads=[cp], writes=[cp])
                        c.op("dve", lambda e, d_=d_: e.tensor_scalar(out=cp[:, 2 * d_ + 1:2 * d_ + 2], in0=cp[:, 2 * d_:2 * d_ + 1],
                                                                     scalar1=-16.0, scalar2=None, op0=ALU.mult), reads=[cp], writes=[cp])
                        c.op("dve", lambda e, d_=d_: e.tensor_scalar(out=cp[:, 2 * d_:2 * d_ + 1], in0=cp[:, 2 * d_:2 * d_ + 1],
                                                                     scalar1=-8.0, scalar2=None, op0=ALU.mult), reads=[cp], writes=[cp])
                    Z = A2
                    c.op("pool", lambda e: e.memset(Z[:, 0:2], 0.0), writes=[Z])
                    c.op("pool", lambda e: e.memset(Z[:, T + 2:T + 4], 0.0), writes=[Z])

                    def ev_z0(bk, a, n):
                        c.op("act", lambda e: e.activation(out=Z[:, 2 + a:2 + a + n], in_=bk[:, 0:n], func=AF.Identity), reads=[bk], writes=[Z])
                    mm_chunks([(wz, (lambda kt: wz[:, kt, :]), 8, hn_rhs(0), hn)], 0, T, ev_z0)
                    cwv = lambda k: chp[:, V_CW0 + k, ct:ct + 1]
                    c.op("dve", lambda e: e.tensor_scalar(out=XCF[:], in0=Z[:, 0:T], scalar1=cwv(0), scalar2=chp[:, V_CONVB, ct:ct + 1],
                                                          op0=ALU.mult, op1=ALU.add), reads=[Z, chp], writes=[XCF])
                    for k in range(1, 4):
                        c.op("dve", lambda e, k=k: e.scalar_tensor_tensor(
                            out=XCF[:], in0=Z[:, k:k + T], scalar=cwv(k), in1=XCF[:], op0=ALU.mult, op1=ALU.add),
                            reads=[Z, chp, XCF], writes=[XCF])
                    c.op("pool", lambda e: e.tensor_copy(out=XCBF[:], in_=XCF[:]), reads=[XCF], writes=[XCBF])

                    for d_ in range(2):
                        def ev_gate(dst, vidx):
                            def f(bk, a, n):
                                c.op("act", lambda e: e.activation(out=dst[:, a:a + n], in_=bk[:, 0:n], func=AF.Sigmoid,
                                                                   bias=chp[:, vidx, ct:ct + 1], scale=1.0),
                                     reads=[bk, chp], writes=[dst])
                            return f
                        for gt, dst, vidx in ((0, R, V_BR0 + d_), (1, GI, V_BI0 + d_)):
                            mm_chunks([(bdw, (lambda kt, gt=gt: bdw[:, d_ * 2 + gt, :]), 1,
                                        (lambda kt, a, n: XCBF[:, a:a + n]), XCBF)], 0, T, ev_gate(dst, vidx))
                        c.op("act", lambda e: e.activation(out=A2[:, 0:T], in_=R[:], func=AF.Exp, scale=cp[:, 2 * d_ + 1:2 * d_ + 2]),
                             reads=[R, cp], writes=[A2])
                        c.op("act", lambda e: e.activation(out=R[:], in_=R[:], func=AF.Exp, scale=cp[:, 2 * d_:2 * d_ + 1]),
                             reads=[R, cp], writes=[R])
                        c.op("act", lambda e: e.activation(out=A2[:, 0:T], in_=A2[:, 0:T], func=AF.Sqrt, scale=-1.0, bias=1.0),
                             reads=[A2], writes=[A2])
                        c.op("dve", lambda e: e.tensor_tensor(out=GI[:], in0=GI[:], in1=XCF[:], op=ALU.mult), reads=[GI, XCF], writes=[GI])
                        c.op("pool", lambda e: e.tensor_tensor(out=GI[:], in0=GI[:], in1=A2[:, 0:T], op=ALU.mult), reads=[GI, A2], writes=[GI])
                        c.op("pool", lambda e: e.memset(carry[:], 0.0), writes=[carry])
                        if d_ == 0:
                            for si in range(T // TS):
                                sl = slice(si * TS, (si + 1) * TS)
                                c.op("dve", lambda e, sl=sl: e.tensor_tensor_scan(out=HF[:, sl], data0=R[:, sl], data1=GI[:, sl],
                                                                                  initial=carry[:, 0:1], op0=ALU.mult, op1=ALU.add),
                                     reads=[R, GI, carry], writes=[HF])
                                c.op("dve", lambda e, sl=sl: e.tensor_copy(out=carry[:], in_=HF[:, sl.stop - 1:sl.stop]), reads=[HF], writes=[carry])
                        else:
                            for si in reversed(range(T // TS)):
                                t0 = si * TS
                                sl = slice(t0, t0 + TS)
                                c.op("dve", lambda e, sl=sl: e.tensor_tensor_scan(out=H[:, ::-1], data0=R[:, sl][:, ::-1], data1=GI[:, sl][:, ::-1],
                                                                                  initial=carry[:, 0:1], op0=ALU.mult, op1=ALU.add),
                                     reads=[R, GI, carry], writes=[H])
                                c.op("dve", lambda e: e.tensor_copy(out=carry[:], in_=H[:, 0:1]), reads=[H], writes=[carry])

                                def ev_sg(bk, a, n, t0=t0):
                                    o = a - t0
                                    c.op("act", lambda e: e.activation(out=SG[:, o:o + n], in_=bk[:, 0:n], func=AF.Silu), reads=[bk], writes=[SG])
                                mm_chunks([(wg, (lambda kt: wg[:, kt, :]), 8, hn_rhs(0), hn)], t0, TS, ev_sg)
                                c.op("pool", lambda e, sl=sl: e.tensor_tensor(out=H[:], in0=HF[:, sl], in1=H[:], op=ALU.add), reads=[HF, H], writes=[H])
                                if dbg:
                                    c.dma(dbg_d[0, ct * 128:(ct + 1) * 128, t0:t0 + TS], H[:], reads=[H], writes=[dram_misc], sem_buf=H, store=True)
                                c.op("dve", lambda e: e.tensor_tensor(out=YS[:], in0=H[:], in1=SG[:], op=ALU.mult), reads=[H, SG], writes=[YS])
                                c.dma(ysT_d[0, ct * 128:(ct + 1) * 128, t0:t0 + TS], YS[:], reads=[YS], writes=[dram_ys], sem_buf=YS, store=True)
                c.barrier()
                c.release(wzs + wgs + bdws + [cp, XCF, XCBF, R, GI, A2, HF, H, SG, YS, carry])

            if stop_after == 'A':
                break
            with ExitStack() as ph:
                wds = [c.sbuf("wd%d" % i, [128, 8, 128], BF16, ph) for i in range(2)]
                wgds = [c.sbuf("wgd%d" % i, [128, 8, 128], BF16, ph) for i in range(2)]
                wps = [c.sbuf("wp%d" % i, [128, 128], BF16, ph) for i in range(2)]

                def loadD(g_):
                    load_w(wds[g_ % 2], wds[g_ % 2][:], win_cols(L, 8 * W + g_ * 128, 128), 8, 128)
                    load_w(wgds[g_ % 2], wgds[g_ % 2][:], win_cols(L, 9 * W + g_ * 128, 128), 8, 128)
                    c.dma(wps[g_ % 2][:], poolw_d[L, g_], writes=[wps[g_ % 2]], sem_buf=wps[g_ % 2], q="pool")
                if 'D' not in skip:
                    loadD(0)
                pedge = c.sbuf("pedge", [128, 4, 16], F32, ph)
                U = c.sbuf("U", [128, T + 16], F32, ph)
                S1 = c.sbuf("S1", [128, T + 16], F32, ph)
                S2 = c.sbuf("S2", [128, T + 16], F32, ph)
                PB = c.sbuf("PB", [128, T], BF16, ph)
                YD = [c.sbuf("YD%d" % i, [128, 1024], F32, ph) for i in range(2)]
                SGd = [c.sbuf("SGd%d" % i, [128, 1024], F32, ph) for i in range(2)]
                YSd = [c.sbuf("YSd%d" % i, [128, 1024], BF16, ph) for i in range(2)]
                for gi in (range(4) if 'D' not in skip else []):
                    win_ = (2, 4, 8, 16)[gi]
                    wd, wgd, wp = wds[gi % 2], wgds[gi % 2], wps[gi % 2]
                    if gi + 1 < 4:
                        loadD(gi + 1)
                    c.dma(pedge[:], pedge_d, writes=[pedge], sem_buf=pedge)
                    c.op("pool", lambda e: e.memset(U[:, 0:8], 0.0), writes=[U])
                    c.op("pool", lambda e: e.memset(U[:, T + 8:T + 16], 0.0), writes=[U])

                    def ev_u(bk, a, n):
                        c.op("act", lambda e: e.activation(out=U[:, 8 + a:8 + a + n], in_=bk[:, 0:n], func=AF.Identity),
                             reads=[bk], writes=[U])
                    mm_chunks([(wd, (lambda kt: wd[:, kt, :]), 8, hn_rhs(0), hn)], 0, T, ev_u)
                    c.op("dve", lambda e: e.tensor_tensor(out=S1[:, 1:T + 16], in0=U[:, 0:T + 15], in1=U[:, 1:T + 16], op=ALU.add),
                         reads=[U], writes=[S1])
                    FB, OB = S1, S2
                    if win_ >= 4:
                        c.op("pool", lambda e: e.tensor_tensor(out=S2[:, 2:T + 15], in0=S1[:, 1:T + 14], in1=S1[:, 3:T + 16], op=ALU.add),
                             reads=[S1], writes=[S2])
                        FB, OB = S2, S1
                    if win_ >= 8:
                        c.op("dve", lambda e: e.tensor_tensor(out=S1[:, 4:T + 13], in0=S2[:, 2:T + 11], in1=S2[:, 6:T + 15], op=ALU.add),
                             reads=[S2], writes=[S1])
                        FB, OB = S1, S2
                    if win_ >= 16:
                        c.op("pool", lambda e: e.tensor_tensor(out=S2[:, 8:T + 8], in0=S1[:, 4:T + 4], in1=S1[:, 12:T + 12], op=ALU.add),
                             reads=[S1], writes=[S2])
                        FB, OB = S2, S1
                    c.op("dve", lambda e: e.scalar_tensor_tensor(out=OB[:, 8:8 + T], in0=FB[:, 8:8 + T], scalar=1.0 / win_,
                                                                 in1=U[:, 8:8 + T], op0=ALU.mult, op1=ALU.subtract),
                         reads=[FB, U], writes=[OB])
                    for (i0, e0) in ((8, 0), (T, 8)):
                        c.op("dve", lambda e, i0=i0, e0=e0: e.tensor_tensor(out=OB[:, i0:i0 + 8], in0=FB[:, i0:i0 + 8],
                                                                            in1=pedge[:, gi, e0:e0 + 8], op=ALU.mult),
                             reads=[FB, pedge], writes=[OB])
                        c.op("dve", lambda e, i0=i0: e.tensor_tensor(out=OB[:, i0:i0 + 8], in0=OB[:, i0:i0 + 8],
                                                                     in1=U[:, i0:i0 + 8], op=ALU.subtract),
                             reads=[OB, U], writes=[OB])
                    c.op("pool", lambda e: e.tensor_copy(out=PB[:], in_=OB[:, 8:8 + T]), reads=[OB], writes=[PB])
                    for si in range(4):
                        t0 = si * 1024
                        yd_, sg_, ys_ = YD[si % 2], SGd[si % 2], YSd[si % 2]

                        def ev_y(bk, a, n, yd_=yd_, t0=t0):
                            c.op("act", lambda e: e.activation(out=yd_[:, a - t0:a - t0 + n], in_=bk[:, 0:n], func=AF.Identity,
                                                               scale=chp[:, V_PSCALE, gi:gi + 1]), reads=[bk, chp], writes=[yd_])

                        def ev_g(bk, a, n, sg_=sg_, t0=t0):
                            c.op("act", lambda e: e.activation(out=sg_[:, a - t0:a - t0 + n], in_=bk[:, 0:n], func=AF.Silu),
                                 reads=[bk], writes=[sg_])
                        mm_chunks([(wp, (lambda kt: wp[:]), 1, (lambda kt, a, n: PB[:, a:a + n]), PB)], t0, 1024, ev_y)
                        mm_chunks([(wgd, (lambda kt: wgd[:, kt, :]), 8, hn_rhs(0), hn)], t0, 1024, ev_g)
                        if dbg:
                            c.dma(dbg_d[3, gi * 128:(gi + 1) * 128, t0:t0 + 1024], yd_[:], reads=[yd_], writes=[dram_misc],
                                  sem_buf=yd_, store=True)
                        c.op("dve", lambda e, yd_=yd_, sg_=sg_, ys_=ys_: e.tensor_tensor(out=ys_[:], in0=yd_[:], in1=sg_[:], op=ALU.mult),
                             reads=[yd_, sg_], writes=[ys_])
                        c.dma(ysT_d[3, gi * 128:(gi + 1) * 128, t0:t0 + 1024], ys_[:], reads=[ys_], writes=[dram_ys],
                              sem_buf=ys_, store=True)
                c.barrier()
                c.release(wds + wgds + wps + [pedge, U, S1, S2, PB] + YD + SGd + YSd)
            if stop_after == 'D':
                break
            if 'Z' in skip:
                with ExitStack() as ph:
                    zb = c.sbuf("zb", [128, T], BF16, ph)
                    c.op("pool", lambda e: e.memset(zb[:], 0.0), writes=[zb])
                    for n_ in (1, 2):
                        for ct in range(4):
                            c.dma(ysT_d[n_, ct * 128:(ct + 1) * 128, :], zb[:], reads=[zb], writes=[dram_ys], sem_buf=zb, store=True)
                    c.barrier()
                    c.release([zb])
            with ExitStack() as ph:
                wc_ = c.sbuf("wc_", [128, 8, 128], BF16, ph)
                spm = c.sbuf("spm", [128, 2, 3, 16], F32, ph)
                P_LR, P_DT, P_X1, P_TH, P_RHO, P_C, P_S, P_T1, P_T2, P_T3, P_AR, P_AI, P_NR, P_DEN, P_FR, P_FI, P_NFI, P_RHO8 = range(18)
                prm = c.sbuf("prm", [128, 2, 18, 16], F32, ph)
                ck = c.sbuf("ck", [128, 2, 11, 16], F32, ph)
                sk = c.sbuf("sk", [128, 2, 11, 16], F32, ph)
                lp = c.sbuf("lp", [128, 2, 9, 2, 16], F32, ph)
                hpi = c.sbuf("hpi", [128, 1], F32, ph)
                ident = c.sbuf("ident", [128, 128], F32, ph)
                ctab = c.sbuf("ctab", [128, 4, 128], F32, ph)
                stab = c.sbuf("stab", [128, 4, 128], F32, ph)
                ttmp = c.sbuf("ttmp", [128, 128], F32, ph)
                st4 = c.sbuf("st4", [128, 4, 2, 128], F32, ph)
                dg = [c.sbuf("dg%d" % i, [128, 3, 4, 128], BF16, ph) for i in range(2)]
                nlpi = c.sbuf("nlpi", [128, 9, 4], F32, ph)
                identb = c.sbuf("identb", [128, 128], BF16, ph)
                BT = c.sbuf("BT", [128, 4, 2, 128], BF16, ph)
                G = c.sbuf("G", [128, 9, 4, 2, 128], BF16, ph)
                WI = c.sbuf("WI", [128, 8, 2, 4, 128], BF16, ph)
                KT = c.sbuf("KT", [128, 8, 128], BF16, ph)
                YACC = c.sbuf("YACC", [128, T], F32, ph)
                XD = [c.sbuf("XD%d" % i, [128, 8, 512], BF16, ph) for i in range(2)]
                SBr = c.sbuf("SBr", [128, 4, 516], BF16, ph)
                SBi = c.sbuf("SBi", [128, 4, 516], BF16, ph)
                carr = c.sbuf("carr", [128, 2, 4], F32, ph)
                ctmp = c.sbuf("ctmp", [128, 4, 4], F32, ph)
                TM = [c.sbuf("TM%d" % i, [128, 4, 128], F32, ph) for i in range(8)]
                YGB = c.sbuf("YGB", [128, T], BF16, ph)
                for ct in (range(4) if 'C' not in skip else []):
                    load_w(wc_, wc_[:], win_cols(L, 6 * W + ct * 128, 128), 8, 128, fold_g=True)
                    c.op("pool", lambda e: e.memset(hpi[:], float(np.pi / 2)), writes=[hpi])
                    c.dma(ident[:], ident_d, writes=[ident], sem_buf=ident)
                    c.op("pool", lambda e: e.tensor_copy(out=identb[:], in_=ident[:]), reads=[ident], writes=[identb])
                    cs = slice(ct * 4, ct * 4 + 4)
                    if ct == 0:
                        for d_ in range(2):
                            c.dma(spm[:, d_], sp_d[L, d_], writes=[spm], sem_buf=spm)

                    def P(d_, k):
                        return prm[:, d_, k, :]

                    def tt(o, a, b, op, eng="dve", rd=(prm,), wr=(prm,)):
                        c.op(eng, lambda e: e.tensor_tensor(out=o, in0=a, in1=b, op=op), reads=list(rd), writes=list(wr))

                    def tsc(o, a, s1, op0, s2=None, op1=None, rd=(prm,), wr=(prm,)):
                        if op1 is None:
                            c.op("dve", lambda e: e.tensor_scalar(out=o, in0=a, scalar1=s1, scalar2=None, op0=op0),
                                 reads=list(rd), writes=list(wr))
                        else:
                            c.op("dve", lambda e: e.tensor_scalar(out=o, in0=a, scalar1=s1, scalar2=s2, op0=op0, op1=op1),
                                 reads=list(rd), writes=list(wr))

                    def ev_uc(bk, a, n):
                        cc = a // 512
                        c.op("act", lambda e: e.activation(out=YACC[:, a:a + n], in_=bk[:, 0:n], func=AF.Identity,
                                                           scale=chp[:, V_SSMD, ct:ct + 1]), reads=[bk, chp], writes=[YACC])
                        src = bk[:, 0:512].rearrange("p (m i) -> p i m", i=8)
                        c.op("act", lambda e: e.activation(out=XD[0][:, :, 64 * cc:64 * cc + 64], in_=src, func=AF.Identity),
                             reads=[bk], writes=[XD[0]])
                    mm_chunks([(wc_, (lambda kt: wc_[:, kt, :]), 8, hn_rhs(0), hn)], 0, T, ev_uc)
                    c.op("pool", lambda e: e.tensor_copy(out=XD[1][:], in_=XD[0][:, ::-1, ::-1]), reads=[XD[0]], writes=[XD[1]])

                    for d_ in ([slice(0, 2)] if ct == 0 else []):
                        tsc(P(d_, P_LR), spm[:, d_, 0, :], -1e-4, ALU.min, rd=(spm,))
                        c.op("act", lambda e: e.activation(out=P(d_, P_DT), in_=spm[:, d_, 2, :], func=AF.Exp), reads=[spm], writes=[prm])
                        tt(P(d_, P_X1), P(d_, P_LR), P(d_, P_DT), ALU.mult)
                        tt(P(d_, P_TH), spm[:, d_, 1, :], P(d_, P_DT), ALU.mult, rd=(prm, spm))
                        c.op("act", lambda e: e.activation(out=P(d_, P_RHO), in_=P(d_, P_X1), func=AF.Exp), reads=[prm], writes=[prm])
                        c.op("act", lambda e: e.activation(out=P(d_, P_RHO8), in_=P(d_, P_X1), func=AF.Exp, scale=8.0), reads=[prm], writes=[prm])
                        c.op("act", lambda e: e.activation(out=P(d_, P_S), in_=P(d_, P_TH), func=AF.Sin, scale=1.0 / 64), reads=[prm], writes=[prm])
                        c.op("act", lambda e: e.activation(out=P(d_, P_C), in_=P(d_, P_TH), func=AF.Sin, scale=1.0 / 64,
                                                           bias=hpi[:, 0:1]), reads=[prm, hpi], writes=[prm])
                        for it in range(6):
                            tt(P(d_, P_T1), P(d_, P_C), P(d_, P_S), ALU.mult)
                            tt(P(d_, P_T2), P(d_, P_C), P(d_, P_C), ALU.mult)
                            tt(P(d_, P_T3), P(d_, P_S), P(d_, P_S), ALU.mult)
                            tt(P(d_, P_C), P(d_, P_T2), P(d_, P_T3), ALU.subtract)
                            tsc(P(d_, P_S), P(d_, P_T1), 2.0, ALU.mult)
                        c.op("dve", lambda e: e.tensor_copy(out=ck[:, d_, 0, :], in_=P(d_, P_C)), reads=[prm], writes=[ck])
                        c.op("dve", lambda e: e.tensor_copy(out=sk[:, d_, 0, :], in_=P(d_, P_S)), reads=[prm], writes=[sk])
                        for k in range(1, 11):
                            tt(P(d_, P_T1), ck[:, d_, k - 1, :], sk[:, d_, k - 1, :], ALU.mult, rd=(ck, sk))
                            tt(P(d_, P_T2), ck[:, d_, k - 1, :], ck[:, d_, k - 1, :], ALU.mult, rd=(ck,))
                            tt(P(d_, P_T3), sk[:, d_, k - 1, :], sk[:, d_, k - 1, :], ALU.mult, rd=(sk,))
                            tt(ck[:, d_, k, :], P(d_, P_T2), P(d_, P_T3), ALU.subtract, wr=(ck,))
                            tsc(sk[:, d_, k, :], P(d_, P_T1), 2.0, ALU.mult, wr=(sk,))
                        tt(P(d_, P_AR), P(d_, P_RHO), P(d_, P_C), ALU.mult)
                        tt(P(d_, P_AI), P(d_, P_RHO), P(d_, P_S), ALU.mult)
                        tsc(P(d_, P_NR), P(d_, P_AR), -1.0, ALU.add)
                        tt(P(d_, P_T1), P(d_, P_LR), P(d_, P_LR), ALU.mult)
                        tt(P(d_, P_T2), spm[:, d_, 1, :], spm[:, d_, 1, :], ALU.mult, rd=(spm,))
                        tt(P(d_, P_DEN), P(d_, P_T1), P(d_, P_T2), ALU.add)
                        c.op("dve", lambda e: e.reciprocal(out=P(d_, P_DEN), in_=P(d_, P_DEN)), reads=[prm], writes=[prm])
                        tt(P(d_, P_T1), P(d_, P_NR), P(d_, P_LR), ALU.mult)
                        tt(P(d_, P_T2), P(d_, P_AI), spm[:, d_, 1, :], ALU.mult, rd=(prm, spm))
                        tt(P(d_, P_T1), P(d_, P_T1), P(d_, P_T2), ALU.add)
                        tt(P(d_, P_FR), P(d_, P_T1), P(d_, P_DEN), ALU.mult)
                        tt(P(d_, P_T1), P(d_, P_AI), P(d_, P_LR), ALU.mult)
                        tt(P(d_, P_T2), P(d_, P_NR), spm[:, d_, 1, :], ALU.mult, rd=(prm, spm))
                        tt(P(d_, P_T1), P(d_, P_T1), P(d_, P_T2), ALU.subtract)
                        tt(P(d_, P_FI), P(d_, P_T1), P(d_, P_DEN), ALU.mult)
                        tsc(P(d_, P_NFI), P(d_, P_FI), -1.0, ALU.mult)
                        c.op("pool", lambda e: e.memset(lp[:, d_, 0, 0, :], 1.0), writes=[lp])
                        c.op("pool", lambda e: e.memset(lp[:, d_, 0, 1, :], 0.0), writes=[lp])
                        c.op("dve", lambda e: e.tensor_copy(out=lp[:, d_, 1, 0, :], in_=P(d_, P_AR)), reads=[prm], writes=[lp])
                        c.op("dve", lambda e: e.tensor_copy(out=lp[:, d_, 1, 1, :], in_=P(d_, P_AI)), reads=[prm], writes=[lp])
                        for k in range(2, 9):
                            pr_, pi__ = lp[:, d_, k - 1, 0, :], lp[:, d_, k - 1, 1, :]
                            tt(P(d_, P_T1), pr_, P(d_, P_AR), ALU.mult, rd=(lp, prm))
                            tt(P(d_, P_T2), pi__, P(d_, P_AI), ALU.mult, rd=(lp, prm))
                            tt(lp[:, d_, k, 0, :], P(d_, P_T1), P(d_, P_T2), ALU.subtract, wr=(lp,))
                            tt(P(d_, P_T1), pr_, P(d_, P_AI), ALU.mult, rd=(lp, prm))
                            tt(P(d_, P_T2), pi__, P(d_, P_AR), ALU.mult, rd=(lp, prm))
                            tt(lp[:, d_, k, 1, :], P(d_, P_T1), P(d_, P_T2), ALU.add, wr=(lp,))
                    for d_ in (range(2) if (bstage is None or bstage >= 2) else []):
                        c.op("pool", lambda e: e.memset(ctab[:, :, 0:1], 1.0), writes=[ctab])
                        c.op("pool", lambda e: e.memset(stab[:, :, 0:1], 0.0), writes=[stab])
                        for k in range(7):
                            n = 1 << k
                            cs_b = ck[:, d_, k + 3, cs].unsqueeze(2).to_broadcast([128, 4, n])
                            ss_b = sk[:, d_, k + 3, cs].unsqueeze(2).to_broadcast([128, 4, n])
                            ta, tb = TM[4], TM[5]
                            c.op("dve", lambda e: e.tensor_tensor(out=ta[:, :, 0:n], in0=stab[:, :, 0:n], in1=ss_b, op=ALU.mult),
                                 reads=[stab, sk], writes=[ta])
                            c.op("dve", lambda e: e.tensor_tensor(out=tb[:, :, 0:n], in0=ctab[:, :, 0:n], in1=cs_b, op=ALU.mult),
                                 reads=[ctab, ck], writes=[tb])
                            c.op("dve", lambda e: e.tensor_tensor(out=ctab[:, :, n:2 * n], in0=tb[:, :, 0:n], in1=ta[:, :, 0:n], op=ALU.subtract),
                                 reads=[ta, tb], writes=[ctab])
                            tc2, td2 = TM[6], TM[7]
                            c.op("dve", lambda e: e.tensor_tensor(out=tc2[:, :, 0:n], in0=ctab[:, :, 0:n], in1=ss_b, op=ALU.mult),
                                 reads=[ctab, sk], writes=[tc2])
                            c.op("dve", lambda e: e.tensor_tensor(out=td2[:, :, 0:n], in0=stab[:, :, 0:n], in1=cs_b, op=ALU.mult),
                                 reads=[stab, ck], writes=[td2])
                            c.op("dve", lambda e: e.tensor_tensor(out=stab[:, :, n:2 * n], in0=td2[:, :, 0:n], in1=tc2[:, :, 0:n], op=ALU.add),
                                 reads=[tc2, td2], writes=[stab])
                        c.dma(BT[:], bpadT_d[L, d_, ct * 4:(ct + 1) * 4].rearrange("s r p c -> p s r c"), writes=[BT], sem_buf=BT, q="pool")
                        c.dma(st4[:], cpad_d[L, d_, ct * 4:(ct + 1) * 4].rearrange("s r p c -> p s r c"), writes=[st4], sem_buf=st4)
                        fr_b, fi_b, nfi_b = (prm[:, d_, k_, cs].unsqueeze(2).to_broadcast([128, 4, 128]) for k_ in (P_FR, P_FI, P_NFI))
                        ta, tb, tc2, td2 = TM[4], TM[5], TM[6], TM[7]
                        c.op("dve", lambda e: e.tensor_tensor(out=ta[:], in0=st4[:, :, 1, :], in1=fi_b, op=ALU.mult), reads=[st4, prm], writes=[ta])
                        c.op("dve", lambda e: e.tensor_tensor(out=tb[:], in0=st4[:, :, 0, :], in1=fr_b, op=ALU.mult), reads=[st4, prm], writes=[tb])
                        c.op("pool", lambda e: e.tensor_tensor(out=G[:, 0, :, 0, :], in0=tb[:], in1=ta[:], op=ALU.subtract), reads=[ta, tb], writes=[G])
                        c.op("dve", lambda e: e.tensor_tensor(out=tc2[:], in0=st4[:, :, 1, :], in1=fr_b, op=ALU.mult), reads=[st4, prm], writes=[tc2])
                        c.op("dve", lambda e: e.tensor_tensor(out=td2[:], in0=st4[:, :, 0, :], in1=nfi_b, op=ALU.mult), reads=[st4, prm], writes=[td2])
                        c.op("pool", lambda e: e.tensor_tensor(out=G[:, 0, :, 1, :], in0=td2[:], in1=tc2[:], op=ALU.subtract), reads=[tc2, td2], writes=[G])
                        tsc(nlpi[:, :, :], lp[:, d_, :, 1, cs], -1.0, ALU.mult, rd=(lp,), wr=(nlpi,))
                        idb = ident[:].unsqueeze(1).to_broadcast([128, 4, 128])
                        for k in range(0, 9):
                            dset = dg[k % 2]
                            if k >= 1:
                                for q_, src_ in ((0, lp[:, d_, k, 0, cs]), (1, lp[:, d_, k, 1, cs]), (2, nlpi[:, k, :])):
                                    c.op("dve", lambda e, q_=q_, src_=src_, dset=dset: e.tensor_tensor(
                                        out=dset[:, q_], in0=idb, in1=src_.unsqueeze(2).to_broadcast([128, 4, 128]), op=ALU.mult),
                                        reads=[ident, lp, nlpi], writes=[dset])
                                bkr, bki = bank(), bank()
                                for st_ in range(4):
                                    sl_ = slice(st_ * 128, (st_ + 1) * 128)
                                    c.op("pe", lambda e, st_=st_, sl_=sl_: e.matmul(bkr[:, sl_], lhsT=dset[:, 0, st_, :], rhs=G[:, 0, st_, 0, :],
                                                                                   start=True, stop=False), reads=[dset, G], writes=[bkr])
                                    c.op("pe", lambda e, st_=st_, sl_=sl_: e.matmul(bkr[:, sl_], lhsT=dset[:, 1, st_, :], rhs=G[:, 0, st_, 1, :],
                                                                                   start=False, stop=True), reads=[dset, G], writes=[bkr])
                                for st_ in range(4):
                                    sl_ = slice(st_ * 128, (st_ + 1) * 128)
                                    c.op("pe", lambda e, st_=st_, sl_=sl_: e.matmul(bki[:, sl_], lhsT=dset[:, 0, st_, :], rhs=G[:, 0, st_, 1, :],
                                                                                   start=True, stop=False), reads=[dset, G], writes=[bki])
                                    c.op("pe", lambda e, st_=st_, sl_=sl_: e.matmul(bki[:, sl_], lhsT=dset[:, 2, st_, :], rhs=G[:, 0, st_, 0, :],
                                                                                   start=False, stop=True), reads=[dset, G], writes=[bki])
                                c.op("act", lambda e, k=k: e.activation(out=G[:, k, :, 0, :], in_=bkr[:].rearrange("p (a b) -> p a b", a=4),
                                                                        func=AF.Identity), reads=[bkr], writes=[G])
                                c.op("act", lambda e, k=k: e.activation(out=G[:, k, :, 1, :], in_=bki[:].rearrange("p (a b) -> p a b", a=4),
                                                                        func=AF.Identity), reads=[bki], writes=[G])
                            if k <= 7:
                                i = 7 - k
                                bkr, bki = bank(), bank()
                                for st_ in range(4):
                                    sl_ = slice(st_ * 128, (st_ + 1) * 128)
                                    if k == 0:
                                        c.op("pe", lambda e, st_=st_, sl_=sl_: e.matmul(bkr[:, sl_], lhsT=BT[:, st_, 0, :], rhs=identb[:],
                                                                                       start=True, stop=True), reads=[BT, identb], writes=[bkr])
                                    else:
                                        c.op("pe", lambda e, st_=st_, sl_=sl_: e.matmul(bkr[:, sl_], lhsT=BT[:, st_, 0, :], rhs=dset[:, 0, st_, :],
                                                                                       start=True, stop=False), reads=[BT, dset], writes=[bkr])
                                        c.op("pe", lambda e, st_=st_, sl_=sl_: e.matmul(bkr[:, sl_], lhsT=BT[:, st_, 1, :], rhs=dset[:, 2, st_, :],
                                                                                       start=False, stop=True), reads=[BT, dset], writes=[bkr])
                                for st_ in range(4):
                                    sl_ = slice(st_ * 128, (st_ + 1) * 128)
                                    if k == 0:
                                        c.op("pe", lambda e, st_=st_, sl_=sl_: e.matmul(bki[:, sl_], lhsT=BT[:, st_, 1, :], rhs=identb[:],
                                                                                       start=True, stop=True), reads=[BT, identb], writes=[bki])
                                    else:
                                        c.op("pe", lambda e, st_=st_, sl_=sl_: e.matmul(bki[:, sl_], lhsT=BT[:, st_, 0, :], rhs=dset[:, 1, st_, :],
                                                                                       start=True, stop=False), reads=[BT, dset], writes=[bki])
                                        c.op("pe", lambda e, st_=st_, sl_=sl_: e.matmul(bki[:, sl_], lhsT=BT[:, st_, 1, :], rhs=dset[:, 0, st_, :],
                                                                                       start=False, stop=True), reads=[BT, dset], writes=[bki])
                                c.op("act", lambda e, i=i: e.activation(out=WI[:, i, 0].rearrange("p a b -> p (a b)"), in_=bkr[:], func=AF.Identity),
                                     reads=[bkr], writes=[WI])
                                c.op("act", lambda e, i=i: e.activation(out=WI[:, i, 1].rearrange("p a b -> p (a b)"), in_=bki[:], func=AF.Identity),
                                     reads=[bki], writes=[WI])
                        for hb in range(2):
                            bk = bank()
                            for tq in range(4):
                                tau = hb * 4 + tq
                                i_ = 0
                                for st_ in range(4):
                                    for ri in range(2):
                                        c.op("pe", lambda e, tq=tq, tau=tau, st_=st_, ri=ri, i_=i_, bk=bk: e.matmul(
                                            bk[:, tq * 128:(tq + 1) * 128], lhsT=BT[:, st_, ri, :], rhs=G[:, tau, st_, ri, :],
                                            start=(i_ == 0), stop=(i_ == 7)), reads=[BT, G], writes=[bk])
                                        i_ += 1
                            c.op("act", lambda e, bk=bk, hb=hb: e.activation(
                                out=KT[:, hb * 4:hb * 4 + 4, :], in_=bk[:].rearrange("p (a b) -> p a b", a=4), func=AF.Identity),
                                reads=[bk], writes=[KT])
                        if bstage is not None and bstage < 3:
                            continue
                        c.op("pool", lambda e: e.memset(carr[:], 0.0), writes=[carr])
                        c.op("pool", lambda e: e.memset(SBr[:, :, 0:1], 0.0), writes=[SBr])
                        c.op("pool", lambda e: e.memset(SBi[:, :, 0:1], 0.0), writes=[SBi])
                        X_ = XD[d_]
                        for u_ in range(4):
                            m0 = u_ * 128
                            PSr, PSi = bank(), bank()
                            for ri, PS_ in ((0, PSr), (1, PSi)):
                                for st_ in range(4):
                                    for i in range(8):
                                        c.op("pe", lambda e, ri=ri, PS_=PS_, st_=st_, i=i: e.matmul(
                                            PS_[:, st_ * 128:(st_ + 1) * 128], lhsT=WI[:, i, ri, st_, :], rhs=X_[:, i, m0:m0 + 128],
                                            start=(i == 0), stop=(i == 7)), reads=[WI, X_], writes=[PS_])
                            pr = PSr[:].rearrange("p (s j) -> p s j", s=4)
                            pi_ = PSi[:].rearrange("p (s j) -> p s j", s=4)
                            T1, T2, T3, T4, BR, BI, SR, SI = TM
                            c.op("dve", lambda e: e.tensor_tensor(out=T1[:], in0=pr, in1=ctab[:], op=ALU.mult), reads=[PSr, ctab], writes=[T1])
                            c.op("dve", lambda e: e.tensor_tensor(out=T2[:], in0=pi_, in1=stab[:], op=ALU.mult), reads=[PSi, stab], writes=[T2])
                            c.op("pool", lambda e: e.tensor_tensor(out=BR[:], in0=T1[:], in1=T2[:], op=ALU.add), reads=[T1, T2], writes=[BR])
                            c.op("dve", lambda e: e.tensor_tensor(out=T3[:], in0=pi_, in1=ctab[:], op=ALU.mult), reads=[PSi, ctab], writes=[T3])
                            c.op("dve", lambda e: e.tensor_tensor(out=T4[:], in0=pr, in1=stab[:], op=ALU.mult), reads=[PSr, stab], writes=[T4])
                            c.op("pool", lambda e: e.tensor_tensor(out=BI[:], in0=T3[:], in1=T4[:], op=ALU.subtract), reads=[T3, T4], writes=[BI])
                            for (B_, S_, ci) in ((BR, SR, 0), (BI, SI, 1)):
                                for st_ in range(4):
                                    c.op("dve", lambda e, B_=B_, S_=S_, st_=st_, ci=ci: e.tensor_tensor_scan(
                                        out=S_[:, st_, :], data0=prm[:, d_, P_RHO8, ct * 4 + st_:ct * 4 + st_ + 1].to_broadcast([128, 128]), data1=B_[:, st_, :],
                                        initial=carr[:, ci, st_:st_ + 1], op0=ALU.mult, op1=ALU.add),
                                        reads=[B_, prm, carr], writes=[S_])
                            lr_, li_ = SR[:, :, 127], SI[:, :, 127]
                            c7, s7 = ck[:, d_, 10, cs], sk[:, d_, 10, cs]
                            c.op("dve", lambda e: e.tensor_tensor(out=ctmp[:, 0, :], in0=lr_, in1=c7, op=ALU.mult), reads=[SR, ck], writes=[ctmp])
                            c.op("dve", lambda e: e.tensor_tensor(out=ctmp[:, 1, :], in0=li_, in1=s7, op=ALU.mult), reads=[SI, sk], writes=[ctmp])
                            c.op("dve", lambda e: e.tensor_tensor(out=ctmp[:, 2, :], in0=lr_, in1=s7, op=ALU.mult), reads=[SR, sk], writes=[ctmp])
                            c.op("dve", lambda e: e.tensor_tensor(out=ctmp[:, 3, :], in0=li_, in1=c7, op=ALU.mult), reads=[SI, ck], writes=[ctmp])
                            c.op("dve", lambda e: e.tensor_tensor(out=carr[:, 0, :], in0=ctmp[:, 0, :], in1=ctmp[:, 1, :], op=ALU.subtract),
                                 reads=[ctmp], writes=[carr])
                            c.op("dve", lambda e: e.tensor_tensor(out=carr[:, 1, :], in0=ctmp[:, 2, :], in1=ctmp[:, 3, :], op=ALU.add),
                                 reads=[ctmp], writes=[carr])
                            R1, R2, R3, R4 = T1, T2, T3, T4
                            c.op("pool", lambda e: e.tensor_tensor(out=R1[:], in0=SR[:], in1=ctab[:], op=ALU.mult), reads=[SR, ctab], writes=[R1])
                            c.op("pool", lambda e: e.tensor_tensor(out=R2[:], in0=SI[:], in1=stab[:], op=ALU.mult), reads=[SI, stab], writes=[R2])
                            c.op("dve", lambda e: e.tensor_tensor(out=SBr[:, :, 1 + m0:1 + m0 + 128], in0=R1[:], in1=R2[:], op=ALU.subtract),
                                 reads=[R1, R2], writes=[SBr])
                            c.op("pool", lambda e: e.tensor_tensor(out=R3[:], in0=SR[:], in1=stab[:], op=ALU.mult), reads=[SR, stab], writes=[R3])
                            c.op("dve", lambda e: e.tensor_tensor(out=R4[:], in0=SI[:], in1=ctab[:], op=ALU.mult), reads=[SI, ctab], writes=[R4])
                            c.op("pool", lambda e: e.tensor_tensor(out=SBi[:, :, 1 + m0:1 + m0 + 128], in0=R3[:], in1=R4[:], op=ALU.add),
                                 reads=[R3, R4], writes=[SBi])
                        if bstage is not None and bstage < 4:
                            continue
                        yv = YACC[:] if d_ == 0 else YACC[:, ::-1]
                        for j in range(8):
                            PY = bank()
                            tot = (j + 1) + 8
                            i_ = 0
                            for i in range(j + 1):
                                c.op("pe", lambda e, i=i, j=j, i_=i_, PY=PY: e.matmul(PY[:, 0:512], lhsT=KT[:, j - i, :], rhs=X_[:, i, :],
                                                                                   start=(i_ == 0), stop=(i_ == tot - 1)),
                                     reads=[KT, X_], writes=[PY])
                                i_ += 1
                            for st_ in range(4):
                                for ri, SB_ in ((0, SBr), (1, SBi)):
                                    c.op("pe", lambda e, st_=st_, ri=ri, SB_=SB_, j=j, i_=i_, PY=PY: e.matmul(
                                        PY[:, 0:512], lhsT=G[:, j + 1, st_, ri, :], rhs=SB_[:, st_, 0:512],
                                        start=(i_ == 0), stop=(i_ == tot - 1)), reads=[G, SB_], writes=[PY])
                                    i_ += 1
                            c.op("dve", lambda e, j=j, PY=PY: e.tensor_tensor(out=yv[:, j::8], in0=PY[:, 0:512], in1=yv[:, j::8], op=ALU.add),
                                 reads=[PY, YACC], writes=[YACC])
                    for cc in range(8):
                        ysl = YACC[:, cc * 512:(cc + 1) * 512]
                        g_ = TM[cc % 4][:].rearrange("p a b -> p (a b)")
                        gb_ = TM[cc % 4]
                        c.op("act", lambda e, ysl=ysl, g_=g_: e.activation(out=g_, in_=ysl, func=AF.Square), reads=[YACC], writes=[gb_])
                        c.op("dve", lambda e, g_=g_: e.tensor_scalar(out=g_, in0=g_, scalar1=0.044715, scalar2=1.0, op0=ALU.mult, op1=ALU.add),
                             reads=[gb_], writes=[gb_])
                        c.op("pool", lambda e, ysl=ysl, g_=g_: e.tensor_tensor(out=g_, in0=g_, in1=ysl, op=ALU.mult), reads=[gb_, YACC], writes=[gb_])
                        c.op("act", lambda e, g_=g_: e.activation(out=g_, in_=g_, func=AF.Sigmoid, scale=1.5957691216057308), reads=[gb_], writes=[gb_])
                        c.op("dve", lambda e, ysl=ysl, g_=g_: e.tensor_tensor(out=ysl, in0=ysl, in1=g_, op=ALU.mult), reads=[gb_, YACC], writes=[YACC])
                    c.op("pool", lambda e: e.tensor_copy(out=YGB[:], in_=YACC[:]), reads=[YACC], writes=[YGB])
                    c.dma(ygT_d[ct * 128:(ct + 1) * 128, :], YACC[:], reads=[YACC], writes=[dram_misc], sem_buf=YACC, store=True)
                    c.dma(ygb_d[ct * 128:(ct + 1) * 128, :], YGB[:], reads=[YGB], writes=[dram_misc], sem_buf=YGB, store=True)
                c.barrier()
                c.release([wc_, spm, prm, ck, sk, lp, hpi, ident, identb, nlpi, st4, ctab, stab, ttmp, BT, G, WI, KT, YACC,
                           SBr, SBi, carr, ctmp, YGB] + XD + TM + dg)
            if 'C' not in skip:
                with ExitStack() as ph:
                    ygb = c.sbuf("ygb", [128, 4, T], BF16, ph)
                    wgl = c.sbuf("wgl", [128, 4, 128], BF16, ph)
                    wcg = c.sbuf("wcg", [128, 8, 128], BF16, ph)
                    SGL = [c.sbuf("SGL%d" % i, [128, 1024], F32, ph) for i in range(2)]
                    SGC = [c.sbuf("SGC%d" % i, [128, 1024], F32, ph) for i in range(2)]
                    YGS = [c.sbuf("YGS%d" % i, [128, 1024], F32, ph) for i in range(2)]
                    YSC = [c.sbuf("YSC%d" % i, [128, 1024], BF16, ph) for i in range(2)]
                    c.dma(ygb[:], ygb_d.rearrange("(k p) t -> p k t", p=128), reads=[dram_misc], writes=[ygb], sem_buf=ygb)
                    for ct in range(4):
                        load_w(wgl, wgl[:], gluw_d[L, :, ct * 128:(ct + 1) * 128].rearrange("(kt p) n -> p kt n", p=128), 4, 128)
                        load_w(wcg, wcg[:], win_cols(L, 7 * W + ct * 128, 128), 8, 128, fold_g=True)
                        for si in range(4):
                            t0 = si * 1024
                            sgl, sgc, ygs, ysc = SGL[si % 2], SGC[si % 2], YGS[si % 2], YSC[si % 2]
                            c.dma(ygs[:], ygT_d[ct * 128:(ct + 1) * 128, t0:t0 + 1024], reads=[dram_misc], writes=[ygs], sem_buf=ygs)

                            def ev_l(bk, a, n, sgl=sgl, t0=t0):
                                c.op("act", lambda e: e.activation(out=sgl[:, a - t0:a - t0 + n], in_=bk[:, 0:n], func=AF.Sigmoid,
                                                                   bias=chp[:, V_GLUB, ct:ct + 1]), reads=[bk, chp], writes=[sgl])

                            def ev_c(bk, a, n, sgc=sgc, t0=t0):
                                c.op("act", lambda e: e.activation(out=sgc[:, a - t0:a - t0 + n], in_=bk[:, 0:n], func=AF.Silu),
                                     reads=[bk], writes=[sgc])
                            mm_chunks([(wgl, (lambda kt: wgl[:, kt, :]), 4, (lambda kt, a, n: ygb[:, kt, a:a + n]), ygb)], t0, 1024, ev_l)
                            mm_chunks([(wcg, (lambda kt: wcg[:, kt, :]), 8, hn_rhs(0), hn)], t0, 1024, ev_c)
                            c.op("dve", lambda e, ygs=ygs, sgl=sgl: e.tensor_tensor(out=ygs[:], in0=ygs[:], in1=sgl[:], op=ALU.mult),
                                 reads=[ygs, sgl], writes=[ygs])
                            if dbg:
                                c.dma(dbg_d[2, ct * 128:(ct + 1) * 128, t0:t0 + 1024], ygs[:], reads=[ygs], writes=[dram_misc],
                                      sem_buf=ygs, store=True)
                            c.op("pool", lambda e, ygs=ygs, sgc=sgc, ysc=ysc: e.tensor_tensor(out=ysc[:], in0=ygs[:], in1=sgc[:], op=ALU.mult),
                                 reads=[ygs, sgc], writes=[ysc])
                            c.dma(ysT_d[2, ct * 128:(ct + 1) * 128, t0:t0 + 1024], ysc[:], reads=[ysc], writes=[dram_ys],
                                  sem_buf=ysc, store=True)
                    c.barrier()
                    c.release([ygb, wgl, wcg] + SGL + SGC + YGS + YSC)
            if stop_after == 'C':
                break
            with ExitStack() as ph:
                bmks = [c.sbuf("bmk%d" % i, [128, 8, 512], F32, ph) for i in range(2)]
                wqs = [c.sbuf("wq%d" % i, [128, 8, 128], BF16, ph) for i in range(2)]
                wks = [c.sbuf("wk%d" % i, [128, 8, 128], BF16, ph) for i in range(2)]
                wvs = [c.sbuf("wv%d" % i, [128, 8, 128], BF16, ph) for i in range(2)]
                wbgs = [c.sbuf("wbg%d" % i, [128, 8, 128], BF16, ph) for i in range(2)]

                def loadB(h_):
                    for lst, col in ((wqs, 2), (wks, 3), (wvs, 4), (wbgs, 5)):
                        load_w(lst[h_ % 2], lst[h_ % 2][:], win_cols(L, col * W + h_ * 128, 128), 8, 128)
                    c.dma(bmks[h_ % 2][:], bm_d[L, h_].rearrange("p c h x -> p c (h x)"), writes=[bmks[h_ % 2]], sem_buf=bmks[h_ % 2])
                if 'B' not in skip:
                    loadB(0)
                qkg = c.sbuf("qkg", [128, 2], F32, ph)
                eps2 = c.sbuf("eps2", [128, 1], F32, ph)
                QZ = c.sbuf("QZ", [128, 64, 2, 64], BF16, ph)
                KN = c.sbuf("KN", [128, T], BF16, ph)
                Zf = c.sbuf("Zf", [128, 1024], F32, ph)
                SQb = c.sbuf("SQb", [128, 1024], BF16, ph)
                RS = c.sbuf("RS", [128, 1024], F32, ph)
                Vs = [c.sbuf("VA%d" % i, [128, 32, 2, 65], BF16, ph) for i in range(2)]
                identf = c.sbuf("identf", [128, 128], F32, ph)
                Yn = [c.sbuf("Yn%d" % i, [64, 2, 64], F32, ph) for i in range(2)]
                rec = [c.sbuf("rec%d" % i, [64, 2], F32, ph) for i in range(2)]
                ES = [c.sbuf("ES%d" % i, [128, 512], F32, ph) for i in range(2)]
                EB = [c.sbuf("EB%d" % i, [128, 512], BF16, ph) for i in range(2)]
                RD = c.sbuf("RD", [128, 512], F32, ph)
                SGF = c.sbuf("SGF", [128, T], F32, ph)
                YB = [c.sbuf("YB%d" % i, [128, 512], F32, ph) for i in range(2)]
                SGb = [c.sbuf("SGb%d" % i, [128, 512], F32, ph) for i in range(2)]
                YSb = [c.sbuf("YSb%d" % i, [128, 512], BF16, ph) for i in range(2)]
                for hp in (range(4) if 'B' not in skip else []):
                    wq, wk, wv, wbg = wqs[hp % 2], wks[hp % 2], wvs[hp % 2], wbgs[hp % 2]
                    if hp + 1 < 4:
                        loadB(hp + 1)
                    c.dma(qkg[:], qkg_d[L], writes=[qkg], sem_buf=qkg)
                    bmk = bmks[hp % 2]
                    c.op("pool", lambda e: e.memset(eps2[:], 1e-6), writes=[eps2])
                    c.op("dve", lambda e: e.tensor_scalar(out=qkg[:, 0:1], in0=qkg[:, 0:1], scalar1=0.125, scalar2=None, op0=ALU.mult),
                         reads=[qkg], writes=[qkg])
                    c.op("pool", lambda e: e.memset(QZ[64:128, :, 0, :], 0.0), writes=[QZ])
                    c.op("pool", lambda e: e.memset(QZ[0:64, :, 1, :], 0.0), writes=[QZ])
                    for (w_, gi_) in ((wq, 0), (wk, 1)):
                        for si in range(4):
                            t0 = si * 1024

                            def ev_z(bk, a, n, t0=t0):
                                o = a - t0
                                c.op("act", lambda e: e.activation(out=Zf[:, o:o + n], in_=bk[:, 0:n], func=AF.Identity),
                                     reads=[bk], writes=[Zf])
                                c.op("act", lambda e: e.activation(out=SQb[:, o:o + n], in_=bk[:, 0:n], func=AF.Square),
                                     reads=[bk], writes=[SQb])
                            mm_chunks([(w_, (lambda kt, w_=w_: w_[:, kt, :]), 8, hn_rhs(0), hn)], t0, 1024, ev_z)

                            def ev_r(bk, a, n, t0=t0):
                                o = a - t0
                                c.op("act", lambda e: e.activation(out=RS[:, o:o + n], in_=bk[:, 0:n], func=AF.Ln, scale=1.0 / 64,
                                                                   bias=eps2[:, 0:1]), reads=[bk, eps2], writes=[RS])
                            mm_chunks([(blk1, (lambda kt: blk1[:]), 1, (lambda kt, a, n, t0=t0: SQb[:, a - t0:a - t0 + n]), SQb)],
                                      t0, 1024, ev_r)
                            c.op("act", lambda e: e.activation(out=RS[:], in_=RS[:], func=AF.Exp, scale=-0.5), reads=[RS], writes=[RS])
                            if gi_ == 1:
                                c.op("dve", lambda e, t0=t0: e.scalar_tensor_tensor(
                                    out=KN[:, t0:t0 + 1024], in0=Zf[:], scalar=qkg[:, 1:2], in1=RS[:], op0=ALU.mult, op1=ALU.mult),
                                    reads=[Zf, qkg, RS], writes=[KN])
                            else:
                                for h2 in range(2):
                                    ps_ = slice(h2 * 64, (h2 + 1) * 64)
                                    c.op("dve", lambda e, t0=t0, h2=h2, ps_=ps_: e.scalar_tensor_tensor(
                                        out=QZ[ps_, t0 // 64:t0 // 64 + 16, h2, :], in0=Zf[ps_, :].rearrange("p (r q) -> p r q", q=64),
                                        scalar=qkg[ps_, 0:1], in1=RS[ps_, :].rearrange("p (r q) -> p r q", q=64),
                                        op0=ALU.mult, op1=ALU.mult), reads=[Zf, qkg, RS], writes=[QZ])
                    if hp == 0:
                        c.dma(identf[:], ident_d, writes=[identf], sem_buf=identf)
                        for par in range(2):
                            c.op("pool", lambda e, par=par: e.memset(Vs[par][:, :, :, 64:65], 1.0), writes=[Vs[par]])
                    for par in range(2):
                        ntile = 32 - par
                        for tg in range(0, ntile, 4):
                            bk = bank()
                            nt_ = min(4, ntile - tg)
                            for j in range(nt_):
                                tt_ = tg + j
                                for kt in range(8):
                                    c.op("pe", lambda e, j=j, tt_=tt_, kt=kt, bk=bk: e.matmul(
                                        bk[:, j * 128:(j + 1) * 128], lhsT=hn[:, kt, 2 + 64 * par + tt_ * 128:2 + 64 * par + (tt_ + 1) * 128],
                                        rhs=wv[:, kt, :], start=(kt == 0), stop=(kt == 7)), reads=[hn, wv], writes=[bk])
                            c.op("act", lambda e, bk=bk, tg=tg, nt_=nt_: e.activation(
                                out=Vs[par][:, tg:tg + nt_, :, 0:64], in_=bk[:, 0:nt_ * 128].rearrange("p (a h d) -> p a h d", a=nt_, h=2),
                                func=AF.Identity), reads=[bk], writes=[Vs[par]])
                    def ev_bg(bk, a, n):
                        c.op("act", lambda e: e.activation(out=SGF[:, a:a + n], in_=bk[:, 0:n], func=AF.Silu), reads=[bk], writes=[SGF])
                    mm_chunks([(wbg, (lambda kt: wbg[:, kt, :]), 8, hn_rhs(0), hn)], 0, T, ev_bg)
                    state["nb"] = 6
                    nrows = 64 if bstage is None else 8 * bstage

                    def emit_qk(r):
                        rs_ = min(max(r - 4, 0), 56)
                        cfg = r - rs_
                        par = rs_ % 2
                        tt0 = (rs_ - par) // 2
                        k0 = 64 * rs_
                        bS = bank()
                        for kt in range(4):
                            c.op("pe", lambda e, kt=kt: e.matmul(
                                bS[:, kt * 128:(kt + 1) * 128], lhsT=KN[:, k0 + kt * 128:k0 + (kt + 1) * 128],
                                rhs=QZ[:, r, :, :].rearrange("p h q -> p (h q)"), start=True, stop=True),
                                reads=[KN, QZ], writes=[bS])
                        es_, eb_ = ES[r % 2], EB[r % 2]
                        c.op("dve", lambda e: e.tensor_tensor(out=es_[:], in0=bS[:], in1=bmk[:, cfg, :], op=ALU.add),
                             reads=[bS, bmk], writes=[es_])
                        c.op("act", lambda e: e.activation(out=eb_[:], in_=es_[:], func=AF.Exp), reads=[es_], writes=[eb_])
                        return eb_, par, tt0

                    def emit_tr(r):
                        rg, rl = r // 8, r % 8
                        bT = pb[6 + rg % 2]
                        yn_ = Yn[r % 2]
                        c.op("pe", lambda e: e.transpose(bT[:, rl * 64:(rl + 1) * 64], yn_[:].rearrange("p h d -> p (h d)"), identf[0:64, 0:64]),
                             reads=[yn_, identf], writes=[bT])
                        if rl != 7:
                            return
                        t0 = rg * 512
                        yb_, ys_ = YB[rg % 2], YSb[rg % 2]
                        if dbg:
                            c.op("act", lambda e: e.activation(out=yb_[:], in_=bT[:], func=AF.Identity), reads=[bT], writes=[yb_])
                            c.dma(dbg_d[1, hp * 128:(hp + 1) * 128, t0:t0 + 512], yb_[:], reads=[yb_], writes=[dram_misc],
                                  sem_buf=yb_, store=True)
                        c.op("dve", lambda e: e.tensor_tensor(out=ys_[:], in0=bT[:], in1=SGF[:, t0:t0 + 512], op=ALU.mult),
                             reads=[bT, SGF], writes=[ys_])
                        c.dma(ysT_d[1, hp * 128:(hp + 1) * 128, t0:t0 + 512], ys_[:], reads=[ys_], writes=[dram_ys],
                              sem_buf=ys_, store=True)

                    pend = emit_qk(0) if nrows else None
                    prev = None
                    for r in range(nrows):
                        eb_, par, tt0 = pend
                        if r + 1 < nrows:
                            pend = emit_qk(r + 1)
                        bY = bank()
                        for h2 in range(2):
                            for kt in range(4):
                                c.op("pe", lambda e, h2=h2, kt=kt: e.matmul(
                                    bY[0:64, h2 * 65:(h2 + 1) * 65], lhsT=eb_[:, (kt * 2 + h2) * 64:(kt * 2 + h2 + 1) * 64],
                                    rhs=Vs[par][:, tt0 + kt, h2, :], start=(kt == 0), stop=(kt == 3)),
                                    reads=[Vs[par], eb_], writes=[bY])
                        byv = bY[0:64, 0:130].rearrange("p (h c) -> p h c", c=65)
                        rc_, yn_ = rec[r % 2], Yn[r % 2]
                        c.op("dve", lambda e: e.reciprocal(out=rc_[:], in_=byv[:, :, 64]), reads=[bY], writes=[rc_])
                        c.op("dve", lambda e: e.tensor_tensor(out=yn_[:], in0=byv[:, :, 0:64], in1=rc_[:].unsqueeze(2).to_broadcast([64, 2, 64]),
                                                              op=ALU.mult), reads=[bY, rc_], writes=[yn_])
                        if prev is not None:
                            emit_tr(prev)
                        prev = r
                    if prev is not None:
                        emit_tr(prev)
                    state["nb"] = 8
                c.barrier()
                c.release(wqs + wks + wvs + wbgs + [qkg, eps2, KN, Zf, SQb, RS, RD, SGF, identf] + bmks + [QZ] + Vs + ES + EB + YB + SGb + YSb + Yn + rec)
            if stop_after == 'B':
                break
            if stop_after == 'mix':
                break
            TB = 512
            with ExitStack() as ph:
                wmg = c.sbuf("wmg", [128, 8, 4 * D], BF16, ph)
                wbs = c.sbuf("wbs", [128, 4, 4, D], BF16, ph)
                ysb = c.sbuf("ysb", [128, 16, TB], BF16, ph)
                mgb = [c.sbuf("mgb%d" % i, [128, 8, TB], BF16, ph) for i in range(1)]
                acc = c.sbuf("acc", [128, TB], F32, ph)
                sgm = [c.sbuf("sgm%d" % i, [128, TB], F32, ph) for i in range(2)]
                tmpm = [c.sbuf("tmpm%d" % i, [128, TB], F32, ph) for i in range(2)]
                wmg_t = [Buf("wmg_t%d" % i, wmg.t) for i in range(8)]
                wbs_t = [Buf("wbs_t%d" % i, wbs.t) for i in range(8)]
                ysb_t = [Buf("ysb_t%d" % i, ysb.t) for i in range(4)]
                for dt in range(8):
                    for n in range(4):
                        cb = n * 8 + dt
                        load_w(wmg_t[dt], wmg[:, :, cb * 128:(cb + 1) * 128], win_cols(L, 10 * W + cb * 128, 128), 8, 128)
                        load_w(wbs_t[dt], wbs[:, n, :, dt * 128:(dt + 1) * 128],
                               wbr_d[L, n, :, dt * 128:(dt + 1) * 128].rearrange("(kt p) n -> p kt n", p=128), 4, 128)
                for blk in range(T // TB):
                    t0 = blk * TB
                    mg_ = mgb[0]
                    for n in range(4):
                        c.dma(ysb[:, n * 4:(n + 1) * 4, :], ysT_d[n, :, t0:t0 + TB].rearrange("(k p) t -> p k t", p=128),
                              reads=[dram_ys], writes=[ysb_t[n]], sem_buf=ysb_t[n])
                    for dt in range(8):
                        for n in range(4):
                            b1 = bank()
                            for kt in range(8):
                                c.op("pe", lambda e, kt=kt, b1=b1, n=n, dt=dt: e.matmul(
                                    b1[:, 0:TB], lhsT=wmg[:, kt, n * D + dt * 128:n * D + (dt + 1) * 128],
                                    rhs=hn[:, kt, 2 + t0:2 + t0 + TB], start=(kt == 0), stop=(kt == 7)),
                                    reads=[wmg_t[dt], hn], writes=[b1])
                            b2 = bank()
                            for kt in range(4):
                                c.op("pe", lambda e, kt=kt, b2=b2, n=n, dt=dt: e.matmul(
                                    b2[:, 0:TB], lhsT=wbs[:, n, kt, dt * 128:(dt + 1) * 128], rhs=ysb[:, n * 4 + kt, :],
                                    start=(kt == 0), stop=(kt == 3)), reads=[wbs_t[dt], ysb_t[n]], writes=[b2])
                            sg_ = sgm[n % 2]
                            c.op("act", lambda e, sg_=sg_, b1=b1: e.activation(out=sg_[:], in_=b1[:, 0:TB], func=AF.Sigmoid),
                                 reads=[b1], writes=[sg_])
                            if n == 0:
                                c.op("dve", lambda e, sg_=sg_, b2=b2: e.tensor_tensor(out=acc[:], in0=b2[:, 0:TB], in1=sg_[:], op=ALU.mult),
                                     reads=[b2, sg_], writes=[acc])
                            else:
                                tm_ = tmpm[n % 2]
                                c.op("dve", lambda e, sg_=sg_, b2=b2, tm_=tm_: e.tensor_tensor(out=tm_[:], in0=b2[:, 0:TB], in1=sg_[:], op=ALU.mult),
                                     reads=[b2, sg_], writes=[tm_])
                                if n < 3:
                                    c.op("pool", lambda e, tm_=tm_: e.tensor_tensor(out=acc[:], in0=acc[:], in1=tm_[:], op=ALU.add),
                                         reads=[acc, tm_], writes=[acc])
                                else:
                                    c.op("pool", lambda e, tm_=tm_, dt=dt, mg_=mg_: e.tensor_tensor(out=mg_[:, dt, :], in0=acc[:], in1=tm_[:], op=ALU.add),
                                         reads=[acc, tm_], writes=[mg_])
                    c.dma(mgT_d[:, t0:t0 + TB].rearrange("(k p) t -> p k t", p=128), mg_[:], reads=[mg_], writes=[dram_mg],
                          sem_buf=mg_, store=True)
                c.barrier()
                c.release([wmg, wbs, ysb, acc] + mgb + sgm + tmpm + wmg_t + wbs_t + ysb_t)
            with ExitStack() as ph:
                wo_sb = c.sbuf("wo_sb", [128, 8, D], BF16, ph)
                pg_sb = c.sbuf("pg_sb", [128, 8, D], BF16, ph)
                pp_sb = c.sbuf("pp_sb", [128, 2, D], BF16, ph)
                mgi = [c.sbuf("mgi%d" % i, [128, 8, TB], BF16, ph) for i in range(2)]
                xk8 = [c.sbuf("xk8%d" % i, [128, 8, TB], F32, ph) for i in range(2)]
                x1b = c.sbuf("x1b", [128, 8, TB], BF16, ph)
                ptf = c.sbuf("ptf", [128, 2, TB], F32, ph)
                ptb = c.sbuf("ptb", [128, 2, TB], BF16, ph)
                sg2 = [c.sbuf("sg2%d" % i, [128, TB], F32, ph) for i in range(2)]
                wo_t = [Buf("wo_t%d" % i, wo_sb.t) for i in range(8)]
                pg_t = [Buf("pg_t%d" % i, pg_sb.t) for i in range(8)]
                pp_t = [Buf("pp_t%d" % i, pp_sb.t) for i in range(8)]
                x1bs = [x1b, c.sbuf("x1b2", [128, 8, TB], BF16, ph)]
                ptbs = [ptb, c.sbuf("ptb2", [128, 2, TB], BF16, ph)]
                for db in range(8):
                    sl = slice(db * 128, (db + 1) * 128)
                    load_w(wo_t[db], wo_sb[:, :, sl], wout_d[L, :, sl].rearrange("(kt p) n -> p kt n", p=128), 8, 128)
                for db in range(8):
                    sl = slice(db * 128, (db + 1) * 128)
                    load_w(pg_t[db], pg_sb[:, :, sl], pgate_d[L, :, sl].rearrange("(kt p) n -> p kt n", p=128), 8, 128)
                    load_w(pp_t[db], pp_sb[:, :, sl], pproj_d[L, :, sl].rearrange("(kt p) n -> p kt n", p=128), 2, 128)
                nblk = T // TB

                def m2_stage1(blk):
                    t0 = blk * TB
                    mg_, x1, x1b_, ptb_ = mgi[blk % 2], xk8[blk % 2], x1bs[blk % 2], ptbs[blk % 2]
                    c.dma(mg_[:], mgT_d[:, t0:t0 + TB].rearrange("(k p) t -> p k t", p=128), reads=[dram_mg], writes=[mg_], sem_buf=mg_)
                    c.dma(ptf[:], pT_d[L, :, t0:t0 + TB].rearrange("(k p) t -> p k t", p=128), writes=[ptf], sem_buf=ptf)
                    c.dma(x1[:], xsrc_d[:, t0:t0 + TB].rearrange("(k p) t -> p k t", p=128), reads=([dram_x1] if L > 0 else []), writes=[x1], sem_buf=x1)
                    c.op("pool", lambda e: e.tensor_copy(out=ptb_[:], in_=ptf[:]), reads=[ptf], writes=[ptb_])
                    for dt in range(8):
                        b1 = bank()
                        for kt in range(8):
                            c.op("pe", lambda e, kt=kt, b1=b1, dt=dt: e.matmul(
                                b1[:, 0:TB], lhsT=wo_sb[:, kt, dt * 128:(dt + 1) * 128], rhs=mg_[:, kt, :],
                                start=(kt == 0), stop=(kt == 7)), reads=[wo_t[dt], mg_], writes=[b1])
                        c.op("dve", lambda e, dt=dt, b1=b1: e.tensor_tensor(out=x1[:, dt, :], in0=b1[:, 0:TB], in1=x1[:, dt, :], op=ALU.add),
                             reads=[b1, x1], writes=[x1])
                        c.op("pool", lambda e, dt=dt: e.tensor_copy(out=x1b_[:, dt, :], in_=x1[:, dt, :]), reads=[x1], writes=[x1b_])

                def m2_stage2(blk):
                    t0 = blk * TB
                    x1, x1b_, ptb_ = xk8[blk % 2], x1bs[blk % 2], ptbs[blk % 2]
                    for dt in range(8):
                        b1 = bank()
                        for kt in range(8):
                            c.op("pe", lambda e, kt=kt, b1=b1, dt=dt: e.matmul(
                                b1[:, 0:TB], lhsT=pg_sb[:, kt, dt * 128:(dt + 1) * 128], rhs=x1b_[:, kt, :],
                                start=(kt == 0), stop=(kt == 7)), reads=[pg_t[dt], x1b_], writes=[b1])
                        b2 = bank()
                        for kt in range(2):
                            c.op("pe", lambda e, kt=kt, b2=b2, dt=dt: e.matmul(
                                b2[:, 0:TB], lhsT=pp_sb[:, kt, dt * 128:(dt + 1) * 128], rhs=ptb_[:, kt, :],
                                start=(kt == 0), stop=(kt == 1)), reads=[pp_t[dt], ptb_], writes=[b2])
                        sg_ = sg2[dt % 2]
                        c.op("act", lambda e, sg_=sg_, b1=b1: e.activation(out=sg_[:], in_=b1[:, 0:TB], func=AF.Sigmoid),
                             reads=[b1], writes=[sg_])
                        c.op("dve", lambda e, sg_=sg_, b2=b2: e.tensor_tensor(out=sg_[:], in0=b2[:, 0:TB], in1=sg_[:], op=ALU.mult),
                             reads=[b2, sg_], writes=[sg_])
                        c.op("pool", lambda e, sg_=sg_, dt=dt: e.tensor_tensor(out=x1[:, dt, :], in0=sg_[:], in1=x1[:, dt, :], op=ALU.add),
                             reads=[sg_, x1], writes=[x1])
                    c.dma(xdst_d[:, t0:t0 + TB].rearrange("(k p) t -> p k t", p=128), x1[:], reads=[x1],
                          writes=[dram_out if L == NL - 1 else dram_x1], sem_buf=x1, store=True)

                m2_stage1(0)
                for blk in range(nblk):
                    if blk + 1 < nblk:
                        m2_stage1(blk + 1)
                    m2_stage2(blk)
                c.barrier()
                c.release([wo_sb, pg_sb, pp_sb, ptf] + x1bs + ptbs + mgi + xk8 + sg2 + wo_t + pg_t + pp_t)
            if stop_after == 'L0':
                break

        c.barrier()
    return nc


def _prep_shared(inp):
    f = np.float32
    sh = {}
    sh["gl"] = np.ascontiguousarray(inp["norm_scale"].reshape(NL, 8, 128).transpose(0, 2, 1)).astype(f)
    sh["w_in"] = np.ascontiguousarray(inp["w_in"]).astype(f)
    sh["cwbc"] = np.ascontiguousarray(np.broadcast_to(inp["lru_conv_w"][:, None], (NL, 128, 4, W))).astype(f)

    def pc(v):
        return v.reshape(NL, 4, 128).transpose(0, 2, 1)
    vecs = [inp["lru_conv_b"], inp["lru_b_r"][:, 0], inp["lru_b_r"][:, 1], inp["lru_b_i"][:, 0], inp["lru_b_i"][:, 1],
            inp["lru_lambda"][:, 0], inp["lru_lambda"][:, 1], inp["ssm_d"], inp["ssm_glu_b"], inp["pool_scale"]]
    vecs += [inp["lru_conv_w"][:, k] for k in range(4)]
    sh["chp"] = np.ascontiguousarray(np.stack([pc(v) for v in vecs], axis=2)).astype(f)
    bd = np.zeros((NL, 2, 2, 4, 128, 128), f)
    for gt, key in enumerate(("lru_w_r", "lru_w_i")):
        w = inp[key]
        for ct in range(4):
            for j in range(2):
                bd[:, :, gt, ct, j * 64:(j + 1) * 64, j * 64:(j + 1) * 64] = w[:, :, 2 * ct + j]
    sh["bd"] = bd
    qkg = np.zeros((NL, 128, 2), f)
    qkg[:, :, 0] = np.tile(inp["na_q_gain"], (1, 2))
    qkg[:, :, 1] = np.tile(inp["na_k_gain"], (1, 2))
    sh["qkg"] = qkg
    rpb = inp["na_rel_bias"]
    kk = np.arange(512)
    ki = kk // 64
    kc = kk % 64
    qc = np.arange(64)
    win = np.clip(qc - 8, 0, 48)
    valid = (kc[:, None] >= win[None, :]) & (kc[:, None] < win[None, :] + 16)
    dc = np.clip(kc[:, None] - qc[None, :], -15, 15) + 15
    bm = np.empty((NL, 4, 128, 8, 2, 256), f)
    for cfg in range(8):
        dr = ki - cfg + 7
        ok = valid & (dr[:, None] >= 0) & (dr[:, None] <= 14)
        drc = np.clip(dr, 0, 14)
        g = rpb[:, :, drc[:, None], dc]
        g = np.where(ok[None, None], g, f(-1e30)).astype(f)
        g = g.reshape(NL, 4, 2, 4, 128, 64)
        bm[:, :, :, cfg] = g.transpose(0, 1, 4, 3, 2, 5).reshape(NL, 4, 128, 2, 256)
    sh["bm"] = bm
    def sl(a):
        return a.reshape(NL, 2, 16, 2, 64).transpose(0, 1, 3, 4, 2).reshape(NL, 2, 128, 16)
    ldt = np.broadcast_to(inp["ssm_log_dt"][..., None], (NL, 2, 32, 64))
    sh["ssmp"] = np.ascontiguousarray(np.stack([sl(inp["ssm_a_re"]), sl(inp["ssm_a_im"]), sl(ldt)], axis=3)).astype(f)
    bpad = np.zeros((NL, 2, 16, 2, 128, 128), f)
    cpad = np.zeros((NL, 2, 16, 2, 128, 128), f)
    for ri, (bk, ck) in enumerate((("ssm_b_re", "ssm_c_re"), ("ssm_b_im", "ssm_c_im"))):
        B = inp[bk]
        C = inp[ck]
        for stg in range(16):
            q = stg % 4
            for gl in range(2):
                g_ = 2 * stg + gl
                r0 = 32 * q + gl * 16
                bpad[:, :, stg, ri, r0:r0 + 16, gl * 64:(gl + 1) * 64] = B[:, :, g_].transpose(0, 1, 3, 2)
                cpad[:, :, stg, ri, gl * 64:(gl + 1) * 64, r0:r0 + 16] = C[:, :, g_].transpose(0, 1, 3, 2)
    sh["bpad"] = bpad
    sh["cpad"] = cpad
    sh["bpadT"] = np.ascontiguousarray(bpad.transpose(0, 1, 2, 3, 5, 4))
    sh["ident"] = np.eye(128, dtype=f)
    sh["gluw"] = np.ascontiguousarray(inp["ssm_glu_w"]).astype(f)
    sh["poolw"] = np.ascontiguousarray(inp["pool_w"]).astype(f)
    pe = np.zeros((128, 4, 16), f)
    for gi, w in enumerate((2, 4, 8, 16)):
        t = np.concatenate([np.arange(8), np.arange(T - 8, T)])
        lo = np.clip(t - w // 2, 0, T)
        hi = np.clip(t - w // 2 + w, 0, T)
        pe[:, gi, :] = (1.0 / (hi - lo)).astype(f)[None]
    sh["pedge"] = pe
    sh["wbr"] = np.ascontiguousarray(inp["w_branch"]).astype(f)
    sh["wout"] = np.ascontiguousarray(inp["w_out"]).astype(f)
    sh["pproj"] = np.ascontiguousarray(inp["ple_proj"]).astype(f)
    sh["pgate"] = np.ascontiguousarray(inp["ple_gate"]).astype(f)
    return sh


def _in_maps(inp, cores):
    sh = _prep_shared(inp)
    maps = []
    for b in cores:
        m = dict(sh)
        m["xT"] = np.ascontiguousarray(inp["x"][b].T).astype(np.float32)
        m["pT"] = np.ascontiguousarray(inp["p"][:, b].transpose(0, 2, 1)).astype(np.float32)
        maps.append(m)
    return maps


def kernel(**inputs):
    inp = {k: np.asarray(v) for k, v in inputs.items()}
    nc = build()
    res = run_bass_kernel_spmd(nc, _in_maps(inp, range(8)), core_ids=list(range(8)))
    out = np.stack([np.ascontiguousarray(r["outT"].T) for r in res.results], axis=0)
    return out.astype(np.float32)
```

```python
import numpy as np
from contextlib import ExitStack
import concourse.bass as bass
import concourse.mybir as mybir
from concourse.bass_utils import run_bass_kernel_spmd

F32 = mybir.dt.float32
BF16 = mybir.dt.bfloat16
AF = mybir.ActivationFunctionType
ALU = mybir.AluOpType

T = 4096
D = 1024
W = 512
NIN = 9216
NL = 2
V_CONVB, V_BR0, V_BR1, V_BI0, V_BI1, V_LAM0, V_LAM1, V_SSMD, V_GLUB, V_PSCALE, V_CW0 = range(11)
NV = 14


class Sem:
    def __init__(self, handle, is_dma):
        self.h = handle
        self.is_dma = is_dma
        self.total = 0


class Buf:
    def __init__(self, name, t=None):
        self.name = name
        self.t = t
        self.last_w = {}
        self.readers = {}
        self.dsems = {}
        self.is_psum = False

    def __getitem__(self, k):
        return self.t[k]


class Eng:
    def __init__(self, name, e, sem):
        self.name = name
        self.e = e
        self.sem = sem
        self.waited = {}


class Ctx:
    def __init__(self, nc, stack):
        self.nc = nc
        self.stack = stack
        self.nsem = 0
        self.sems = []
        self.engs = {}
        for name, e in (("pe", nc.tensor), ("act", nc.scalar), ("dve", nc.vector),
                        ("pool", nc.gpsimd), ("sp", nc.sync)):
            self.engs[name] = Eng(name, e, self.new_sem(name, False))
        self.ninstr = 0
        self.free_dma_sems = {"hw": [], "sw": []}

    def new_sem(self, name, is_dma):
        self.nsem += 1
        h = self.stack.enter_context(self.nc.semaphore("s%d_%s" % (self.nsem, name)))
        s = Sem(h, is_dma)
        self.sems.append(s)
        return s

    def dma_sem(self, kind):
        if self.free_dma_sems[kind]:
            return self.free_dma_sems[kind].pop()
        return self.new_sem("dma" + kind, True)

    def sbuf(self, name, shape, dtype, stack=None):
        self.ninstr += 1
        name = "sb%d_%s" % (self.ninstr, name)
        t = (stack or self.stack).enter_context(self.nc.sbuf_tensor(name, list(shape), dtype))
        return Buf(name, t)

    def psum(self, name, shape, dtype):
        t = self.stack.enter_context(self.nc.psum_tensor(name, list(shape), dtype))
        b = Buf(name, t)
        b.is_psum = True
        return b

    def _wait(self, eng, deps):
        for sem, val in deps.items():
            if sem.is_dma:
                val = sem.total
            if sem is eng.sem and eng.name == "pe":
                continue
            if eng.waited.get(sem, 0) >= val:
                continue
            eng.e.wait_ge(sem.h, val)
            eng.waited[sem] = val

    @staticmethod
    def _merge(d, sem, val):
        if d.get(sem, 0) < val:
            d[sem] = val

    def _deps(self, reads, writes):
        deps = {}
        for b in reads:
            for s, v in b.last_w.items():
                self._merge(deps, s, v)
            if b.is_psum:
                for s, v in b.readers.items():
                    self._merge(deps, s, v)
        for b in writes:
            for s, v in b.last_w.items():
                self._merge(deps, s, v)
            for s, v in b.readers.items():
                self._merge(deps, s, v)
        return deps

    def op(self, engname, fn, reads=(), writes=()):
        eng = self.engs[engname]
        self._wait(eng, self._deps(reads, writes))
        ins = fn(eng.e)
        eng.sem.total += 1
        ins.then_inc(eng.sem.h, 1)
        for b in writes:
            b.last_w = {eng.sem: eng.sem.total}
            b.readers = {}
        for b in reads:
            if b not in writes:
                self._merge(b.readers, eng.sem, eng.sem.total)
        self.ninstr += 1
        return ins

    def dma(self, out, in_, reads=(), writes=(), sem_buf=None, store=False, q="sp"):
        eng = self.engs[q]
        kind = "sw" if q == "pool" else "hw"
        key = ("st" if store else "ld", kind)
        if key not in sem_buf.dsems:
            sem_buf.dsems[key] = self.dma_sem(kind)
        sem = sem_buf.dsems[key]
        self._wait(eng, self._deps(reads, writes))
        ins = eng.e.dma_start(out=out, in_=in_)
        sem.total += 16
        ins.then_inc(sem.h, 16)
        for b in writes:
            b.last_w = {sem: sem.total}
            b.readers = {}
        for b in reads:
            if b not in writes:
                self._merge(b.readers, sem, sem.total)
        self.ninstr += 1
        return ins

    def barrier(self):
        allv = {s: s.total for s in self.sems if s.total > 0}
        for eng in self.engs.values():
            self._wait(eng, dict(allv))

    def release(self, bufs):
        for b in bufs:
            for (_, kind), sm in b.dsems.items():
                self.free_dma_sems[kind].append(sm)
            b.dsems = {}


def build(dbg=False, stop_after=None, skip=(), bstage=None):
    nc = bass.Bass("TRN2", target_bir_lowering=False)

    def din(name, shape, dt=F32):
        return nc.dram_tensor(name, list(shape), dt, kind="ExternalInput").ap()

    def dscr(name, shape, dt=F32):
        return nc.dram_tensor(name, list(shape), dt).ap()

    xT_d = din("xT", [D, T])
    pT_d = din("pT", [NL, 256, T])
    g_d = din("gl", [NL, 128, 8])
    win_d = din("w_in", [NL, D, NIN])
    cwbc_d = din("cwbc", [NL, 128, 4, W])
    chp_d = din("chp", [NL, 128, NV, 4])
    bd_d = din("bd", [NL, 2, 2, 4, 128, 128])
    qkg_d = din("qkg", [NL, 128, 2])
    bm_d = din("bm", [NL, 4, 128, 8, 2, 256])
    sp_d = din("ssmp", [NL, 2, 128, 3, 16])
    bpad_d = din("bpad", [NL, 2, 16, 2, 128, 128])
    cpad_d = din("cpad", [NL, 2, 16, 2, 128, 128])
    bpadT_d = din("bpadT", [NL, 2, 16, 2, 128, 128])
    ident_d = din("ident", [128, 128])
    gluw_d = din("gluw", [NL, W, W])
    poolw_d = din("poolw", [NL, 4, 128, 128])
    pedge_d = din("pedge", [128, 4, 16])
    wbr_d = din("wbr", [NL, 4, W, D])
    wout_d = din("wout", [NL, D, D])
    pproj_d = din("pproj", [NL, 256, D])
    pgate_d = din("pgate", [NL, D, D])
    outT_d = nc.dram_tensor("outT", [D, T], F32, kind="ExternalOutput").ap()
    x1T_d = nc.dram_tensor("x1T", [D, T], F32, kind="ExternalOutput").ap() if dbg else dscr("x1T", [D, T])
    ysT_d = dscr("ysT", [4, W, T], BF16)
    hfT_d = dscr("hfT", [W, T])
    ygT_d = dscr("ygT", [W, T])
    ygb_d = dscr("ygb", [W, T], BF16)
    mgT_d = dscr("mgT", [D, T], BF16)
    if dbg:
        dbg_d = nc.dram_tensor("dbg", [4, W, T], F32, kind="ExternalOutput").ap()

    with ExitStack() as st:
        c = Ctx(nc, st)
        pb = [c.psum("pb%d" % i, [128, 512], F32) for i in range(8)]
        state = {"bank": 0, "cast": 0, "nb": 8}

        def bank():
            b = pb[state["bank"] % state["nb"]]
            state["bank"] += 1
            return b

        hn = c.sbuf("hn", [128, 8, T + 4], BF16)
        gsb = c.sbuf("gsb", [128, 8], F32)
        chp = c.sbuf("chp", [128, NV, 4], F32)
        onesf = c.sbuf("onesf", [128, 128], F32)
        onesb = c.sbuf("onesb", [128, 128], BF16)
        blk1 = c.sbuf("blk1", [128, 128], BF16)
        dram_x1 = Buf("x1T_dram")
        dram_ys = Buf("ys_dram")
        dram_out = Buf("out_dram")
        dram_misc = Buf("misc_dram")
        dram_mg = Buf("mg_dram")

        c.op("pool", lambda e: e.memset(onesf[:], 1.0), writes=[onesf])
        c.op("pool", lambda e: e.memset(onesb[:], 1.0), writes=[onesb])
        c.op("pool", lambda e: e.memset(blk1[:], 0.0), writes=[blk1])
        c.op("pool", lambda e: e.memset(blk1[0:64, 0:64], 1.0), writes=[blk1])
        c.op("pool", lambda e: e.memset(blk1[64:128, 64:128], 1.0), writes=[blk1])
        c.op("pool", lambda e: e.memset(hn[:, :, 0:2], 0.0), writes=[hn])
        c.op("pool", lambda e: e.memset(hn[:, :, T + 2:T + 4], 0.0), writes=[hn])

        def cast_eng():
            state["cast"] += 1
            return ("dve", "pool")[state["cast"] % 2]

        def load_w(dst, dst_ap, src_ap, nkt, ncols, fold_g=False, eng=None):
            c.dma(dst_ap, src_ap, writes=[dst], sem_buf=dst, q="pool")

        def win_cols(L, c0, n):
            return win_d[L, :, c0:c0 + n].rearrange("(kt p) n -> p kt n", p=128)

        def mm_chunks(wlist, t0, ntok, evac):
            for a in range(t0, t0 + ntok, 512):
                n = min(512, t0 + ntok - a)
                bk = bank()
                tot = sum(w_[2] for w_ in wlist)
                i = 0
                for (wbuf, lf, nkt, rf, rbuf) in wlist:
                    for kt in range(nkt):
                        c.op("pe", lambda e, lf=lf, rf=rf, kt=kt, i=i: e.matmul(
                            bk[:, 0:n], lhsT=lf(kt), rhs=rf(kt, a, n), start=(i == 0), stop=(i == tot - 1)),
                            reads=[wbuf, rbuf], writes=[bk])
                        i += 1
                evac(bk, a, n)

        def hn_rhs(shift=0):
            return lambda kt, a, n: hn[:, kt, 2 + a + shift:2 + a + shift + n]

        for L in range(NL):
            xsrc_d = xT_d if L == 0 else x1T_d
            xdst_d = x1T_d if L == 0 else outT_d
            c.dma(gsb[:], g_d[L], writes=[gsb], sem_buf=gsb)
            c.dma(chp[:], chp_d[L], writes=[chp], sem_buf=chp)
            with ExitStack() as ph:
                xk = [c.sbuf("xk%d" % i, [128, T], F32, ph) for i in range(2)]
                sq = [c.sbuf("sq%d" % i, [128, T], F32, ph) for i in range(2)]
                rstd = c.sbuf("rstd", [128, T], F32, ph)
                for kt in range(8):
                    xb = xk[kt % 2]
                    sb_ = sq[kt % 2]
                    c.dma(xb[:], xsrc_d[kt * 128:(kt + 1) * 128, :], reads=([dram_x1] if L > 0 else []), writes=[xb], sem_buf=xb,
                          q=("sp", "pool")[kt % 2])
                    c.op("act", lambda e, xb=xb, sb_=sb_: e.activation(out=sb_[:], in_=xb[:], func=AF.Square),
                         reads=[xb], writes=[sb_])
                    for ch in range(8):
                        c.op("pe", lambda e, ch=ch, sb_=sb_, kt=kt: e.matmul(
                            pb[ch][:], lhsT=onesf[:], rhs=sb_[:, ch * 512:(ch + 1) * 512], start=(kt == 0), stop=(kt == 7)),
                            reads=[onesf, sb_], writes=[pb[ch]])
                epsb = c.sbuf("epsb", [128, 1], F32, ph)
                c.op("pool", lambda e: e.memset(epsb[:], 1e-6), writes=[epsb])
                for ch in range(8):
                    sl = slice(ch * 512, (ch + 1) * 512)
                    c.op("act", lambda e, ch=ch, sl=sl: e.activation(out=rstd[:, sl], in_=pb[ch][:], func=AF.Ln,
                                                                     scale=1.0 / D, bias=epsb[:, 0:1]),
                         reads=[pb[ch], epsb], writes=[rstd])
                c.op("act", lambda e: e.activation(out=rstd[:], in_=rstd[:], func=AF.Exp, scale=-0.5), reads=[rstd], writes=[rstd])
                for kt in range(8):
                    xb = xk[kt % 2]
                    c.dma(xb[:], xsrc_d[kt * 128:(kt + 1) * 128, :], reads=([dram_x1] if L > 0 else []), writes=[xb], sem_buf=xb,
                          q=("sp", "pool")[kt % 2])
                    c.op("dve", lambda e, xb=xb, kt=kt: e.scalar_tensor_tensor(
                        out=hn[:, kt, 2:2 + T], in0=xb[:], scalar=gsb[:, kt:kt + 1], in1=rstd[:], op0=ALU.mult, op1=ALU.mult),
                        reads=[xb, rstd, gsb], writes=[hn])
                c.barrier()
                c.release(xk + sq + [rstd, epsb])

            TS = 1024
            with ExitStack() as ph:
                wzs = [c.sbuf("wz%d" % i, [128, 8, 128], BF16, ph) for i in range(2)]
                wgs = [c.sbuf("wg%d" % i, [128, 8, 128], BF16, ph) for i in range(2)]
                bdws = [c.sbuf("bdw%d" % i, [128, 4, 128], BF16, ph) for i in range(2)]

                def loadA(ct_):
                    load_w(wzs[ct_ % 2], wzs[ct_ % 2][:], win_cols(L, 0 * W + ct_ * 128, 128), 8, 128)
                    load_w(wgs[ct_ % 2], wgs[ct_ % 2][:], win_cols(L, 1 * W + ct_ * 128, 128), 8, 128)
                    for d__ in range(2):
                        for gt_ in range(2):
                            c.dma(bdws[ct_ % 2][:, d__ * 2 + gt_, :], bd_d[L, d__, gt_, ct_], writes=[bdws[ct_ % 2]],
                                  sem_buf=bdws[ct_ % 2], q="pool")
                if 'A' not in skip:
                    loadA(0)
                cp = c.sbuf("cp", [128, 4], F32, ph)
                XCF = c.sbuf("XCF", [128, T], F32, ph)
                XCBF = c.sbuf("XCBF", [128, T], BF16, ph)
                R = c.sbuf("R", [128, T], F32, ph)
                GI = c.sbuf("GI", [128, T], F32, ph)
                A2 = c.sbuf("A2", [128, T + 4], F32, ph)
                HF = c.sbuf("HF", [128, T], F32, ph)
                H = c.sbuf("H", [128, TS], F32, ph)
                SG = c.sbuf("SG", [128, TS], F32, ph)
                YS = c.sbuf("YS", [128, TS], BF16, ph)
                carry = c.sbuf("carry", [128, 1], F32, ph)
                for ct in (range(4) if 'A' not in skip else []):
                    wz, wg, bdw = wzs[ct % 2], wgs[ct % 2], bdws[ct % 2]
                    if ct + 1 < 4:
                        loadA(ct + 1)
                    for d_ in range(2):
                        c.op("act", lambda e, d_=d_: e.activation(out=cp[:, 2 * d_:2 * d_ + 1], in_=chp[:, V_LAM0 + d_, ct:ct + 1],
                                                                  func=AF.Exp, scale=-1.0), reads=[chp], writes=[cp])
                        c.op("act", lambda e, d_=d_: e.activation(out=cp[:, 2 * d_:2 * d_ + 1], in_=cp[:, 2 * d_:2 * d_ + 1],
                                                                  func=AF.Ln, bias=1.0, scale=1.0), reads=[cp], writes=[cp])
                        c.op("dve", lambda e, d_=d_: e.tensor_scalar(out=cp[:, 2 * d_ + 1:2 * d_ + 2], in0=cp[:, 2 * d_:2 * d_ + 1],
                                                                     scalar1=-16.0, scalar2=None, op0=ALU.mult), reads=[cp], writes=[cp])
                        c.op("dve", lambda e, d_=d_: e.tensor_scalar(out=cp[:, 2 * d_:2 * d_ + 1], in0=cp[:, 2 * d_:2 * d_ + 1],
                                                                     scalar1=-8.0, scalar2=None, op0=ALU.mult), reads=[cp], writes=[cp])
                    Z = A2
                    c.op("pool", lambda e: e.memset(Z[:, 0:2], 0.0), writes=[Z])
                    c.op("pool", lambda e: e.memset(Z[:, T + 2:T + 4], 0.0), writes=[Z])

                    def ev_z0(bk, a, n):
                        c.op("act", lambda e: e.activation(out=Z[:, 2 + a:2 + a + n], in_=bk[:, 0:n], func=AF.Identity), reads=[bk], writes=[Z])
                    mm_chunks([(wz, (lambda kt: wz[:, kt, :]), 8, hn_rhs(0), hn)], 0, T, ev_z0)
                    cwv = lambda k: chp[:, V_CW0 + k, ct:ct + 1]
                    c.op("dve", lambda e: e.tensor_scalar(out=XCF[:], in0=Z[:, 0:T], scalar1=cwv(0), scalar2=chp[:, V_CONVB, ct:ct + 1],
                                                          op0=ALU.mult, op1=ALU.add), reads=[Z, chp], writes=[XCF])
                    for k in range(1, 4):
                        c.op("dve", lambda e, k=k: e.scalar_tensor_tensor(
                            out=XCF[:], in0=Z[:, k:k + T], scalar=cwv(k), in1=XCF[:], op0=ALU.mult, op1=ALU.add),
                            reads=[Z, chp, XCF], writes=[XCF])
                    c.op("pool", lambda e: e.tensor_copy(out=XCBF[:], in_=XCF[:]), reads=[XCF], writes=[XCBF])

                    for d_ in range(2):
                        def ev_gate(dst, vidx):
                            def f(bk, a, n):
                                c.op("act", lambda e: e.activation(out=dst[:, a:a + n], in_=bk[:, 0:n], func=AF.Sigmoid,
                                                                   bias=chp[:, vidx, ct:ct + 1], scale=1.0),
                                     reads=[bk, chp], writes=[dst])
                            return f
                        for gt, dst, vidx in ((0, R, V_BR0 + d_), (1, GI, V_BI0 + d_)):
                            mm_chunks([(bdw, (lambda kt, gt=gt: bdw[:, d_ * 2 + gt, :]), 1,
                                        (lambda kt, a, n: XCBF[:, a:a + n]), XCBF)], 0, T, ev_gate(dst, vidx))
                        c.op("act", lambda e: e.activation(out=A2[:, 0:T], in_=R[:], func=AF.Exp, scale=cp[:, 2 * d_ + 1:2 * d_ + 2]),
                             reads=[R, cp], writes=[A2])
                        c.op("act", lambda e: e.activation(out=R[:], in_=R[:], func=AF.Exp, scale=cp[:, 2 * d_:2 * d_ + 1]),
                             reads=[R, cp], writes=[R])
                        c.op("act", lambda e: e.activation(out=A2[:, 0:T], in_=A2[:, 0:T], func=AF.Sqrt, scale=-1.0, bias=1.0),
                             reads=[A2], writes=[A2])
                        c.op("dve", lambda e: e.tensor_tensor(out=GI[:], in0=GI[:], in1=XCF[:], op=ALU.mult), reads=[GI, XCF], writes=[GI])
                        c.op("pool", lambda e: e.tensor_tensor(out=GI[:], in0=GI[:], in1=A2[:, 0:T], op=ALU.mult), reads=[GI, A2], writes=[GI])
                        c.op("pool", lambda e: e.memset(carry[:], 0.0), writes=[carry])
                        if d_ == 0:
                            for si in range(T // TS):
                                sl = slice(si * TS, (si + 1) * TS)
                                c.op("dve", lambda e, sl=sl: e.tensor_tensor_scan(out=HF[:, sl], data0=R[:, sl], data1=GI[:, sl],
                                                                                  initial=carry[:, 0:1], op0=ALU.mult, op1=ALU.add),
                                     reads=[R, GI, carry], writes=[HF])
                                c.op("dve", lambda e, sl=sl: e.tensor_copy(out=carry[:], in_=HF[:, sl.stop - 1:sl.stop]), reads=[HF], writes=[carry])
                        else:
                            for si in reversed(range(T // TS)):
                                t0 = si * TS
                                sl = slice(t0, t0 + TS)
                                c.op("dve", lambda e, sl=sl: e.tensor_tensor_scan(out=H[:, ::-1], data0=R[:, sl][:, ::-1], data1=GI[:, sl][:, ::-1],
                                                                                  initial=carry[:, 0:1], op0=ALU.mult, op1=ALU.add),
                                     reads=[R, GI, carry], writes=[H])
                                c.op("dve", lambda e: e.tensor_copy(out=carry[:], in_=H[:, 0:1]), reads=[H], writes=[carry])

                                def ev_sg(bk, a, n, t0=t0):
                                    o = a - t0
                                    c.op("act", lambda e: e.activation(out=SG[:, o:o + n], in_=bk[:, 0:n], func=AF.Silu), reads=[bk], writes=[SG])
                                mm_chunks([(wg, (lambda kt: wg[:, kt, :]), 8, hn_rhs(0), hn)], t0, TS, ev_sg)
                                c.op("pool", lambda e, sl=sl: e.tensor_tensor(out=H[:], in0=HF[:, sl], in1=H[:], op=ALU.add), reads=[HF, H], writes=[H])
                                if dbg:
                                    c.dma(dbg_d[0, ct * 128:(ct + 1) * 128, t0:t0 + TS], H[:], reads=[H], writes=[dram_misc], sem_buf=H, store=True)
                                c.op("dve", lambda e: e.tensor_tensor(out=YS[:], in0=H[:], in1=SG[:], op=ALU.mult), reads=[H, SG], writes=[YS])
                                c.dma(ysT_d[0, ct * 128:(ct + 1) * 128, t0:t0 + TS], YS[:], reads=[YS], writes=[dram_ys], sem_buf=YS, store=True)
                c.barrier()
                c.release(wzs + wgs + bdws + [cp, XCF, XCBF, R, GI, A2, HF, H, SG, YS, carry])

            if stop_after == 'A':
                break
            with ExitStack() as ph:
                wds = [c.sbuf("wd%d" % i, [128, 8, 128], BF16, ph) for i in range(2)]
                wgds = [c.sbuf("wgd%d" % i, [128, 8, 128], BF16, ph) for i in range(2)]
                wps = [c.sbuf("wp%d" % i, [128, 128], BF16, ph) for i in range(2)]

                def loadD(g_):
                    load_w(wds[g_ % 2], wds[g_ % 2][:], win_cols(L, 8 * W + g_ * 128, 128), 8, 128)
                    load_w(wgds[g_ % 2], wgds[g_ % 2][:], win_cols(L, 9 * W + g_ * 128, 128), 8, 128)
                    c.dma(wps[g_ % 2][:], poolw_d[L, g_], writes=[wps[g_ % 2]], sem_buf=wps[g_ % 2], q="pool")
                if 'D' not in skip:
                    loadD(0)
                pedge = c.sbuf("pedge", [128, 4, 16], F32, ph)
                U = c.sbuf("U", [128, T + 16], F32, ph)
                S1 = c.sbuf("S1", [128, T + 16], F32, ph)
                S2 = c.sbuf("S2", [128, T + 16], F32, ph)
                PB = c.sbuf("PB", [128, T], BF16, ph)
                YD = [c.sbuf("YD%d" % i, [128, 1024], F32, ph) for i in range(2)]
                SGd = [c.sbuf("SGd%d" % i, [128, 1024], F32, ph) for i in range(2)]
                YSd = [c.sbuf("YSd%d" % i, [128, 1024], BF16, ph) for i in range(2)]
                for gi in (range(4) if 'D' not in skip else []):
                    win_ = (2, 4, 8, 16)[gi]
                    wd, wgd, wp = wds[gi % 2], wgds[gi % 2], wps[gi % 2]
                    if gi + 1 < 4:
                        loadD(gi + 1)
                    c.dma(pedge[:], pedge_d, writes=[pedge], sem_buf=pedge)
                    c.op("pool", lambda e: e.memset(U[:, 0:8], 0.0), writes=[U])
                    c.op("pool", lambda e: e.memset(U[:, T + 8:T + 16], 0.0), writes=[U])

                    def ev_u(bk, a, n):
                        c.op("act", lambda e: e.activation(out=U[:, 8 + a:8 + a + n], in_=bk[:, 0:n], func=AF.Identity),
                             reads=[bk], writes=[U])
                    mm_chunks([(wd, (lambda kt: wd[:, kt, :]), 8, hn_rhs(0), hn)], 0, T, ev_u)
                    c.op("dve", lambda e: e.tensor_tensor(out=S1[:, 1:T + 16], in0=U[:, 0:T + 15], in1=U[:, 1:T + 16], op=ALU.add),
                         reads=[U], writes=[S1])
                    FB, OB = S1, S2
                    if win_ >= 4:
                        c.op("pool", lambda e: e.tensor_tensor(out=S2[:, 2:T + 15], in0=S1[:, 1:T + 14], in1=S1[:, 3:T + 16], op=ALU.add),
                             reads=[S1], writes=[S2])
                        FB, OB = S2, S1
                    if win_ >= 8:
                        c.op("dve", lambda e: e.tensor_tensor(out=S1[:, 4:T + 13], in0=S2[:, 2:T + 11], in1=S2[:, 6:T + 15], op=ALU.add),
                             reads=[S2], writes=[S1])
                        FB, OB = S1, S2
                    if win_ >= 16:
                        c.op("pool", lambda e: e.tensor_tensor(out=S2[:, 8:T + 8], in0=S1[:, 4:T + 4], in1=S1[:, 12:T + 12], op=ALU.add),
                             reads=[S1], writes=[S2])
                        FB, OB = S2, S1
                    c.op("dve", lambda e: e.scalar_tensor_tensor(out=OB[:, 8:8 + T], in0=FB[:, 8:8 + T], scalar=1.0 / win_,
                                                                 in1=U[:, 8:8 + T], op0=ALU.mult, op1=ALU.subtract),
                         reads=[FB, U], writes=[OB])
                    for (i0, e0) in ((8, 0), (T, 8)):
                        c.op("dve", lambda e, i0=i0, e0=e0: e.tensor_tensor(out=OB[:, i0:i0 + 8], in0=FB[:, i0:i0 + 8],
                                                                            in1=pedge[:, gi, e0:e0 + 8], op=ALU.mult),
                             reads=[FB, pedge], writes=[OB])
                        c.op("dve", lambda e, i0=i0: e.tensor_tensor(out=OB[:, i0:i0 + 8], in0=OB[:, i0:i0 + 8],
                                                                     in1=U[:, i0:i0 + 8], op=ALU.subtract),
                             reads=[OB, U], writes=[OB])
                    c.op("pool", lambda e: e.tensor_copy(out=PB[:], in_=OB[:, 8:8 + T]), reads=[OB], writes=[PB])
                    for si in range(4):
                        t0 = si * 1024
                        yd_, sg_, ys_ = YD[si % 2], SGd[si % 2], YSd[si % 2]

                        def ev_y(bk, a, n, yd_=yd_, t0=t0):
                            c.op("act", lambda e: e.activation(out=yd_[:, a - t0:a - t0 + n], in_=bk[:, 0:n], func=AF.Identity,
                                                               scale=chp[:, V_PSCALE, gi:gi + 1]), reads=[bk, chp], writes=[yd_])

                        def ev_g(bk, a, n, sg_=sg_, t0=t0):
                            c.op("act", lambda e: e.activation(out=sg_[:, a - t0:a - t0 + n], in_=bk[:, 0:n], func=AF.Silu),
                                 reads=[bk], writes=[sg_])
                        mm_chunks([(wp, (lambda kt: wp[:]), 1, (lambda kt, a, n: PB[:, a:a + n]), PB)], t0, 1024, ev_y)
                        mm_chunks([(wgd, (lambda kt: wgd[:, kt, :]), 8, hn_rhs(0), hn)], t0, 1024, ev_g)
                        if dbg:
                            c.dma(dbg_d[3, gi * 128:(gi + 1) * 128, t0:t0 + 1024], yd_[:], reads=[yd_], writes=[dram_misc],
                                  sem_buf=yd_, store=True)
                        c.op("dve", lambda e, yd_=yd_, sg_=sg_, ys_=ys_: e.tensor_tensor(out=ys_[:], in0=yd_[:], in1=sg_[:], op=ALU.mult),
                             reads=[yd_, sg_], writes=[ys_])
                        c.dma(ysT_d[3, gi * 128:(gi + 1) * 128, t0:t0 + 1024], ys_[:], reads=[ys_], writes=[dram_ys],
                              sem_buf=ys_, store=True)
                c.barrier()
                c.release(wds + wgds + wps + [pedge, U, S1, S2, PB] + YD + SGd + YSd)
            if stop_after == 'D':
                break
            if 'Z' in skip:
                with ExitStack() as ph:
                    zb = c.sbuf("zb", [128, T], BF16, ph)
                    c.op("pool", lambda e: e.memset(zb[:], 0.0), writes=[zb])
                    for n_ in (1, 2):
                        for ct in range(4):
                            c.dma(ysT_d[n_, ct * 128:(ct + 1) * 128, :], zb[:], reads=[zb], writes=[dram_ys], sem_buf=zb, store=True)
                    c.barrier()
                    c.release([zb])
            with ExitStack() as ph:
                wc_ = c.sbuf("wc_", [128, 8, 128], BF16, ph)
                spm = c.sbuf("spm", [128, 2, 3, 16], F32, ph)
                P_LR, P_DT, P_X1, P_TH, P_RHO, P_C, P_S, P_T1, P_T2, P_T3, P_AR, P_AI, P_NR, P_DEN, P_FR, P_FI, P_NFI, P_RHO8 = range(18)
                prm = c.sbuf("prm", [128, 2, 18, 16], F32, ph)
                ck = c.sbuf("ck", [128, 2, 11, 16], F32, ph)
                sk = c.sbuf("sk", [128, 2, 11, 16], F32, ph)
                lp = c.sbuf("lp", [128, 2, 9, 2, 16], F32, ph)
                hpi = c.sbuf("hpi", [128, 1], F32, ph)
                ident = c.sbuf("ident", [128, 128], F32, ph)
                ctab = c.sbuf("ctab", [128, 4, 128], F32, ph)
                stab = c.sbuf("stab", [128, 4, 128], F32, ph)
                ttmp = c.sbuf("ttmp", [128, 128], F32, ph)
                st4 = c.sbuf("st4", [128, 4, 2, 128], F32, ph)
                dg = [c.sbuf("dg%d" % i, [128, 3, 4, 128], BF16, ph) for i in range(2)]
                nlpi = c.sbuf("nlpi", [128, 9, 4], F32, ph)
                identb = c.sbuf("identb", [128, 128], BF16, ph)
                BT = c.sbuf("BT", [128, 4, 2, 128], BF16, ph)
                G = c.sbuf("G", [128, 9, 4, 2, 128], BF16, ph)
                WI = c.sbuf("WI", [128, 8, 2, 4, 128], BF16, ph)
                KT = c.sbuf("KT", [128, 8, 128], BF16, ph)
                YACC = c.sbuf("YACC", [128, T], F32, ph)
                XD = [c.sbuf("XD%d" % i, [128, 8, 512], BF16, ph) for i in range(2)]
                SBr = c.sbuf("SBr", [128, 4, 516], BF16, ph)
                SBi = c.sbuf("SBi", [128, 4, 516], BF16, ph)
                carr = c.sbuf("carr", [128, 2, 4], F32, ph)
                ctmp = c.sbuf("ctmp", [128, 4, 4], F32, ph)
                TM = [c.sbuf("TM%d" % i, [128, 4, 128], F32, ph) for i in range(8)]
                YGB = c.sbuf("YGB", [128, T], BF16, ph)
                for ct in (range(4) if 'C' not in skip else []):
                    load_w(wc_, wc_[:], win_cols(L, 6 * W + ct * 128, 128), 8, 128, fold_g=True)
                    c.op("pool", lambda e: e.memset(hpi[:], float(np.pi / 2)), writes=[hpi])
                    c.dma(ident[:], ident_d, writes=[ident], sem_buf=ident)
                    c.op("pool", lambda e: e.tensor_copy(out=identb[:], in_=ident[:]), reads=[ident], writes=[identb])
                    cs = slice(ct * 4, ct * 4 + 4)
                    if ct == 0:
                        for d_ in range(2):
                            c.dma(spm[:, d_], sp_d[L, d_], writes=[spm], sem_buf=spm)

                    def P(d_, k):
                        return prm[:, d_, k, :]

                    def tt(o, a, b, op, eng="dve", rd=(prm,), wr=(prm,)):
                        c.op(eng, lambda e: e.tensor_tensor(out=o, in0=a, in1=b, op=op), reads=list(rd), writes=list(wr))

                    def tsc(o, a, s1, op0, s2=None, op1=None, rd=(prm,), wr=(prm,)):
                        if op1 is None:
                            c.op("dve", lambda e: e.tensor_scalar(out=o, in0=a, scalar1=s1, scalar2=None, op0=op0),
                                 reads=list(rd), writes=list(wr))
                        else:
                            c.op("dve", lambda e: e.tensor_scalar(out=o, in0=a, scalar1=s1, scalar2=s2, op0=op0, op1=op1),
                                 reads=list(rd), writes=list(wr))

                    def ev_uc(bk, a, n):
                        cc = a // 512
                        c.op("act", lambda e: e.activation(out=YACC[:, a:a + n], in_=bk[:, 0:n], func=AF.Identity,
                                                           scale=chp[:, V_SSMD, ct:ct + 1]), reads=[bk, chp], writes=[YACC])
                        src = bk[:, 0:512].rearrange("p (m i) -> p i m", i=8)
                        c.op("act", lambda e: e.activation(out=XD[0][:, :, 64 * cc:64 * cc + 64], in_=src, func=AF.Identity),
                             reads=[bk], writes=[XD[0]])
                    mm_chunks([(wc_, (lambda kt: wc_[:, kt, :]), 8, hn_rhs(0), hn)], 0, T, ev_uc)
                    c.op("pool", lambda e: e.tensor_copy(out=XD[1][:], in_=XD[0][:, ::-1, ::-1]), reads=[XD[0]], writes=[XD[1]])

                    for d_ in ([slice(0, 2)] if ct == 0 else []):
                        tsc(P(d_, P_LR), spm[:, d_, 0, :], -1e-4, ALU.min, rd=(spm,))
                        c.op("act", lambda e: e.activation(out=P(d_, P_DT), in_=spm[:, d_, 2, :], func=AF.Exp), reads=[spm], writes=[prm])
                        tt(P(d_, P_X1), P(d_, P_LR), P(d_, P_DT), ALU.mult)
                        tt(P(d_, P_TH), spm[:, d_, 1, :], P(d_, P_DT), ALU.mult, rd=(prm, spm))
                        c.op("act", lambda e: e.activation(out=P(d_, P_RHO), in_=P(d_, P_X1), func=AF.Exp), reads=[prm], writes=[prm])
                        c.op("act", lambda e: e.activation(out=P(d_, P_RHO8), in_=P(d_, P_X1), func=AF.Exp, scale=8.0), reads=[prm], writes=[prm])
                        c.op("act", lambda e: e.activation(out=P(d_, P_S), in_=P(d_, P_TH), func=AF.Sin, scale=1.0 / 64), reads=[prm], writes=[prm])
                        c.op("act", lambda e: e.activation(out=P(d_, P_C), in_=P(d_, P_TH), func=AF.Sin, scale=1.0 / 64,
                                                           bias=hpi[:, 0:1]), reads=[prm, hpi], writes=[prm])
                        for it in range(6):
                            tt(P(d_, P_T1), P(d_, P_C), P(d_, P_S), ALU.mult)
                            tt(P(d_, P_T2), P(d_, P_C), P(d_, P_C), ALU.mult)
                            tt(P(d_, P_T3), P(d_, P_S), P(d_, P_S), ALU.mult)
                            tt(P(d_, P_C), P(d_, P_T2), P(d_, P_T3), ALU.subtract)
                            tsc(P(d_, P_S), P(d_, P_T1), 2.0, ALU.mult)
                        c.op("dve", lambda e: e.tensor_copy(out=ck[:, d_, 0, :], in_=P(d_, P_C)), reads=[prm], writes=[ck])
                        c.op("dve", lambda e: e.tensor_copy(out=sk[:, d_, 0, :], in_=P(d_, P_S)), reads=[prm], writes=[sk])
                        for k in range(1, 11):
                            tt(P(d_, P_T1), ck[:, d_, k - 1, :], sk[:, d_, k - 1, :], ALU.mult, rd=(ck, sk))
                            tt(P(d_, P_T2), ck[:, d_, k - 1, :], ck[:, d_, k - 1, :], ALU.mult, rd=(ck,))
                            tt(P(d_, P_T3), sk[:, d_, k - 1, :], sk[:, d_, k - 1, :], ALU.mult, rd=(sk,))
                            tt(ck[:, d_, k, :], P(d_, P_T2), P(d_, P_T3), ALU.subtract, wr=(ck,))
                            tsc(sk[:, d_, k, :], P(d_, P_T1), 2.0, ALU.mult, wr=(sk,))
                        tt(P(d_, P_AR), P(d_, P_RHO), P(d_, P_C), ALU.mult)
                        tt(P(d_, P_AI), P(d_, P_RHO), P(d_, P_S), ALU.mult)
                        tsc(P(d_, P_NR), P(d_, P_AR), -1.0, ALU.add)
                        tt(P(d_, P_T1), P(d_, P_LR), P(d_, P_LR), ALU.mult)
                        tt(P(d_, P_T2), spm[:, d_, 1, :], spm[:, d_, 1, :], ALU.mult, rd=(spm,))
                        tt(P(d_, P_DEN), P(d_, P_T1), P(d_, P_T2), ALU.add)
                        c.op("dve", lambda e: e.reciprocal(out=P(d_, P_DEN), in_=P(d_, P_DEN)), reads=[prm], writes=[prm])
                        tt(P(d_, P_T1), P(d_, P_NR), P(d_, P_LR), ALU.mult)
                        tt(P(d_, P_T2), P(d_, P_AI), spm[:, d_, 1, :], ALU.mult, rd=(prm, spm))
                        tt(P(d_, P_T1), P(d_, P_T1), P(d_, P_T2), ALU.add)
                        tt(P(d_, P_FR), P(d_, P_T1), P(d_, P_DEN), ALU.mult)
                        tt(P(d_, P_T1), P(d_, P_AI), P(d_, P_LR), ALU.mult)
                        tt(P(d_, P_T2), P(d_, P_NR), spm[:, d_, 1, :], ALU.mult, rd=(prm, spm))
                        tt(P(d_, P_T1), P(d_, P_T1), P(d_, P_T2), ALU.subtract)
                        tt(P(d_, P_FI), P(d_, P_T1), P(d_, P_DEN), ALU.mult)
                        tsc(P(d_, P_NFI), P(d_, P_FI), -1.0, ALU.mult)
                        c.op("pool", lambda e: e.memset(lp[:, d_, 0, 0, :], 1.0), writes=[lp])
                        c.op("pool", lambda e: e.memset(lp[:, d_, 0, 1, :], 0.0), writes=[lp])
                        c.op("dve", lambda e: e.tensor_copy(out=lp[:, d_, 1, 0, :], in_=P(d_, P_AR)), reads=[prm], writes=[lp])
                        c.op("dve", lambda e: e.tensor_copy(out=lp[:, d_, 1, 1, :], in_=P(d_, P_AI)), reads=[prm], writes=[lp])
                        for k in range(2, 9):
                            pr_, pi__ = lp[:, d_, k - 1, 0, :], lp[:, d_, k - 1, 1, :]
                            tt(P(d_, P_T1), pr_, P(d_, P_AR), ALU.mult, rd=(lp, prm))
                            tt(P(d_, P_T2), pi__, P(d_, P_AI), ALU.mult, rd=(lp, prm))
                            tt(lp[:, d_, k, 0, :], P(d_, P_T1), P(d_, P_T2), ALU.subtract, wr=(lp,))
                            tt(P(d_, P_T1), pr_, P(d_, P_AI), ALU.mult, rd=(lp, prm))
                            tt(P(d_, P_T2), pi__, P(d_, P_AR), ALU.mult, rd=(lp, prm))
                            tt(lp[:, d_, k, 1, :], P(d_, P_T1), P(d_, P_T2), ALU.add, wr=(lp,))
                    for d_ in (range(2) if (bstage is None or bstage >= 2) else []):
                        c.op("pool", lambda e: e.memset(ctab[:, :, 0:1], 1.0), writes=[ctab])
                        c.op("pool", lambda e: e.memset(stab[:, :, 0:1], 0.0), writes=[stab])
                        for k in range(7):
                            n = 1 << k
                            cs_b = ck[:, d_, k + 3, cs].unsqueeze(2).to_broadcast([128, 4, n])
                            ss_b = sk[:, d_, k + 3, cs].unsqueeze(2).to_broadcast([128, 4, n])
                            ta, tb = TM[4], TM[5]
                            c.op("dve", lambda e: e.tensor_tensor(out=ta[:, :, 0:n], in0=stab[:, :, 0:n], in1=ss_b, op=ALU.mult),
                                 reads=[stab, sk], writes=[ta])
                            c.op("dve", lambda e: e.tensor_tensor(out=tb[:, :, 0:n], in0=ctab[:, :, 0:n], in1=cs_b, op=ALU.mult),
                                 reads=[ctab, ck], writes=[tb])
                            c.op("dve", lambda e: e.tensor_tensor(out=ctab[:, :, n:2 * n], in0=tb[:, :, 0:n], in1=ta[:, :, 0:n], op=ALU.subtract),
                                 reads=[ta, tb], writes=[ctab])
                            tc2, td2 = TM[6], TM[7]
                            c.op("dve", lambda e: e.tensor_tensor(out=tc2[:, :, 0:n], in0=ctab[:, :, 0:n], in1=ss_b, op=ALU.mult),
                                 reads=[ctab, sk], writes=[tc2])
                            c.op("dve", lambda e: e.tensor_tensor(out=td2[:, :, 0:n], in0=stab[:, :, 0:n], in1=cs_b, op=ALU.mult),
                                 reads=[stab, ck], writes=[td2])
                            c.op("dve", lambda e: e.tensor_tensor(out=stab[:, :, n:2 * n], in0=td2[:, :, 0:n], in1=tc2[:, :, 0:n], op=ALU.add),
                                 reads=[tc2, td2], writes=[stab])
                        c.dma(BT[:], bpadT_d[L, d_, ct * 4:(ct + 1) * 4].rearrange("s r p c -> p s r c"), writes=[BT], sem_buf=BT, q="pool")
                        c.dma(st4[:], cpad_d[L, d_, ct * 4:(ct + 1) * 4].rearrange("s r p c -> p s r c"), writes=[st4], sem_buf=st4)
                        fr_b, fi_b, nfi_b = (prm[:, d_, k_, cs].unsqueeze(2).to_broadcast([128, 4, 128]) for k_ in (P_FR, P_FI, P_NFI))
                        ta, tb, tc2, td2 = TM[4], TM[5], TM[6], TM[7]
                        c.op("dve", lambda e: e.tensor_tensor(out=ta[:], in0=st4[:, :, 1, :], in1=fi_b, op=ALU.mult), reads=[st4, prm], writes=[ta])
                        c.op("dve", lambda e: e.tensor_tensor(out=tb[:], in0=st4[:, :, 0, :], in1=fr_b, op=ALU.mult), reads=[st4, prm], writes=[tb])
                        c.op("pool", lambda e: e.tensor_tensor(out=G[:, 0, :, 0, :], in0=tb[:], in1=ta[:], op=ALU.subtract), reads=[ta, tb], writes=[G])
                        c.op("dve", lambda e: e.tensor_tensor(out=tc2[:], in0=st4[:, :, 1, :], in1=fr_b, op=ALU.mult), reads=[st4, prm], writes=[tc2])
                        c.op("dve", lambda e: e.tensor_tensor(out=td2[:], in0=st4[:, :, 0, :], in1=nfi_b, op=ALU.mult), reads=[st4, prm], writes=[td2])
                        c.op("pool", lambda e: e.tensor_tensor(out=G[:, 0, :, 1, :], in0=td2[:], in1=tc2[:], op=ALU.subtract), reads=[tc2, td2], writes=[G])
                        tsc(nlpi[:, :, :], lp[:, d_, :, 1, cs], -1.0, ALU.mult, rd=(lp,), wr=(nlpi,))
                        idb = ident[:].unsqueeze(1).to_broadcast([128, 4, 128])
                        for k in range(0, 9):
                            dset = dg[k % 2]
                            if k >= 1:
                                for q_, src_ in ((0, lp[:, d_, k, 0, cs]), (1, lp[:, d_, k, 1, cs]), (2, nlpi[:, k, :])):
                                    c.op("dve", lambda e, q_=q_, src_=src_, dset=dset: e.tensor_tensor(
                                        out=dset[:, q_], in0=idb, in1=src_.unsqueeze(2).to_broadcast([128, 4, 128]), op=ALU.mult),
                                        reads=[ident, lp, nlpi], writes=[dset])
                                bkr, bki = bank(), bank()
                                for st_ in range(4):
                                    sl_ = slice(st_ * 128, (st_ + 1) * 128)
                                    c.op("pe", lambda e, st_=st_, sl_=sl_: e.matmul(bkr[:, sl_], lhsT=dset[:, 0, st_, :], rhs=G[:, 0, st_, 0, :],
                                                                                   start=True, stop=False), reads=[dset, G], writes=[bkr])
                                    c.op("pe", lambda e, st_=st_, sl_=sl_: e.matmul(bkr[:, sl_], lhsT=dset[:, 1, st_, :], rhs=G[:, 0, st_, 1, :],
                                                                                   start=False, stop=True), reads=[dset, G], writes=[bkr])
                                for st_ in range(4):
                                    sl_ = slice(st_ * 128, (st_ + 1) * 128)
                                    c.op("pe", lambda e, st_=st_, sl_=sl_: e.matmul(bki[:, sl_], lhsT=dset[:, 0, st_, :], rhs=G[:, 0, st_, 1, :],
                                                                                   start=True, stop=False), reads=[dset, G], writes=[bki])
                                    c.op("pe", lambda e, st_=st_, sl_=sl_: e.matmul(bki[:, sl_], lhsT=dset[:, 2, st_, :], rhs=G[:, 0, st_, 0, :],
                                                                                   start=False, stop=True), reads=[dset, G], writes=[bki])
                                c.op("act", lambda e, k=k: e.activation(out=G[:, k, :, 0, :], in_=bkr[:].rearrange("p (a b) -> p a b", a=4),
                                                                        func=AF.Identity), reads=[bkr], writes=[G])
                                c.op("act", lambda e, k=k: e.activation(out=G[:, k, :, 1, :], in_=bki[:].rearrange("p (a b) -> p a b", a=4),
                                                                        func=AF.Identity), reads=[bki], writes=[G])
                            if k <= 7:
                                i = 7 - k
                                bkr, bki = bank(), bank()
                                for st_ in range(4):
                                    sl_ = slice(st_ * 128, (st_ + 1) * 128)
                                    if k == 0:
                                        c.op("pe", lambda e, st_=st_, sl_=sl_: e.matmul(bkr[:, sl_], lhsT=BT[:, st_, 0, :], rhs=identb[:],
                                                                                       start=True, stop=True), reads=[BT, identb], writes=[bkr])
                                    else:
                                        c.op("pe", lambda e, st_=st_, sl_=sl_: e.matmul(bkr[:, sl_], lhsT=BT[:, st_, 0, :], rhs=dset[:, 0, st_, :],
                                                                                       start=True, stop=False), reads=[BT, dset], writes=[bkr])
                                        c.op("pe", lambda e, st_=st_, sl_=sl_: e.matmul(bkr[:, sl_], lhsT=BT[:, st_, 1, :], rhs=dset[:, 2, st_, :],
                                                                                       start=False, stop=True), reads=[BT, dset], writes=[bkr])
                                for st_ in range(4):
                                    sl_ = slice(st_ * 128, (st_ + 1) * 128)
                                    if k == 0:
                                        c.op("pe", lambda e, st_=st_, sl_=sl_: e.matmul(bki[:, sl_], lhsT=BT[:, st_, 1, :], rhs=identb[:],
                                                                                       start=True, stop=True), reads=[BT, identb], writes=[bki])
                                    else:
                                        c.op("pe", lambda e, st_=st_, sl_=sl_: e.matmul(bki[:, sl_], lhsT=BT[:, st_, 0, :], rhs=dset[:, 1, st_, :],
                                                                                       start=True, stop=False), reads=[BT, dset], writes=[bki])
                                        c.op("pe", lambda e, st_=st_, sl_=sl_: e.matmul(bki[:, sl_], lhsT=BT[:, st_, 1, :], rhs=dset[:, 0, st_, :],
                                                                                       start=False, stop=True), reads=[BT, dset], writes=[bki])
                                c.op("act", lambda e, i=i: e.activation(out=WI[:, i, 0].rearrange("p a b -> p (a b)"), in_=bkr[:], func=AF.Identity),
                                     reads=[bkr], writes=[WI])
                                c.op("act", lambda e, i=i: e.activation(out=WI[:, i, 1].rearrange("p a b -> p (a b)"), in_=bki[:], func=AF.Identity),
                                     reads=[bki], writes=[WI])
                        for hb in range(2):
                            bk = bank()
                            for tq in range(4):
                                tau = hb * 4 + tq
                                i_ = 0
                                for st_ in range(4):
                                    for ri in range(2):
                                        c.op("pe", lambda e, tq=tq, tau=tau, st_=st_, ri=ri, i_=i_, bk=bk: e.matmul(
                                            bk[:, tq * 128:(tq + 1) * 128], lhsT=BT[:, st_, ri, :], rhs=G[:, tau, st_, ri, :],
                                            start=(i_ == 0), stop=(i_ == 7)), reads=[BT, G], writes=[bk])
                                        i_ += 1
                            c.op("act", lambda e, bk=bk, hb=hb: e.activation(
                                out=KT[:, hb * 4:hb * 4 + 4, :], in_=bk[:].rearrange("p (a b) -> p a b", a=4), func=AF.Identity),
                                reads=[bk], writes=[KT])
                        if bstage is not None and bstage < 3:
                            continue
                        c.op("pool", lambda e: e.memset(carr[:], 0.0), writes=[carr])
                        c.op("pool", lambda e: e.memset(SBr[:, :, 0:1], 0.0), writes=[SBr])
                        c.op("pool", lambda e: e.memset(SBi[:, :, 0:1], 0.0), writes=[SBi])
                        X_ = XD[d_]
                        for u_ in range(4):
                            m0 = u_ * 128
                            PSr, PSi = bank(), bank()
                            for ri, PS_ in ((0, PSr), (1, PSi)):
                                for st_ in range(4):
                                    for i in range(8):
                                        c.op("pe", lambda e, ri=ri, PS_=PS_, st_=st_, i=i: e.matmul(
                                            PS_[:, st_ * 128:(st_ + 1) * 128], lhsT=WI[:, i, ri, st_, :], rhs=X_[:, i, m0:m0 + 128],
                                            start=(i == 0), stop=(i == 7)), reads=[WI, X_], writes=[PS_])
                            pr = PSr[:].rearrange("p (s j) -> p s j", s=4)
                            pi_ = PSi[:].rearrange("p (s j) -> p s j", s=4)
                            T1, T2, T3, T4, BR, BI, SR, SI = TM
                            c.op("dve", lambda e: e.tensor_tensor(out=T1[:], in0=pr, in1=ctab[:], op=ALU.mult), reads=[PSr, ctab], writes=[T1])
                            c.op("dve", lambda e: e.tensor_tensor(out=T2[:], in0=pi_, in1=stab[:], op=ALU.mult), reads=[PSi, stab], writes=[T2])
                            c.op("pool", lambda e: e.tensor_tensor(out=BR[:], in0=T1[:], in1=T2[:], op=ALU.add), reads=[T1, T2], writes=[BR])
                            c.op("dve", lambda e: e.tensor_tensor(out=T3[:], in0=pi_, in1=ctab[:], op=ALU.mult), reads=[PSi, ctab], writes=[T3])
                            c.op("dve", lambda e: e.tensor_tensor(out=T4[:], in0=pr, in1=stab[:], op=ALU.mult), reads=[PSr, stab], writes=[T4])
                            c.op("pool", lambda e: e.tensor_tensor(out=BI[:], in0=T3[:], in1=T4[:], op=ALU.subtract), reads=[T3, T4], writes=[BI])
                            for (B_, S_, ci) in ((BR, SR, 0), (BI, SI, 1)):
                                for st_ in range(4):
                                    c.op("dve", lambda e, B_=B_, S_=S_, st_=st_, ci=ci: e.tensor_tensor_scan(
                                        out=S_[:, st_, :], data0=prm[:, d_, P_RHO8, ct * 4 + st_:ct * 4 + st_ + 1].to_broadcast([128, 128]), data1=B_[:, st_, :],
                                        initial=carr[:, ci, st_:st_ + 1], op0=ALU.mult, op1=ALU.add),
                                        reads=[B_, prm, carr], writes=[S_])
                            lr_, li_ = SR[:, :, 127], SI[:, :, 127]
                            c7, s7 = ck[:, d_, 10, cs], sk[:, d_, 10, cs]
                            c.op("dve", lambda e: e.tensor_tensor(out=ctmp[:, 0, :], in0=lr_, in1=c7, op=ALU.mult), reads=[SR, ck], writes=[ctmp])
                            c.op("dve", lambda e: e.tensor_tensor(out=ctmp[:, 1, :], in0=li_, in1=s7, op=ALU.mult), reads=[SI, sk], writes=[ctmp])
                            c.op("dve", lambda e: e.tensor_tensor(out=ctmp[:, 2, :], in0=lr_, in1=s7, op=ALU.mult), reads=[SR, sk], writes=[ctmp])
                            c.op("dve", lambda e: e.tensor_tensor(out=ctmp[:, 3, :], in0=li_, in1=c7, op=ALU.mult), reads=[SI, ck], writes=[ctmp])
                            c.op("dve", lambda e: e.tensor_tensor(out=carr[:, 0, :], in0=ctmp[:, 0, :], in1=ctmp[:, 1, :], op=ALU.subtract),
                                 reads=[ctmp], writes=[carr])
                            c.op("dve", lambda e: e.tensor_tensor(out=carr[:, 1, :], in0=ctmp[:, 2, :], in1=ctmp[:, 3, :], op=ALU.add),
                                 reads=[ctmp], writes=[carr])
                            R1, R2, R3, R4 = T1, T2, T3, T4
                            c.op("pool", lambda e: e.tensor_tensor(out=R1[:], in0=SR[:], in1=ctab[:], op=ALU.mult), reads=[SR, ctab], writes=[R1])
                            c.op("pool", lambda e: e.tensor_tensor(out=R2[:], in0=SI[:], in1=stab[:], op=ALU.mult), reads=[SI, stab], writes=[R2])
                            c.op("dve", lambda e: e.tensor_tensor(out=SBr[:, :, 1 + m0:1 + m0 + 128], in0=R1[:], in1=R2[:], op=ALU.subtract),
                                 reads=[R1, R2], writes=[SBr])
                            c.op("pool", lambda e: e.tensor_tensor(out=R3[:], in0=SR[:], in1=stab[:], op=ALU.mult), reads=[SR, stab], writes=[R3])
                            c.op("dve", lambda e: e.tensor_tensor(out=R4[:], in0=SI[:], in1=ctab[:], op=ALU.mult), reads=[SI, ctab], writes=[R4])
                            c.op("pool", lambda e: e.tensor_tensor(out=SBi[:, :, 1 + m0:1 + m0 + 128], in0=R3[:], in1=R4[:], op=ALU.add),
                                 reads=[R3, R4], writes=[SBi])
                        if bstage is not None and bstage < 4:
                            continue
                        yv = YACC[:] if d_ == 0 else YACC[:, ::-1]
                        for j in range(8):
                            PY = bank()
                            tot = (j + 1) + 8
                            i_ = 0
                            for i in range(j + 1):
                                c.op("pe", lambda e, i=i, j=j, i_=i_, PY=PY: e.matmul(PY[:, 0:512], lhsT=KT[:, j - i, :], rhs=X_[:, i, :],
                                                                                   start=(i_ == 0), stop=(i_ == tot - 1)),
                                     reads=[KT, X_], writes=[PY])
                                i_ += 1
                            for st_ in range(4):
                                for ri, SB_ in ((0, SBr), (1, SBi)):
                                    c.op("pe", lambda e, st_=st_, ri=ri, SB_=SB_, j=j, i_=i_, PY=PY: e.matmul(
                                        PY[:, 0:512], lhsT=G[:, j + 1, st_, ri, :], rhs=SB_[:, st_, 0:512],
                                        start=(i_ == 0), stop=(i_ == tot - 1)), reads=[G, SB_], writes=[PY])
                                    i_ += 1
                            c.op("dve", lambda e, j=j, PY=PY: e.tensor_tensor(out=yv[:, j::8], in0=PY[:, 0:512], in1=yv[:, j::8], op=ALU.add),
                                 reads=[PY, YACC], writes=[YACC])
                    for cc in range(8):
                        ysl = YACC[:, cc * 512:(cc + 1) * 512]
                        g_ = TM[cc % 4][:].rearrange("p a b -> p (a b)")
                        gb_ = TM[cc % 4]
                        c.op("act", lambda e, ysl=ysl, g_=g_: e.activation(out=g_, in_=ysl, func=AF.Square), reads=[YACC], writes=[gb_])
                        c.op("dve", lambda e, g_=g_: e.tensor_scalar(out=g_, in0=g_, scalar1=0.044715, scalar2=1.0, op0=ALU.mult, op1=ALU.add),
                             reads=[gb_], writes=[gb_])
                        c.op("pool", lambda e, ysl=ysl, g_=g_: e.tensor_tensor(out=g_, in0=g_, in1=ysl, op=ALU.mult), reads=[gb_, YACC], writes=[gb_])
                        c.op("act", lambda e, g_=g_: e.activation(out=g_, in_=g_, func=AF.Sigmoid, scale=1.5957691216057308), reads=[gb_], writes=[gb_])
                        c.op("dve", lambda e, ysl=ysl, g_=g_: e.tensor_tensor(out=ysl, in0=ysl, in1=g_, op=ALU.mult), reads=[gb_, YACC], writes=[YACC])
                    c.op("pool", lambda e: e.tensor_copy(out=YGB[:], in_=YACC[:]), reads=[YACC], writes=[YGB])
                    c.dma(ygT_d[ct * 128:(ct + 1) * 128, :], YACC[:], reads=[YACC], writes=[dram_misc], sem_buf=YACC, store=True)
                    c.dma(ygb_d[ct * 128:(ct + 1) * 128, :], YGB[:], reads=[YGB], writes=[dram_misc], sem_buf=YGB, store=True)
                c.barrier()
                c.release([wc_, spm, prm, ck, sk, lp, hpi, ident, identb, nlpi, st4, ctab, stab, ttmp, BT, G, WI, KT, YACC,
                           SBr, SBi, carr, ctmp, YGB] + XD + TM + dg)
            if 'C' not in skip:
                with ExitStack() as ph:
                    ygb = c.sbuf("ygb", [128, 4, T], BF16, ph)
                    wgl = c.sbuf("wgl", [128, 4, 128], BF16, ph)
                    wcg = c.sbuf("wcg", [128, 8, 128], BF16, ph)
                    SGL = [c.sbuf("SGL%d" % i, [128, 1024], F32, ph) for i in range(2)]
                    SGC = [c.sbuf("SGC%d" % i, [128, 1024], F32, ph) for i in range(2)]
                    YGS = [c.sbuf("YGS%d" % i, [128, 1024], F32, ph) for i in range(2)]
                    YSC = [c.sbuf("YSC%d" % i, [128, 1024], BF16, ph) for i in range(2)]
                    c.dma(ygb[:], ygb_d.rearrange("(k p) t -> p k t", p=128), reads=[dram_misc], writes=[ygb], sem_buf=ygb)
                    for ct in range(4):
                        load_w(wgl, wgl[:], gluw_d[L, :, ct * 128:(ct + 1) * 128].rearrange("(kt p) n -> p kt n", p=128), 4, 128)
                        load_w(wcg, wcg[:], win_cols(L, 7 * W + ct * 128, 128), 8, 128, fold_g=True)
                        for si in range(4):
                            t0 = si * 1024
                            sgl, sgc, ygs, ysc = SGL[si % 2], SGC[si % 2], YGS[si % 2], YSC[si % 2]
                            c.dma(ygs[:], ygT_d[ct * 128:(ct + 1) * 128, t0:t0 + 1024], reads=[dram_misc], writes=[ygs], sem_buf=ygs)

                            def ev_l(bk, a, n, sgl=sgl, t0=t0):
                                c.op("act", lambda e: e.activation(out=sgl[:, a - t0:a - t0 + n], in_=bk[:, 0:n], func=AF.Sigmoid,
                                                                   bias=chp[:, V_GLUB, ct:ct + 1]), reads=[bk, chp], writes=[sgl])

                            def ev_c(bk, a, n, sgc=sgc, t0=t0):
                                c.op("act", lambda e: e.activation(out=sgc[:, a - t0:a - t0 + n], in_=bk[:, 0:n], func=AF.Silu),
                                     reads=[bk], writes=[sgc])
                            mm_chunks([(wgl, (lambda kt: wgl[:, kt, :]), 4, (lambda kt, a, n: ygb[:, kt, a:a + n]), ygb)], t0, 1024, ev_l)
                            mm_chunks([(wcg, (lambda kt: wcg[:, kt, :]), 8, hn_rhs(0), hn)], t0, 1024, ev_c)
                            c.op("dve", lambda e, ygs=ygs, sgl=sgl: e.tensor_tensor(out=ygs[:], in0=ygs[:], in1=sgl[:], op=ALU.mult),
                                 reads=[ygs, sgl], writes=[ygs])
                            if dbg:
                                c.dma(dbg_d[2, ct * 128:(ct + 1) * 128, t0:t0 + 1024], ygs[:], reads=[ygs], writes=[dram_misc],
                                      sem_buf=ygs, store=True)
                            c.op("pool", lambda e, ygs=ygs, sgc=sgc, ysc=ysc: e.tensor_tensor(out=ysc[:], in0=ygs[:], in1=sgc[:], op=ALU.mult),
                                 reads=[ygs, sgc], writes=[ysc])
                            c.dma(ysT_d[2, ct * 128:(ct + 1) * 128, t0:t0 + 1024], ysc[:], reads=[ysc], writes=[dram_ys],
                                  sem_buf=ysc, store=True)
                    c.barrier()
                    c.release([ygb, wgl, wcg] + SGL + SGC + YGS + YSC)
            if stop_after == 'C':
                break
            with ExitStack() as ph:
                wqs = [c.sbuf("wq%d" % i, [128, 8, 128], BF16, ph) for i in range(2)]
                wks = [c.sbuf("wk%d" % i, [128, 8, 128], BF16, ph) for i in range(2)]
                wvs = [c.sbuf("wv%d" % i, [128, 8, 128], BF16, ph) for i in range(2)]
                wbgs = [c.sbuf("wbg%d" % i, [128, 8, 128], BF16, ph) for i in range(2)]

                def loadB(h_):
                    for lst, col in ((wqs, 2), (wks, 3), (wvs, 4), (wbgs, 5)):
                        load_w(lst[h_ % 2], lst[h_ % 2][:], win_cols(L, col * W + h_ * 128, 128), 8, 128)
                if 'B' not in skip:
                    loadB(0)
                qkg = c.sbuf("qkg", [128, 2], F32, ph)
                eps2 = c.sbuf("eps2", [128, 1], F32, ph)
                QZ = c.sbuf("QZ", [128, 64, 2, 64], BF16, ph)
                KN = c.sbuf("KN", [128, T], BF16, ph)
                Zf = c.sbuf("Zf", [128, 1024], F32, ph)
                SQb = c.sbuf("SQb", [128, 1024], BF16, ph)
                RS = c.sbuf("RS", [128, 1024], F32, ph)
                Vs = [c.sbuf("VA%d" % i, [128, 32, 2, 65], BF16, ph) for i in range(2)]
                identf = c.sbuf("identf", [128, 128], F32, ph)
                Yn = [c.sbuf("Yn%d" % i, [64, 2, 64], F32, ph) for i in range(2)]
                rec = [c.sbuf("rec%d" % i, [64, 2], F32, ph) for i in range(2)]
                bmk = c.sbuf("bmk", [128, 8, 512], F32, ph)
                ES = [c.sbuf("ES%d" % i, [128, 512], F32, ph) for i in range(2)]
                EB = [c.sbuf("EB%d" % i, [128, 512], BF16, ph) for i in range(2)]
                RD = c.sbuf("RD", [128, 512], F32, ph)
                SGF = c.sbuf("SGF", [128, T], F32, ph)
                YB = [c.sbuf("YB%d" % i, [128, 512], F32, ph) for i in range(2)]
                SGb = [c.sbuf("SGb%d" % i, [128, 512], F32, ph) for i in range(2)]
                YSb = [c.sbuf("YSb%d" % i, [128, 512], BF16, ph) for i in range(2)]
                for hp in (range(4) if 'B' not in skip else []):
                    wq, wk, wv, wbg = wqs[hp % 2], wks[hp % 2], wvs[hp % 2], wbgs[hp % 2]
                    if hp + 1 < 4:
                        loadB(hp + 1)
                    c.dma(qkg[:], qkg_d[L], writes=[qkg], sem_buf=qkg)
                    c.dma(bmk[:], bm_d[L, hp].rearrange("p c h x -> p c (h x)"), writes=[bmk], sem_buf=bmk)
                    c.op("pool", lambda e: e.memset(eps2[:], 1e-6), writes=[eps2])
                    c.op("dve", lambda e: e.tensor_scalar(out=qkg[:, 0:1], in0=qkg[:, 0:1], scalar1=0.125, scalar2=None, op0=ALU.mult),
                         reads=[qkg], writes=[qkg])
                    c.op("pool", lambda e: e.memset(QZ[64:128, :, 0, :], 0.0), writes=[QZ])
                    c.op("pool", lambda e: e.memset(QZ[0:64, :, 1, :], 0.0), writes=[QZ])
                    for (w_, gi_) in ((wq, 0), (wk, 1)):
                        for si in range(4):
                            t0 = si * 1024

                            def ev_z(bk, a, n, t0=t0):
                                o = a - t0
                                c.op("act", lambda e: e.activation(out=Zf[:, o:o + n], in_=bk[:, 0:n], func=AF.Identity),
                                     reads=[bk], writes=[Zf])
                                c.op("act", lambda e: e.activation(out=SQb[:, o:o + n], in_=bk[:, 0:n], func=AF.Square),
                                     reads=[bk], writes=[SQb])
                            mm_chunks([(w_, (lambda kt, w_=w_: w_[:, kt, :]), 8, hn_rhs(0), hn)], t0, 1024, ev_z)

                            def ev_r(bk, a, n, t0=t0):
                                o = a - t0
                                c.op("act", lambda e: e.activation(out=RS[:, o:o + n], in_=bk[:, 0:n], func=AF.Ln, scale=1.0 / 64,
                                                                   bias=eps2[:, 0:1]), reads=[bk, eps2], writes=[RS])
                            mm_chunks([(blk1, (lambda kt: blk1[:]), 1, (lambda kt, a, n, t0=t0: SQb[:, a - t0:a - t0 + n]), SQb)],
                                      t0, 1024, ev_r)
                            c.op("act", lambda e: e.activation(out=RS[:], in_=RS[:], func=AF.Exp, scale=-0.5), reads=[RS], writes=[RS])
                            if gi_ == 1:
                                c.op("dve", lambda e, t0=t0: e.scalar_tensor_tensor(
                                    out=KN[:, t0:t0 + 1024], in0=Zf[:], scalar=qkg[:, 1:2], in1=RS[:], op0=ALU.mult, op1=ALU.mult),
                                    reads=[Zf, qkg, RS], writes=[KN])
                            else:
                                for h2 in range(2):
                                    ps_ = slice(h2 * 64, (h2 + 1) * 64)
                                    c.op("dve", lambda e, t0=t0, h2=h2, ps_=ps_: e.scalar_tensor_tensor(
                                        out=QZ[ps_, t0 // 64:t0 // 64 + 16, h2, :], in0=Zf[ps_, :].rearrange("p (r q) -> p r q", q=64),
                                        scalar=qkg[ps_, 0:1], in1=RS[ps_, :].rearrange("p (r q) -> p r q", q=64),
                                        op0=ALU.mult, op1=ALU.mult), reads=[Zf, qkg, RS], writes=[QZ])
                    if hp == 0:
                        c.dma(identf[:], ident_d, writes=[identf], sem_buf=identf)
                        for par in range(2):
                            c.op("pool", lambda e, par=par: e.memset(Vs[par][:, :, :, 64:65], 1.0), writes=[Vs[par]])
                    for par in range(2):
                        ntile = 32 - par
                        for tg in range(0, ntile, 4):
                            bk = bank()
                            nt_ = min(4, ntile - tg)
                            for j in range(nt_):
                                tt_ = tg + j
                                for kt in range(8):
                                    c.op("pe", lambda e, j=j, tt_=tt_, kt=kt, bk=bk: e.matmul(
                                        bk[:, j * 128:(j + 1) * 128], lhsT=hn[:, kt, 2 + 64 * par + tt_ * 128:2 + 64 * par + (tt_ + 1) * 128],
                                        rhs=wv[:, kt, :], start=(kt == 0), stop=(kt == 7)), reads=[hn, wv], writes=[bk])
                            c.op("act", lambda e, bk=bk, tg=tg, nt_=nt_: e.activation(
                                out=Vs[par][:, tg:tg + nt_, :, 0:64], in_=bk[:, 0:nt_ * 128].rearrange("p (a h d) -> p a h d", a=nt_, h=2),
                                func=AF.Identity), reads=[bk], writes=[Vs[par]])
                    def ev_bg(bk, a, n):
                        c.op("act", lambda e: e.activation(out=SGF[:, a:a + n], in_=bk[:, 0:n], func=AF.Silu), reads=[bk], writes=[SGF])
                    mm_chunks([(wbg, (lambda kt: wbg[:, kt, :]), 8, hn_rhs(0), hn)], 0, T, ev_bg)
                    state["nb"] = 6
                    nrows = 64 if bstage is None else 8 * bstage

                    def emit_qk(r):
                        rs_ = min(max(r - 4, 0), 56)
                        cfg = r - rs_
                        par = rs_ % 2
                        tt0 = (rs_ - par) // 2
                        k0 = 64 * rs_
                        bS = bank()
                        for kt in range(4):
                            c.op("pe", lambda e, kt=kt: e.matmul(
                                bS[:, kt * 128:(kt + 1) * 128], lhsT=KN[:, k0 + kt * 128:k0 + (kt + 1) * 128],
                                rhs=QZ[:, r, :, :].rearrange("p h q -> p (h q)"), start=True, stop=True),
                                reads=[KN, QZ], writes=[bS])
                        es_, eb_ = ES[r % 2], EB[r % 2]
                        c.op("dve", lambda e: e.tensor_tensor(out=es_[:], in0=bS[:], in1=bmk[:, cfg, :], op=ALU.add),
                             reads=[bS, bmk], writes=[es_])
                        c.op("act", lambda e: e.activation(out=eb_[:], in_=es_[:], func=AF.Exp), reads=[es_], writes=[eb_])
                        return eb_, par, tt0

                    def emit_tr(r):
                        rg, rl = r // 8, r % 8
                        bT = pb[6 + rg % 2]
                        yn_ = Yn[r % 2]
                        c.op("pe", lambda e: e.transpose(bT[:, rl * 64:(rl + 1) * 64], yn_[:].rearrange("p h d -> p (h d)"), identf[0:64, 0:64]),
                             reads=[yn_, identf], writes=[bT])
                        if rl != 7:
                            return
                        t0 = rg * 512
                        yb_, ys_ = YB[rg % 2], YSb[rg % 2]
                        if dbg:
                            c.op("act", lambda e: e.activation(out=yb_[:], in_=bT[:], func=AF.Identity), reads=[bT], writes=[yb_])
                            c.dma(dbg_d[1, hp * 128:(hp + 1) * 128, t0:t0 + 512], yb_[:], reads=[yb_], writes=[dram_misc],
                                  sem_buf=yb_, store=True)
                        c.op("dve", lambda e: e.tensor_tensor(out=ys_[:], in0=bT[:], in1=SGF[:, t0:t0 + 512], op=ALU.mult),
                             reads=[bT, SGF], writes=[ys_])
                        c.dma(ysT_d[1, hp * 128:(hp + 1) * 128, t0:t0 + 512], ys_[:], reads=[ys_], writes=[dram_ys],
                              sem_buf=ys_, store=True)

                    pend = emit_qk(0) if nrows else None
                    prev = None
                    for r in range(nrows):
                        eb_, par, tt0 = pend
                        if r + 1 < nrows:
                            pend = emit_qk(r + 1)
                        bY = bank()
                        for h2 in range(2):
                            for kt in range(4):
                                c.op("pe", lambda e, h2=h2, kt=kt: e.matmul(
                                    bY[0:64, h2 * 65:(h2 + 1) * 65], lhsT=eb_[:, (kt * 2 + h2) * 64:(kt * 2 + h2 + 1) * 64],
                                    rhs=Vs[par][:, tt0 + kt, h2, :], start=(kt == 0), stop=(kt == 3)),
                                    reads=[Vs[par], eb_], writes=[bY])
                        byv = bY[0:64, 0:130].rearrange("p (h c) -> p h c", c=65)
                        rc_, yn_ = rec[r % 2], Yn[r % 2]
                        c.op("dve", lambda e: e.reciprocal(out=rc_[:], in_=byv[:, :, 64]), reads=[bY], writes=[rc_])
                        c.op("dve", lambda e: e.tensor_tensor(out=yn_[:], in0=byv[:, :, 0:64], in1=rc_[:].unsqueeze(2).to_broadcast([64, 2, 64]),
                                                              op=ALU.mult), reads=[bY, rc_], writes=[yn_])
                        if prev is not None:
                            emit_tr(prev)
                        prev = r
                    if prev is not None:
                        emit_tr(prev)
                    state["nb"] = 8
                c.barrier()
                c.release(wqs + wks + wvs + wbgs + [qkg, eps2, KN, Zf, SQb, RS, bmk, RD, SGF, identf] + [QZ] + Vs + ES + EB + YB + SGb + YSb + Yn + rec)
            if stop_after == 'B':
                break
            if stop_after == 'mix':
                break
            TB = 512
            with ExitStack() as ph:
                wmg = c.sbuf("wmg", [128, 8, 4 * D], BF16, ph)
                wbs = c.sbuf("wbs", [128, 4, 4, D], BF16, ph)
                ysb = c.sbuf("ysb", [128, 16, TB], BF16, ph)
                mgb = [c.sbuf("mgb%d" % i, [128, 8, TB], BF16, ph) for i in range(2)]
                acc = c.sbuf("acc", [128, TB], F32, ph)
                sgm = [c.sbuf("sgm%d" % i, [128, TB], F32, ph) for i in range(2)]
                tmpm = [c.sbuf("tmpm%d" % i, [128, TB], F32, ph) for i in range(2)]
                wmg_t = [Buf("wmg_t%d" % i, wmg.t) for i in range(8)]
                wbs_t = [Buf("wbs_t%d" % i, wbs.t) for i in range(8)]
                ysb_t = [Buf("ysb_t%d" % i, ysb.t) for i in range(4)]
                for dt in range(8):
                    for n in range(4):
                        cb = n * 8 + dt
                        load_w(wmg_t[dt], wmg[:, :, cb * 128:(cb + 1) * 128], win_cols(L, 10 * W + cb * 128, 128), 8, 128)
                        load_w(wbs_t[dt], wbs[:, n, :, dt * 128:(dt + 1) * 128],
                               wbr_d[L, n, :, dt * 128:(dt + 1) * 128].rearrange("(kt p) n -> p kt n", p=128), 4, 128)
                for blk in range(T // TB):
                    t0 = blk * TB
                    mg_ = mgb[blk % 2]
                    for n in range(4):
                        c.dma(ysb[:, n * 4:(n + 1) * 4, :], ysT_d[n, :, t0:t0 + TB].rearrange("(k p) t -> p k t", p=128),
                              reads=[dram_ys], writes=[ysb_t[n]], sem_buf=ysb_t[n])
                    for dt in range(8):
                        for n in range(4):
                            b1 = bank()
                            for kt in range(8):
                                c.op("pe", lambda e, kt=kt, b1=b1, n=n, dt=dt: e.matmul(
                                    b1[:, 0:TB], lhsT=wmg[:, kt, n * D + dt * 128:n * D + (dt + 1) * 128],
                                    rhs=hn[:, kt, 2 + t0:2 + t0 + TB], start=(kt == 0), stop=(kt == 7)),
                                    reads=[wmg_t[dt], hn], writes=[b1])
                            b2 = bank()
                            for kt in range(4):
                                c.op("pe", lambda e, kt=kt, b2=b2, n=n, dt=dt: e.matmul(
                                    b2[:, 0:TB], lhsT=wbs[:, n, kt, dt * 128:(dt + 1) * 128], rhs=ysb[:, n * 4 + kt, :],
                                    start=(kt == 0), stop=(kt == 3)), reads=[wbs_t[dt], ysb_t[n]], writes=[b2])
                            sg_ = sgm[n % 2]
                            c.op("act", lambda e, sg_=sg_, b1=b1: e.activation(out=sg_[:], in_=b1[:, 0:TB], func=AF.Sigmoid),
                                 reads=[b1], writes=[sg_])
                            if n == 0:
                                c.op("dve", lambda e, sg_=sg_, b2=b2: e.tensor_tensor(out=acc[:], in0=b2[:, 0:TB], in1=sg_[:], op=ALU.mult),
                                     reads=[b2, sg_], writes=[acc])
                            else:
                                tm_ = tmpm[n % 2]
                                c.op("dve", lambda e, sg_=sg_, b2=b2, tm_=tm_: e.tensor_tensor(out=tm_[:], in0=b2[:, 0:TB], in1=sg_[:], op=ALU.mult),
                                     reads=[b2, sg_], writes=[tm_])
                                if n < 3:
                                    c.op("pool", lambda e, tm_=tm_: e.tensor_tensor(out=acc[:], in0=acc[:], in1=tm_[:], op=ALU.add),
                                         reads=[acc, tm_], writes=[acc])
                                else:
                                    c.op("pool", lambda e, tm_=tm_, dt=dt, mg_=mg_: e.tensor_tensor(out=mg_[:, dt, :], in0=acc[:], in1=tm_[:], op=ALU.add),
                                         reads=[acc, tm_], writes=[mg_])
                    c.dma(mgT_d[:, t0:t0 + TB].rearrange("(k p) t -> p k t", p=128), mg_[:], reads=[mg_], writes=[dram_mg],
                          sem_buf=mg_, store=True)
                c.barrier()
                c.release([wmg, wbs, ysb, acc] + mgb + sgm + tmpm + wmg_t + wbs_t + ysb_t)
            with ExitStack() as ph:
                wo_sb = c.sbuf("wo_sb", [128, 8, D], BF16, ph)
                pg_sb = c.sbuf("pg_sb", [128, 8, D], BF16, ph)
                pp_sb = c.sbuf("pp_sb", [128, 2, D], BF16, ph)
                mgi = [c.sbuf("mgi%d" % i, [128, 8, TB], BF16, ph) for i in range(2)]
                xk8 = [c.sbuf("xk8%d" % i, [128, 8, TB], F32, ph) for i in range(2)]
                x1b = c.sbuf("x1b", [128, 8, TB], BF16, ph)
                ptf = c.sbuf("ptf", [128, 2, TB], F32, ph)
                ptb = c.sbuf("ptb", [128, 2, TB], BF16, ph)
                sg2 = [c.sbuf("sg2%d" % i, [128, TB], F32, ph) for i in range(2)]
                wo_t = [Buf("wo_t%d" % i, wo_sb.t) for i in range(8)]
                pg_t = [Buf("pg_t%d" % i, pg_sb.t) for i in range(8)]
                pp_t = [Buf("pp_t%d" % i, pp_sb.t) for i in range(8)]
                x1bs = [x1b, c.sbuf("x1b2", [128, 8, TB], BF16, ph)]
                ptbs = [ptb, c.sbuf("ptb2", [128, 2, TB], BF16, ph)]
                for db in range(8):
                    sl = slice(db * 128, (db + 1) * 128)
                    load_w(wo_t[db], wo_sb[:, :, sl], wout_d[L, :, sl].rearrange("(kt p) n -> p kt n", p=128), 8, 128)
                for db in range(8):
                    sl = slice(db * 128, (db + 1) * 128)
                    load_w(pg_t[db], pg_sb[:, :, sl], pgate_d[L, :, sl].rearrange("(kt p) n -> p kt n", p=128), 8, 128)
                    load_w(pp_t[db], pp_sb[:, :, sl], pproj_d[L, :, sl].rearrange("(kt p) n -> p kt n", p=128), 2, 128)
                nblk = T // TB

                def m2_stage1(blk):
                    t0 = blk * TB
                    mg_, x1, x1b_, ptb_ = mgi[blk % 2], xk8[blk % 2], x1bs[blk % 2], ptbs[blk % 2]
                    c.dma(mg_[:], mgT_d[:, t0:t0 + TB].rearrange("(k p) t -> p k t", p=128), reads=[dram_mg], writes=[mg_], sem_buf=mg_)
                    c.dma(ptf[:], pT_d[L, :, t0:t0 + TB].rearrange("(k p) t -> p k t", p=128), writes=[ptf], sem_buf=ptf)
                    c.dma(x1[:], xsrc_d[:, t0:t0 + TB].rearrange("(k p) t -> p k t", p=128), reads=([dram_x1] if L > 0 else []), writes=[x1], sem_buf=x1)
                    c.op("pool", lambda e: e.tensor_copy(out=ptb_[:], in_=ptf[:]), reads=[ptf], writes=[ptb_])
                    for dt in range(8):
                        b1 = bank()
                        for kt in range(8):
                            c.op("pe", lambda e, kt=kt, b1=b1, dt=dt: e.matmul(
                                b1[:, 0:TB], lhsT=wo_sb[:, kt, dt * 128:(dt + 1) * 128], rhs=mg_[:, kt, :],
                                start=(kt == 0), stop=(kt == 7)), reads=[wo_t[dt], mg_], writes=[b1])
                        c.op("dve", lambda e, dt=dt, b1=b1: e.tensor_tensor(out=x1[:, dt, :], in0=b1[:, 0:TB], in1=x1[:, dt, :], op=ALU.add),
                             reads=[b1, x1], writes=[x1])
                        c.op("pool", lambda e, dt=dt: e.tensor_copy(out=x1b_[:, dt, :], in_=x1[:, dt, :]), reads=[x1], writes=[x1b_])

                def m2_stage2(blk):
                    t0 = blk * TB
                    x1, x1b_, ptb_ = xk8[blk % 2], x1bs[blk % 2], ptbs[blk % 2]
                    for dt in range(8):
                        b1 = bank()
                        for kt in range(8):
                            c.op("pe", lambda e, kt=kt, b1=b1, dt=dt: e.matmul(
                                b1[:, 0:TB], lhsT=pg_sb[:, kt, dt * 128:(dt + 1) * 128], rhs=x1b_[:, kt, :],
                                start=(kt == 0), stop=(kt == 7)), reads=[pg_t[dt], x1b_], writes=[b1])
                        b2 = bank()
                        for kt in range(2):
                            c.op("pe", lambda e, kt=kt, b2=b2, dt=dt: e.matmul(
                                b2[:, 0:TB], lhsT=pp_sb[:, kt, dt * 128:(dt + 1) * 128], rhs=ptb_[:, kt, :],
                                start=(kt == 0), stop=(kt == 1)), reads=[pp_t[dt], ptb_], writes=[b2])
                        sg_ = sg2[dt % 2]
                        c.op("act", lambda e, sg_=sg_, b1=b1: e.activation(out=sg_[:], in_=b1[:, 0:TB], func=AF.Sigmoid),
                             reads=[b1], writes=[sg_])
                        c.op("dve", lambda e, sg_=sg_, b2=b2: e.tensor_tensor(out=sg_[:], in0=b2[:, 0:TB], in1=sg_[:], op=ALU.mult),
                             reads=[b2, sg_], writes=[sg_])
                        c.op("pool", lambda e, sg_=sg_, dt=dt: e.tensor_tensor(out=x1[:, dt, :], in0=sg_[:], in1=x1[:, dt, :], op=ALU.add),
                             reads=[sg_, x1], writes=[x1])
                    c.dma(xdst_d[:, t0:t0 + TB].rearrange("(k p) t -> p k t", p=128), x1[:], reads=[x1],
                          writes=[dram_out if L == NL - 1 else dram_x1], sem_buf=x1, store=True)

                m2_stage1(0)
                for blk in range(nblk):
                    if blk + 1 < nblk:
                        m2_stage1(blk + 1)
                    m2_stage2(blk)
                c.barrier()
                c.release([wo_sb, pg_sb, pp_sb, ptf] + x1bs + ptbs + mgi + xk8 + sg2 + wo_t + pg_t + pp_t)
            if stop_after == 'L0':
                break

        c.barrier()
    return nc


def _prep_shared(inp):
    f = np.float32
    sh = {}
    sh["gl"] = np.ascontiguousarray(inp["norm_scale"].reshape(NL, 8, 128).transpose(0, 2, 1)).astype(f)
    sh["w_in"] = np.ascontiguousarray(inp["w_in"]).astype(f)
    sh["cwbc"] = np.ascontiguousarray(np.broadcast_to(inp["lru_conv_w"][:, None], (NL, 128, 4, W))).astype(f)

    def pc(v):
        return v.reshape(NL, 4, 128).transpose(0, 2, 1)
    vecs = [inp["lru_conv_b"], inp["lru_b_r"][:, 0], inp["lru_b_r"][:, 1], inp["lru_b_i"][:, 0], inp["lru_b_i"][:, 1],
            inp["lru_lambda"][:, 0], inp["lru_lambda"][:, 1], inp["ssm_d"], inp["ssm_glu_b"], inp["pool_scale"]]
    vecs += [inp["lru_conv_w"][:, k] for k in range(4)]
    sh["chp"] = np.ascontiguousarray(np.stack([pc(v) for v in vecs], axis=2)).astype(f)
    bd = np.zeros((NL, 2, 2, 4, 128, 128), f)
    for gt, key in enumerate(("lru_w_r", "lru_w_i")):
        w = inp[key]
        for ct in range(4):
            for j in range(2):
                bd[:, :, gt, ct, j * 64:(j + 1) * 64, j * 64:(j + 1) * 64] = w[:, :, 2 * ct + j]
    sh["bd"] = bd
    qkg = np.zeros((NL, 128, 2), f)
    qkg[:, :, 0] = np.tile(inp["na_q_gain"], (1, 2))
    qkg[:, :, 1] = np.tile(inp["na_k_gain"], (1, 2))
    sh["qkg"] = qkg
    rpb = inp["na_rel_bias"]
    kk = np.arange(512)
    ki = kk // 64
    kc = kk % 64
    qc = np.arange(64)
    win = np.clip(qc - 8, 0, 48)
    valid = (kc[:, None] >= win[None, :]) & (kc[:, None] < win[None, :] + 16)
    dc = np.clip(kc[:, None] - qc[None, :], -15, 15) + 15
    bm = np.empty((NL, 4, 128, 8, 2, 256), f)
    for cfg in range(8):
        dr = ki - cfg + 7
        ok = valid & (dr[:, None] >= 0) & (dr[:, None] <= 14)
        drc = np.clip(dr, 0, 14)
        g = rpb[:, :, drc[:, None], dc]
        g = np.where(ok[None, None], g, f(-1e30)).astype(f)
        g = g.reshape(NL, 4, 2, 4, 128, 64)
        bm[:, :, :, cfg] = g.transpose(0, 1, 4, 3, 2, 5).reshape(NL, 4, 128, 2, 256)
    sh["bm"] = bm
    def sl(a):
        return a.reshape(NL, 2, 16, 2, 64).transpose(0, 1, 3, 4, 2).reshape(NL, 2, 128, 16)
    ldt = np.broadcast_to(inp["ssm_log_dt"][..., None], (NL, 2, 32, 64))
    sh["ssmp"] = np.ascontiguousarray(np.stack([sl(inp["ssm_a_re"]), sl(inp["ssm_a_im"]), sl(ldt)], axis=3)).astype(f)
    bpad = np.zeros((NL, 2, 16, 2, 128, 128), f)
    cpad = np.zeros((NL, 2, 16, 2, 128, 128), f)
    for ri, (bk, ck) in enumerate((("ssm_b_re", "ssm_c_re"), ("ssm_b_im", "ssm_c_im"))):
        B = inp[bk]
        C = inp[ck]
        for stg in range(16):
            q = stg % 4
            for gl in range(2):
                g_ = 2 * stg + gl
                r0 = 32 * q + gl * 16
                bpad[:, :, stg, ri, r0:r0 + 16, gl * 64:(gl + 1) * 64] = B[:, :, g_].transpose(0, 1, 3, 2)
                cpad[:, :, stg, ri, gl * 64:(gl + 1) * 64, r0:r0 + 16] = C[:, :, g_].transpose(0, 1, 3, 2)
    sh["bpad"] = bpad
    sh["cpad"] = cpad
    sh["bpadT"] = np.ascontiguousarray(bpad.transpose(0, 1, 2, 3, 5, 4))
    sh["ident"] = np.eye(128, dtype=f)
    sh["gluw"] = np.ascontiguousarray(inp["ssm_glu_w"]).astype(f)
    sh["poolw"] = np.ascontiguousarray(inp["pool_w"]).astype(f)
    pe = np.zeros((128, 4, 16), f)
    for gi, w in enumerate((2, 4, 8, 16)):
        t = np.concatenate([np.arange(8), np.arange(T - 8, T)])
        lo = np.clip(t - w // 2, 0, T)
        hi = np.clip(t - w // 2 + w, 0, T)
        pe[:, gi, :] = (1.0 / (hi - lo)).astype(f)[None]
    sh["pedge"] = pe
    sh["wbr"] = np.ascontiguousarray(inp["w_branch"]).astype(f)
    sh["wout"] = np.ascontiguousarray(inp["w_out"]).astype(f)
    sh["pproj"] = np.ascontiguousarray(inp["ple_proj"]).astype(f)
    sh["pgate"] = np.ascontiguousarray(inp["ple_gate"]).astype(f)
    return sh


def _in_maps(inp, cores):
    sh = _prep_shared(inp)
    maps = []
    for b in cores:
        m = dict(sh)
        m["xT"] = np.ascontiguousarray(inp["x"][b].T).astype(np.float32)
        m["pT"] = np.ascontiguousarray(inp["p"][:, b].transpose(0, 2, 1)).astype(np.float32)
        maps.append(m)
    return maps


def kernel(**inputs):
    inp = {k: np.asarray(v) for k, v in inputs.items()}
    nc = build()
    res = run_bass_kernel_spmd(nc, _in_maps(inp, range(8)), core_ids=list(range(8)))
    out = np.stack([np.ascontiguousarray(r["outT"].T) for r in res.results], axis=0)
    return out.astype(np.float32)
```

```python
import numpy as np
from contextlib import ExitStack
import concourse.bass as bass
import concourse.mybir as mybir
from concourse.bass_utils import run_bass_kernel_spmd

F32 = mybir.dt.float32
BF16 = mybir.dt.bfloat16
AF = mybir.ActivationFunctionType
ALU = mybir.AluOpType

T = 4096
D = 1024
W = 512
NIN = 9216
NL = 2
V_CONVB, V_BR0, V_BR1, V_BI0, V_BI1, V_LAM0, V_LAM1, V_SSMD, V_GLUB, V_PSCALE, V_CW0 = range(11)
NV = 14


class Sem:
    def __init__(self, handle, is_dma):
        self.h = handle
        self.is_dma = is_dma
        self.total = 0


class Buf:
    def __init__(self, name, t=None):
        self.name = name
        self.t = t
        self.last_w = {}
        self.readers = {}
        self.dsems = {}
        self.is_psum = False

    def __getitem__(self, k):
        return self.t[k]


class Eng:
    def __init__(self, name, e, sem):
        self.name = name
        self.e = e
        self.sem = sem
        self.waited = {}


class Ctx:
    def __init__(self, nc, stack):
        self.nc = nc
        self.stack = stack
        self.nsem = 0
        self.sems = []
        self.engs = {}
        for name, e in (("pe", nc.tensor), ("act", nc.scalar), ("dve", nc.vector),
                        ("pool", nc.gpsimd), ("sp", nc.sync)):
            self.engs[name] = Eng(name, e, self.new_sem(name, False))
        self.ninstr = 0
        self.free_dma_sems = {"hw": [], "sw": []}

    def new_sem(self, name, is_dma):
        self.nsem += 1
        h = self.stack.enter_context(self.nc.semaphore("s%d_%s" % (self.nsem, name)))
        s = Sem(h, is_dma)
        self.sems.append(s)
        return s

    def dma_sem(self, kind):
        if self.free_dma_sems[kind]:
            return self.free_dma_sems[kind].pop()
        return self.new_sem("dma" + kind, True)

    def sbuf(self, name, shape, dtype, stack=None):
        self.ninstr += 1
        name = "sb%d_%s" % (self.ninstr, name)
        t = (stack or self.stack).enter_context(self.nc.sbuf_tensor(name, list(shape), dtype))
        return Buf(name, t)

    def psum(self, name, shape, dtype):
        t = self.stack.enter_context(self.nc.psum_tensor(name, list(shape), dtype))
        b = Buf(name, t)
        b.is_psum = True
        return b

    def _wait(self, eng, deps):
        for sem, val in deps.items():
            if sem.is_dma:
                val = sem.total
            if sem is eng.sem and eng.name == "pe":
                continue
            if eng.waited.get(sem, 0) >= val:
                continue
            eng.e.wait_ge(sem.h, val)
            eng.waited[sem] = val

    @staticmethod
    def _merge(d, sem, val):
        if d.get(sem, 0) < val:
            d[sem] = val

    def _deps(self, reads, writes):
        deps = {}
        for b in reads:
            for s, v in b.last_w.items():
                self._merge(deps, s, v)
            if b.is_psum:
                for s, v in b.readers.items():
                    self._merge(deps, s, v)
        for b in writes:
            for s, v in b.last_w.items():
                self._merge(deps, s, v)
            for s, v in b.readers.items():
                self._merge(deps, s, v)
        return deps

    def op(self, engname, fn, reads=(), writes=()):
        eng = self.engs[engname]
        self._wait(eng, self._deps(reads, writes))
        ins = fn(eng.e)
        eng.sem.total += 1
        ins.then_inc(eng.sem.h, 1)
        for b in writes:
            b.last_w = {eng.sem: eng.sem.total}
            b.readers = {}
        for b in reads:
            if b not in writes:
                self._merge(b.readers, eng.sem, eng.sem.total)
        self.ninstr += 1
        return ins

    def dma(self, out, in_, reads=(), writes=(), sem_buf=None, store=False, q="sp"):
        eng = self.engs[q]
        kind = "sw" if q == "pool" else "hw"
        key = ("st" if store else "ld", kind)
        if key not in sem_buf.dsems:
            sem_buf.dsems[key] = self.dma_sem(kind)
        sem = sem_buf.dsems[key]
        self._wait(eng, self._deps(reads, writes))
        ins = eng.e.dma_start(out=out, in_=in_)
        sem.total += 16
        ins.then_inc(sem.h, 16)
        for b in writes:
            b.last_w = {sem: sem.total}
            b.readers = {}
        for b in reads:
            if b not in writes:
                self._merge(b.readers, sem, sem.total)
        self.ninstr += 1
        return ins

    def barrier(self):
        allv = {s: s.total for s in self.sems if s.total > 0}
        for eng in self.engs.values():
            self._wait(eng, dict(allv))

    def release(self, bufs):
        for b in bufs:
            for (_, kind), sm in b.dsems.items():
                self.free_dma_sems[kind].append(sm)
            b.dsems = {}


def build(dbg=False, stop_after=None, skip=(), bstage=None):
    nc = bass.Bass("TRN2", target_bir_lowering=False)

    def din(name, shape, dt=F32):
        return nc.dram_tensor(name, list(shape), dt, kind="ExternalInput").ap()

    def dscr(name, shape, dt=F32):
        return nc.dram_tensor(name, list(shape), dt).ap()

    xT_d = din("xT", [D, T])
    pT_d = din("pT", [NL, 256, T])
    g_d = din("gl", [NL, 128, 8])
    win_d = din("w_in", [NL, D, NIN])
    cwbc_d = din("cwbc", [NL, 128, 4, W])
    chp_d = din("chp", [NL, 128, NV, 4])
    bd_d = din("bd", [NL, 2, 2, 4, 128, 128])
    qkg_d = din("qkg", [NL, 128, 2])
    bm_d = din("bm", [NL, 4, 128, 8, 2, 256])
    sp_d = din("ssmp", [NL, 2, 128, 3, 16])
    bpad_d = din("bpad", [NL, 2, 16, 2, 128, 128])
    cpad_d = din("cpad", [NL, 2, 16, 2, 128, 128])
    bpadT_d = din("bpadT", [NL, 2, 16, 2, 128, 128])
    ident_d = din("ident", [128, 128])
    gluw_d = din("gluw", [NL, W, W])
    poolw_d = din("poolw", [NL, 4, 128, 128])
    pedge_d = din("pedge", [128, 4, 16])
    wbr_d = din("wbr", [NL, 4, W, D])
    wout_d = din("wout", [NL, D, D])
    pproj_d = din("pproj", [NL, 256, D])
    pgate_d = din("pgate", [NL, D, D])
    outT_d = nc.dram_tensor("outT", [D, T], F32, kind="ExternalOutput").ap()
    x1T_d = nc.dram_tensor("x1T", [D, T], F32, kind="ExternalOutput").ap() if dbg else dscr("x1T", [D, T])
    ysT_d = dscr("ysT", [4, W, T], BF16)
    hfT_d = dscr("hfT", [W, T])
    ygT_d = dscr("ygT", [W, T])
    ygb_d = dscr("ygb", [W, T], BF16)
    mgT_d = dscr("mgT", [D, T], BF16)
    if dbg:
        dbg_d = nc.dram_tensor("dbg", [4, W, T], F32, kind="ExternalOutput").ap()

    with ExitStack() as st:
        c = Ctx(nc, st)
        pb = [c.psum("pb%d" % i, [128, 512], F32) for i in range(8)]
        state = {"bank": 0, "cast": 0, "nb": 8}

        def bank():
            b = pb[state["bank"] % state["nb"]]
            state["bank"] += 1
            return b

        hn = c.sbuf("hn", [128, 8, T + 4], BF16)
        gsb = c.sbuf("gsb", [128, 8], F32)
        chp = c.sbuf("chp", [128, NV, 4], F32)
        onesf = c.sbuf("onesf", [128, 128], F32)
        onesb = c.sbuf("onesb", [128, 128], BF16)
        blk1 = c.sbuf("blk1", [128, 128], BF16)
        dram_x1 = Buf("x1T_dram")
        dram_ys = Buf("ys_dram")
        dram_out = Buf("out_dram")
        dram_misc = Buf("misc_dram")
        dram_mg = Buf("mg_dram")

        c.op("pool", lambda e: e.memset(onesf[:], 1.0), writes=[onesf])
        c.op("pool", lambda e: e.memset(onesb[:], 1.0), writes=[onesb])
        c.op("pool", lambda e: e.memset(blk1[:], 0.0), writes=[blk1])
        c.op("pool", lambda e: e.memset(blk1[0:64, 0:64], 1.0), writes=[blk1])
        c.op("pool", lambda e: e.memset(blk1[64:128, 64:128], 1.0), writes=[blk1])
        c.op("pool", lambda e: e.memset(hn[:, :, 0:2], 0.0), writes=[hn])
        c.op("pool", lambda e: e.memset(hn[:, :, T + 2:T + 4], 0.0), writes=[hn])

        def cast_eng():
            state["cast"] += 1
            return ("dve", "pool")[state["cast"] % 2]

        def load_w(dst, dst_ap, src_ap, nkt, ncols, fold_g=False, eng=None):
            c.dma(dst_ap, src_ap, writes=[dst], sem_buf=dst, q="pool")

        def win_cols(L, c0, n):
            return win_d[L, :, c0:c0 + n].rearrange("(kt p) n -> p kt n", p=128)

        def mm_chunks(wlist, t0, ntok, evac):
            for a in range(t0, t0 + ntok, 512):
                n = min(512, t0 + ntok - a)
                bk = bank()
                tot = sum(w_[2] for w_ in wlist)
                i = 0
                for (wbuf, lf, nkt, rf, rbuf) in wlist:
                    for kt in range(nkt):
                        c.op("pe", lambda e, lf=lf, rf=rf, kt=kt, i=i: e.matmul(
                            bk[:, 0:n], lhsT=lf(kt), rhs=rf(kt, a, n), start=(i == 0), stop=(i == tot - 1)),
                            reads=[wbuf, rbuf], writes=[bk])
                        i += 1
                evac(bk, a, n)

        def hn_rhs(shift=0):
            return lambda kt, a, n: hn[:, kt, 2 + a + shift:2 + a + shift + n]

        for L in range(NL):
            xsrc_d = xT_d if L == 0 else x1T_d
            xdst_d = x1T_d if L == 0 else outT_d
            c.dma(gsb[:], g_d[L], writes=[gsb], sem_buf=gsb)
            c.dma(chp[:], chp_d[L], writes=[chp], sem_buf=chp)
            with ExitStack() as ph:
                xk = [c.sbuf("xk%d" % i, [128, T], F32, ph) for i in range(2)]
                sq = [c.sbuf("sq%d" % i, [128, T], F32, ph) for i in range(2)]
                rstd = c.sbuf("rstd", [128, T], F32, ph)
                for kt in range(8):
                    xb = xk[kt % 2]
                    sb_ = sq[kt % 2]
                    c.dma(xb[:], xsrc_d[kt * 128:(kt + 1) * 128, :], reads=([dram_x1] if L > 0 else []), writes=[xb], sem_buf=xb,
                          q=("sp", "pool")[kt % 2])
                    c.op("act", lambda e, xb=xb, sb_=sb_: e.activation(out=sb_[:], in_=xb[:], func=AF.Square),
                         reads=[xb], writes=[sb_])
                    for ch in range(8):
                        c.op("pe", lambda e, ch=ch, sb_=sb_, kt=kt: e.matmul(
                            pb[ch][:], lhsT=onesf[:], rhs=sb_[:, ch * 512:(ch + 1) * 512], start=(kt == 0), stop=(kt == 7)),
                            reads=[onesf, sb_], writes=[pb[ch]])
                epsb = c.sbuf("epsb", [128, 1], F32, ph)
                c.op("pool", lambda e: e.memset(epsb[:], 1e-6), writes=[epsb])
                for ch in range(8):
                    sl = slice(ch * 512, (ch + 1) * 512)
                    c.op("act", lambda e, ch=ch, sl=sl: e.activation(out=rstd[:, sl], in_=pb[ch][:], func=AF.Ln,
                                                                     scale=1.0 / D, bias=epsb[:, 0:1]),
                         reads=[pb[ch], epsb], writes=[rstd])
                c.op("act", lambda e: e.activation(out=rstd[:], in_=rstd[:], func=AF.Exp, scale=-0.5), reads=[rstd], writes=[rstd])
                for kt in range(8):
                    xb = xk[kt % 2]
                    c.dma(xb[:], xsrc_d[kt * 128:(kt + 1) * 128, :], reads=([dram_x1] if L > 0 else []), writes=[xb], sem_buf=xb,
                          q=("sp", "pool")[kt % 2])
                    c.op("dve", lambda e, xb=xb, kt=kt: e.scalar_tensor_tensor(
                        out=hn[:, kt, 2:2 + T], in0=xb[:], scalar=gsb[:, kt:kt + 1], in1=rstd[:], op0=ALU.mult, op1=ALU.mult),
                        reads=[xb, rstd, gsb], writes=[hn])
                c.barrier()
                c.release(xk + sq + [rstd, epsb])

            TS = 1024
            with ExitStack() as ph:
                wzs = [c.sbuf("wz%d" % i, [128, 8, 128], BF16, ph) for i in range(2)]
                wgs = [c.sbuf("wg%d" % i, [128, 8, 128], BF16, ph) for i in range(2)]
                bdws = [c.sbuf("bdw%d" % i, [128, 4, 128], BF16, ph) for i in range(2)]

                def loadA(ct_):
                    load_w(wzs[ct_ % 2], wzs[ct_ % 2][:], win_cols(L, 0 * W + ct_ * 128, 128), 8, 128)
                    load_w(wgs[ct_ % 2], wgs[ct_ % 2][:], win_cols(L, 1 * W + ct_ * 128, 128), 8, 128)
                    for d__ in range(2):
                        for gt_ in range(2):
                            c.dma(bdws[ct_ % 2][:, d__ * 2 + gt_, :], bd_d[L, d__, gt_, ct_], writes=[bdws[ct_ % 2]],
                                  sem_buf=bdws[ct_ % 2], q="pool")
                if 'A' not in skip:
                    loadA(0)
                cp = c.sbuf("cp", [128, 4], F32, ph)
                XCF = c.sbuf("XCF", [128, T], F32, ph)
                XCBF = c.sbuf("XCBF", [128, T], BF16, ph)
                R = c.sbuf("R", [128, T], F32, ph)
                GI = c.sbuf("GI", [128, T], F32, ph)
                A2 = c.sbuf("A2", [128, T + 4], F32, ph)
                HF = c.sbuf("HF", [128, T], F32, ph)
                H = c.sbuf("H", [128, TS], F32, ph)
                SG = c.sbuf("SG", [128, TS], F32, ph)
                YS = c.sbuf("YS", [128, TS], BF16, ph)
                carry = c.sbuf("carry", [128, 1], F32, ph)
                for ct in (range(4) if 'A' not in skip else []):
                    wz, wg, bdw = wzs[ct % 2], wgs[ct % 2], bdws[ct % 2]
                    if ct + 1 < 4:
                        loadA(ct + 1)
                    for d_ in range(2):
                        c.op("act", lambda e, d_=d_: e.activation(out=cp[:, 2 * d_:2 * d_ + 1], in_=chp[:, V_LAM0 + d_, ct:ct + 1],
                                                                  func=AF.Exp, scale=-1.0), reads=[chp], writes=[cp])
                        c.op("act", lambda e, d_=d_: e.activation(out=cp[:, 2 * d_:2 * d_ + 1], in_=cp[:, 2 * d_:2 * d_ + 1],
                                                                  func=AF.Ln, bias=1.0, scale=1.0), reads=[cp], writes=[cp])
                        c.op("dve", lambda e, d_=d_: e.tensor_scalar(out=cp[:, 2 * d_ + 1:2 * d_ + 2], in0=cp[:, 2 * d_:2 * d_ + 1],
                                                                     scalar1=-16.0, scalar2=None, op0=ALU.mult), reads=[cp], writes=[cp])
                        c.op("dve", lambda e, d_=d_: e.tensor_scalar(out=cp[:, 2 * d_:2 * d_ + 1], in0=cp[:, 2 * d_:2 * d_ + 1],
                                                                     scalar1=-8.0, scalar2=None, op0=ALU.mult), reads=[cp], writes=[cp])
                    Z = A2
                    c.op("pool", lambda e: e.memset(Z[:, 0:2], 0.0), writes=[Z])
                    c.op("pool", lambda e: e.memset(Z[:, T + 2:T + 4], 0.0), writes=[Z])

                    def ev_z0(bk, a, n):
                        c.op("act", lambda e: e.activation(out=Z[:, 2 + a:2 + a + n], in_=bk[:, 0:n], func=AF.Identity), reads=[bk], writes=[Z])
                    mm_chunks([(wz, (lambda kt: wz[:, kt, :]), 8, hn_rhs(0), hn)], 0, T, ev_z0)
                    cwv = lambda k: chp[:, V_CW0 + k, ct:ct + 1]
                    c.op("dve", lambda e: e.tensor_scalar(out=XCF[:], in0=Z[:, 0:T], scalar1=cwv(0), scalar2=chp[:, V_CONVB, ct:ct + 1],
                                                          op0=ALU.mult, op1=ALU.add), reads=[Z, chp], writes=[XCF])
                    for k in range(1, 4):
                        c.op("dve", lambda e, k=k: e.scalar_tensor_tensor(
                            out=XCF[:], in0=Z[:, k:k + T], scalar=cwv(k), in1=XCF[:], op0=ALU.mult, op1=ALU.add),
                            reads=[Z, chp, XCF], writes=[XCF])
                    c.op("pool", lambda e: e.tensor_copy(out=XCBF[:], in_=XCF[:]), reads=[XCF], writes=[XCBF])

                    for d_ in range(2):
                        def ev_gate(dst, vidx):
                            def f(bk, a, n):
                                c.op("act", lambda e: e.activation(out=dst[:, a:a + n], in_=bk[:, 0:n], func=AF.Sigmoid,
                                                                   bias=chp[:, vidx, ct:ct + 1], scale=1.0),
                                     reads=[bk, chp], writes=[dst])
                            return f
                        for gt, dst, vidx in ((0, R, V_BR0 + d_), (1, GI, V_BI0 + d_)):
                            mm_chunks([(bdw, (lambda kt, gt=gt: bdw[:, d_ * 2 + gt, :]), 1,
                                        (lambda kt, a, n: XCBF[:, a:a + n]), XCBF)], 0, T, ev_gate(dst, vidx))
                        c.op("act", lambda e: e.activation(out=A2[:, 0:T], in_=R[:], func=AF.Exp, scale=cp[:, 2 * d_ + 1:2 * d_ + 2]),
                             reads=[R, cp], writes=[A2])
                        c.op("act", lambda e: e.activation(out=R[:], in_=R[:], func=AF.Exp, scale=cp[:, 2 * d_:2 * d_ + 1]),
                             reads=[R, cp], writes=[R])
                        c.op("act", lambda e: e.activation(out=A2[:, 0:T], in_=A2[:, 0:T], func=AF.Sqrt, scale=-1.0, bias=1.0),
                             reads=[A2], writes=[A2])
                        c.op("dve", lambda e: e.tensor_tensor(out=GI[:], in0=GI[:], in1=XCF[:], op=ALU.mult), reads=[GI, XCF], writes=[GI])
                        c.op("pool", lambda e: e.tensor_tensor(out=GI[:], in0=GI[:], in1=A2[:, 0:T], op=ALU.mult), reads=[GI, A2], writes=[GI])
                        c.op("pool", lambda e: e.memset(carry[:], 0.0), writes=[carry])
                        if d_ == 0:
                            for si in range(T // TS):
                                sl = slice(si * TS, (si + 1) * TS)
                                c.op("dve", lambda e, sl=sl: e.tensor_tensor_scan(out=HF[:, sl], data0=R[:, sl], data1=GI[:, sl],
                                                                                  initial=carry[:, 0:1], op0=ALU.mult, op1=ALU.add),
                                     reads=[R, GI, carry], writes=[HF])
                                c.op("dve", lambda e, sl=sl: e.tensor_copy(out=carry[:], in_=HF[:, sl.stop - 1:sl.stop]), reads=[HF], writes=[carry])
                        else:
                            for si in reversed(range(T // TS)):
                                t0 = si * TS
                                sl = slice(t0, t0 + TS)
                                c.op("dve", lambda e, sl=sl: e.tensor_tensor_scan(out=H[:, ::-1], data0=R[:, sl][:, ::-1], data1=GI[:, sl][:, ::-1],
                                                                                  initial=carry[:, 0:1], op0=ALU.mult, op1=ALU.add),
                                     reads=[R, GI, carry], writes=[H])
                                c.op("dve", lambda e: e.tensor_copy(out=carry[:], in_=H[:, 0:1]), reads=[H], writes=[carry])

                                def ev_sg(bk, a, n, t0=t0):
                                    o = a - t0
                                    c.op("act", lambda e: e.activation(out=SG[:, o:o + n], in_=bk[:, 0:n], func=AF.Silu), reads=[bk], writes=[SG])
                                mm_chunks([(wg, (lambda kt: wg[:, kt, :]), 8, hn_rhs(0), hn)], t0, TS, ev_sg)
                                c.op("pool", lambda e, sl=sl: e.tensor_tensor(out=H[:], in0=HF[:, sl], in1=H[:], op=ALU.add), reads=[HF, H], writes=[H])
                                if dbg:
                                    c.dma(dbg_d[0, ct * 128:(ct + 1) * 128, t0:t0 + TS], H[:], reads=[H], writes=[dram_misc], sem_buf=H, store=True)
                                c.op("dve", lambda e: e.tensor_tensor(out=YS[:], in0=H[:], in1=SG[:], op=ALU.mult), reads=[H, SG], writes=[YS])
                                c.dma(ysT_d[0, ct * 128:(ct + 1) * 128, t0:t0 + TS], YS[:], reads=[YS], writes=[dram_ys], sem_buf=YS, store=True)
                c.barrier()
                c.release(wzs + wgs + bdws + [cp, XCF, XCBF, R, GI, A2, HF, H, SG, YS, carry])

            if stop_after == 'A':
                break
            with ExitStack() as ph:
                wds = [c.sbuf("wd%d" % i, [128, 8, 128], BF16, ph) for i in range(2)]
                wgds = [c.sbuf("wgd%d" % i, [128, 8, 128], BF16, ph) for i in range(2)]
                wps = [c.sbuf("wp%d" % i, [128, 128], BF16, ph) for i in range(2)]

                def loadD(g_):
                    load_w(wds[g_ % 2], wds[g_ % 2][:], win_cols(L, 8 * W + g_ * 128, 128), 8, 128)
                    load_w(wgds[g_ % 2], wgds[g_ % 2][:], win_cols(L, 9 * W + g_ * 128, 128), 8, 128)
                    c.dma(wps[g_ % 2][:], poolw_d[L, g_], writes=[wps[g_ % 2]], sem_buf=wps[g_ % 2], q="pool")
                if 'D' not in skip:
                    loadD(0)
                pedge = c.sbuf("pedge", [128, 4, 16], F32, ph)
                U = c.sbuf("U", [128, T + 16], F32, ph)
                S1 = c.sbuf("S1", [128, T + 16], F32, ph)
                S2 = c.sbuf("S2", [128, T + 16], F32, ph)
                PB = c.sbuf("PB", [128, T], BF16, ph)
                YD = [c.sbuf("YD%d" % i, [128, 1024], F32, ph) for i in range(2)]
                SGd = [c.sbuf("SGd%d" % i, [128, 1024], F32, ph) for i in range(2)]
                YSd = [c.sbuf("YSd%d" % i, [128, 1024], BF16, ph) for i in range(2)]
                for gi in (range(4) if 'D' not in skip else []):
                    win_ = (2, 4, 8, 16)[gi]
                    wd, wgd, wp = wds[gi % 2], wgds[gi % 2], wps[gi % 2]
                    if gi + 1 < 4:
                        loadD(gi + 1)
                    c.dma(pedge[:], pedge_d, writes=[pedge], sem_buf=pedge)
                    c.op("pool", lambda e: e.memset(U[:, 0:8], 0.0), writes=[U])
                    c.op("pool", lambda e: e.memset(U[:, T + 8:T + 16], 0.0), writes=[U])

                    def ev_u(bk, a, n):
                        c.op("act", lambda e: e.activation(out=U[:, 8 + a:8 + a + n], in_=bk[:, 0:n], func=AF.Identity),
                             reads=[bk], writes=[U])
                    mm_chunks([(wd, (lambda kt: wd[:, kt, :]), 8, hn_rhs(0), hn)], 0, T, ev_u)
                    c.op("dve", lambda e: e.tensor_tensor(out=S1[:, 1:T + 16], in0=U[:, 0:T + 15], in1=U[:, 1:T + 16], op=ALU.add),
                         reads=[U], writes=[S1])
                    FB, OB = S1, S2
                    if win_ >= 4:
                        c.op("dve", lambda e: e.tensor_tensor(out=S2[:, 2:T + 15], in0=S1[:, 1:T + 14], in1=S1[:, 3:T + 16], op=ALU.add),
                             reads=[S1], writes=[S2])
                        FB, OB = S2, S1
                    if win_ >= 8:
                        c.op("dve", lambda e: e.tensor_tensor(out=S1[:, 4:T + 13], in0=S2[:, 2:T + 11], in1=S2[:, 6:T + 15], op=ALU.add),
                             reads=[S2], writes=[S1])
                        FB, OB = S1, S2
                    if win_ >= 16:
                        c.op("dve", lambda e: e.tensor_tensor(out=S2[:, 8:T + 8], in0=S1[:, 4:T + 4], in1=S1[:, 12:T + 12], op=ALU.add),
                             reads=[S1], writes=[S2])
                        FB, OB = S2, S1
                    c.op("dve", lambda e: e.scalar_tensor_tensor(out=OB[:, 8:8 + T], in0=FB[:, 8:8 + T], scalar=1.0 / win_,
                                                                 in1=U[:, 8:8 + T], op0=ALU.mult, op1=ALU.subtract),
                         reads=[FB, U], writes=[OB])
                    for (i0, e0) in ((8, 0), (T, 8)):
                        c.op("dve", lambda e, i0=i0, e0=e0: e.tensor_tensor(out=OB[:, i0:i0 + 8], in0=FB[:, i0:i0 + 8],
                                                                            in1=pedge[:, gi, e0:e0 + 8], op=ALU.mult),
                             reads=[FB, pedge], writes=[OB])
                        c.op("dve", lambda e, i0=i0: e.tensor_tensor(out=OB[:, i0:i0 + 8], in0=OB[:, i0:i0 + 8],
                                                                     in1=U[:, i0:i0 + 8], op=ALU.subtract),
                             reads=[OB, U], writes=[OB])
                    c.op("act", lambda e: e.activation(out=PB[:], in_=OB[:, 8:8 + T], func=AF.Identity), reads=[OB], writes=[PB])
                    for si in range(4):
                        t0 = si * 1024
                        yd_, sg_, ys_ = YD[si % 2], SGd[si % 2], YSd[si % 2]

                        def ev_y(bk, a, n, yd_=yd_, t0=t0):
                            c.op("act", lambda e: e.activation(out=yd_[:, a - t0:a - t0 + n], in_=bk[:, 0:n], func=AF.Identity,
                                                               scale=chp[:, V_PSCALE, gi:gi + 1]), reads=[bk, chp], writes=[yd_])

                        def ev_g(bk, a, n, sg_=sg_, t0=t0):
                            c.op("act", lambda e: e.activation(out=sg_[:, a - t0:a - t0 + n], in_=bk[:, 0:n], func=AF.Silu),
                                 reads=[bk], writes=[sg_])
                        mm_chunks([(wp, (lambda kt: wp[:]), 1, (lambda kt, a, n: PB[:, a:a + n]), PB)], t0, 1024, ev_y)
                        mm_chunks([(wgd, (lambda kt: wgd[:, kt, :]), 8, hn_rhs(0), hn)], t0, 1024, ev_g)
                        if dbg:
                            c.dma(dbg_d[3, gi * 128:(gi + 1) * 128, t0:t0 + 1024], yd_[:], reads=[yd_], writes=[dram_misc],
                                  sem_buf=yd_, store=True)
                        c.op("dve", lambda e, yd_=yd_, sg_=sg_, ys_=ys_: e.tensor_tensor(out=ys_[:], in0=yd_[:], in1=sg_[:], op=ALU.mult),
                             reads=[yd_, sg_], writes=[ys_])
                        c.dma(ysT_d[3, gi * 128:(gi + 1) * 128, t0:t0 + 1024], ys_[:], reads=[ys_], writes=[dram_ys],
                              sem_buf=ys_, store=True)
                c.barrier()
                c.release(wds + wgds + wps + [pedge, U, S1, S2, PB] + YD + SGd + YSd)
            if stop_after == 'D':
                break
            if 'Z' in skip:
                with ExitStack() as ph:
                    zb = c.sbuf("zb", [128, T], BF16, ph)
                    c.op("pool", lambda e: e.memset(zb[:], 0.0), writes=[zb])
                    for n_ in (1, 2):
                        for ct in range(4):
                            c.dma(ysT_d[n_, ct * 128:(ct + 1) * 128, :], zb[:], reads=[zb], writes=[dram_ys], sem_buf=zb, store=True)
                    c.barrier()
                    c.release([zb])
            with ExitStack() as ph:
                wc_ = c.sbuf("wc_", [128, 8, 128], BF16, ph)
                spm = c.sbuf("spm", [128, 2, 3, 16], F32, ph)
                P_LR, P_DT, P_X1, P_TH, P_RHO, P_C, P_S, P_T1, P_T2, P_T3, P_AR, P_AI, P_NR, P_DEN, P_FR, P_FI, P_NFI, P_RHO8 = range(18)
                prm = c.sbuf("prm", [128, 2, 18, 16], F32, ph)
                ck = c.sbuf("ck", [128, 2, 11, 16], F32, ph)
                sk = c.sbuf("sk", [128, 2, 11, 16], F32, ph)
                lp = c.sbuf("lp", [128, 2, 9, 2, 16], F32, ph)
                hpi = c.sbuf("hpi", [128, 1], F32, ph)
                ident = c.sbuf("ident", [128, 128], F32, ph)
                ctab = c.sbuf("ctab", [128, 4, 128], F32, ph)
                stab = c.sbuf("stab", [128, 4, 128], F32, ph)
                ttmp = c.sbuf("ttmp", [128, 128], F32, ph)
                st4 = c.sbuf("st4", [128, 4, 2, 128], F32, ph)
                dg = [c.sbuf("dg%d" % i, [128, 3, 4, 128], BF16, ph) for i in range(2)]
                nlpi = c.sbuf("nlpi", [128, 9, 4], F32, ph)
                identb = c.sbuf("identb", [128, 128], BF16, ph)
                BT = c.sbuf("BT", [128, 4, 2, 128], BF16, ph)
                G = c.sbuf("G", [128, 9, 4, 2, 128], BF16, ph)
                WI = c.sbuf("WI", [128, 8, 2, 4, 128], BF16, ph)
                KT = c.sbuf("KT", [128, 8, 128], BF16, ph)
                YACC = c.sbuf("YACC", [128, T], F32, ph)
                XD = [c.sbuf("XD%d" % i, [128, 8, 512], BF16, ph) for i in range(2)]
                SBr = c.sbuf("SBr", [128, 4, 516], BF16, ph)
                SBi = c.sbuf("SBi", [128, 4, 516], BF16, ph)
                carr = c.sbuf("carr", [128, 2, 4], F32, ph)
                ctmp = c.sbuf("ctmp", [128, 4, 4], F32, ph)
                TM = [c.sbuf("TM%d" % i, [128, 4, 128], F32, ph) for i in range(8)]
                YGB = c.sbuf("YGB", [128, T], BF16, ph)
                for ct in (range(4) if 'C' not in skip else []):
                    load_w(wc_, wc_[:], win_cols(L, 6 * W + ct * 128, 128), 8, 128, fold_g=True)
                    c.op("pool", lambda e: e.memset(hpi[:], float(np.pi / 2)), writes=[hpi])
                    c.dma(ident[:], ident_d, writes=[ident], sem_buf=ident)
                    c.op("pool", lambda e: e.tensor_copy(out=identb[:], in_=ident[:]), reads=[ident], writes=[identb])
                    cs = slice(ct * 4, ct * 4 + 4)
                    if ct == 0:
                        for d_ in range(2):
                            c.dma(spm[:, d_], sp_d[L, d_], writes=[spm], sem_buf=spm)

                    def P(d_, k):
                        return prm[:, d_, k, :]

                    def tt(o, a, b, op, eng="dve", rd=(prm,), wr=(prm,)):
                        c.op(eng, lambda e: e.tensor_tensor(out=o, in0=a, in1=b, op=op), reads=list(rd), writes=list(wr))

                    def tsc(o, a, s1, op0, s2=None, op1=None, rd=(prm,), wr=(prm,)):
                        if op1 is None:
                            c.op("dve", lambda e: e.tensor_scalar(out=o, in0=a, scalar1=s1, scalar2=None, op0=op0),
                                 reads=list(rd), writes=list(wr))
                        else:
                            c.op("dve", lambda e: e.tensor_scalar(out=o, in0=a, scalar1=s1, scalar2=s2, op0=op0, op1=op1),
                                 reads=list(rd), writes=list(wr))

                    def ev_uc(bk, a, n):
                        cc = a // 512
                        c.op("act", lambda e: e.activation(out=YACC[:, a:a + n], in_=bk[:, 0:n], func=AF.Identity,
                                                           scale=chp[:, V_SSMD, ct:ct + 1]), reads=[bk, chp], writes=[YACC])
                        src = bk[:, 0:512].rearrange("p (m i) -> p i m", i=8)
                        c.op("act", lambda e: e.activation(out=XD[0][:, :, 64 * cc:64 * cc + 64], in_=src, func=AF.Identity),
                             reads=[bk], writes=[XD[0]])
                    mm_chunks([(wc_, (lambda kt: wc_[:, kt, :]), 8, hn_rhs(0), hn)], 0, T, ev_uc)
                    c.op("pool", lambda e: e.tensor_copy(out=XD[1][:], in_=XD[0][:, ::-1, ::-1]), reads=[XD[0]], writes=[XD[1]])

                    for d_ in ([slice(0, 2)] if ct == 0 else []):
                        tsc(P(d_, P_LR), spm[:, d_, 0, :], -1e-4, ALU.min, rd=(spm,))
                        c.op("act", lambda e: e.activation(out=P(d_, P_DT), in_=spm[:, d_, 2, :], func=AF.Exp), reads=[spm], writes=[prm])
                        tt(P(d_, P_X1), P(d_, P_LR), P(d_, P_DT), ALU.mult)
                        tt(P(d_, P_TH), spm[:, d_, 1, :], P(d_, P_DT), ALU.mult, rd=(prm, spm))
                        c.op("act", lambda e: e.activation(out=P(d_, P_RHO), in_=P(d_, P_X1), func=AF.Exp), reads=[prm], writes=[prm])
                        c.op("act", lambda e: e.activation(out=P(d_, P_RHO8), in_=P(d_, P_X1), func=AF.Exp, scale=8.0), reads=[prm], writes=[prm])
                        c.op("act", lambda e: e.activation(out=P(d_, P_S), in_=P(d_, P_TH), func=AF.Sin, scale=1.0 / 64), reads=[prm], writes=[prm])
                        c.op("act", lambda e: e.activation(out=P(d_, P_C), in_=P(d_, P_TH), func=AF.Sin, scale=1.0 / 64,
                                                           bias=hpi[:, 0:1]), reads=[prm, hpi], writes=[prm])
                        for it in range(6):
                            tt(P(d_, P_T1), P(d_, P_C), P(d_, P_S), ALU.mult)
                            tt(P(d_, P_T2), P(d_, P_C), P(d_, P_C), ALU.mult)
                            tt(P(d_, P_T3), P(d_, P_S), P(d_, P_S), ALU.mult)
                            tt(P(d_, P_C), P(d_, P_T2), P(d_, P_T3), ALU.subtract)
                            tsc(P(d_, P_S), P(d_, P_T1), 2.0, ALU.mult)
                        c.op("dve", lambda e: e.tensor_copy(out=ck[:, d_, 0, :], in_=P(d_, P_C)), reads=[prm], writes=[ck])
                        c.op("dve", lambda e: e.tensor_copy(out=sk[:, d_, 0, :], in_=P(d_, P_S)), reads=[prm], writes=[sk])
                        for k in range(1, 11):
                            tt(P(d_, P_T1), ck[:, d_, k - 1, :], sk[:, d_, k - 1, :], ALU.mult, rd=(ck, sk))
                            tt(P(d_, P_T2), ck[:, d_, k - 1, :], ck[:, d_, k - 1, :], ALU.mult, rd=(ck,))
                            tt(P(d_, P_T3), sk[:, d_, k - 1, :], sk[:, d_, k - 1, :], ALU.mult, rd=(sk,))
                            tt(ck[:, d_, k, :], P(d_, P_T2), P(d_, P_T3), ALU.subtract, wr=(ck,))
                            tsc(sk[:, d_, k, :], P(d_, P_T1), 2.0, ALU.mult, wr=(sk,))
                        tt(P(d_, P_AR), P(d_, P_RHO), P(d_, P_C), ALU.mult)
                        tt(P(d_, P_AI), P(d_, P_RHO), P(d_, P_S), ALU.mult)
                        tsc(P(d_, P_NR), P(d_, P_AR), -1.0, ALU.add)
                        tt(P(d_, P_T1), P(d_, P_LR), P(d_, P_LR), ALU.mult)
                        tt(P(d_, P_T2), spm[:, d_, 1, :], spm[:, d_, 1, :], ALU.mult, rd=(spm,))
                        tt(P(d_, P_DEN), P(d_, P_T1), P(d_, P_T2), ALU.add)
                        c.op("dve", lambda e: e.reciprocal(out=P(d_, P_DEN), in_=P(d_, P_DEN)), reads=[prm], writes=[prm])
                        tt(P(d_, P_T1), P(d_, P_NR), P(d_, P_LR), ALU.mult)
                        tt(P(d_, P_T2), P(d_, P_AI), spm[:, d_, 1, :], ALU.mult, rd=(prm, spm))
                        tt(P(d_, P_T1), P(d_, P_T1), P(d_, P_T2), ALU.add)
                        tt(P(d_, P_FR), P(d_, P_T1), P(d_, P_DEN), ALU.mult)
                        tt(P(d_, P_T1), P(d_, P_AI), P(d_, P_LR), ALU.mult)
                        tt(P(d_, P_T2), P(d_, P_NR), spm[:, d_, 1, :], ALU.mult, rd=(prm, spm))
                        tt(P(d_, P_T1), P(d_, P_T1), P(d_, P_T2), ALU.subtract)
                        tt(P(d_, P_FI), P(d_, P_T1), P(d_, P_DEN), ALU.mult)
                        tsc(P(d_, P_NFI), P(d_, P_FI), -1.0, ALU.mult)
                        c.op("pool", lambda e: e.memset(lp[:, d_, 0, 0, :], 1.0), writes=[lp])
                        c.op("pool", lambda e: e.memset(lp[:, d_, 0, 1, :], 0.0), writes=[lp])
                        c.op("dve", lambda e: e.tensor_copy(out=lp[:, d_, 1, 0, :], in_=P(d_, P_AR)), reads=[prm], writes=[lp])
                        c.op("dve", lambda e: e.tensor_copy(out=lp[:, d_, 1, 1, :], in_=P(d_, P_AI)), reads=[prm], writes=[lp])
                        for k in range(2, 9):
                            pr_, pi__ = lp[:, d_, k - 1, 0, :], lp[:, d_, k - 1, 1, :]
                            tt(P(d_, P_T1), pr_, P(d_, P_AR), ALU.mult, rd=(lp, prm))
                            tt(P(d_, P_T2), pi__, P(d_, P_AI), ALU.mult, rd=(lp, prm))
                            tt(lp[:, d_, k, 0, :], P(d_, P_T1), P(d_, P_T2), ALU.subtract, wr=(lp,))
                            tt(P(d_, P_T1), pr_, P(d_, P_AI), ALU.mult, rd=(lp, prm))
                            tt(P(d_, P_T2), pi__, P(d_, P_AR), ALU.mult, rd=(lp, prm))
                            tt(lp[:, d_, k, 1, :], P(d_, P_T1), P(d_, P_T2), ALU.add, wr=(lp,))
                    for d_ in (range(2) if (bstage is None or bstage >= 2) else []):
                        c.op("pool", lambda e: e.memset(ctab[:, :, 0:1], 1.0), writes=[ctab])
                        c.op("pool", lambda e: e.memset(stab[:, :, 0:1], 0.0), writes=[stab])
                        for k in range(7):
                            n = 1 << k
                            cs_b = ck[:, d_, k + 3, cs].unsqueeze(2).to_broadcast([128, 4, n])
                            ss_b = sk[:, d_, k + 3, cs].unsqueeze(2).to_broadcast([128, 4, n])
                            ta, tb = TM[4], TM[5]
                            c.op("dve", lambda e: e.tensor_tensor(out=ta[:, :, 0:n], in0=stab[:, :, 0:n], in1=ss_b, op=ALU.mult),
                                 reads=[stab, sk], writes=[ta])
                            c.op("dve", lambda e: e.tensor_tensor(out=tb[:, :, 0:n], in0=ctab[:, :, 0:n], in1=cs_b, op=ALU.mult),
                                 reads=[ctab, ck], writes=[tb])
                            c.op("dve", lambda e: e.tensor_tensor(out=ctab[:, :, n:2 * n], in0=tb[:, :, 0:n], in1=ta[:, :, 0:n], op=ALU.subtract),
                                 reads=[ta, tb], writes=[ctab])
                            tc2, td2 = TM[6], TM[7]
                            c.op("dve", lambda e: e.tensor_tensor(out=tc2[:, :, 0:n], in0=ctab[:, :, 0:n], in1=ss_b, op=ALU.mult),
                                 reads=[ctab, sk], writes=[tc2])
                            c.op("dve", lambda e: e.tensor_tensor(out=td2[:, :, 0:n], in0=stab[:, :, 0:n], in1=cs_b, op=ALU.mult),
                                 reads=[stab, ck], writes=[td2])
                            c.op("dve", lambda e: e.tensor_tensor(out=stab[:, :, n:2 * n], in0=td2[:, :, 0:n], in1=tc2[:, :, 0:n], op=ALU.add),
                                 reads=[tc2, td2], writes=[stab])
                        c.dma(BT[:], bpadT_d[L, d_, ct * 4:(ct + 1) * 4].rearrange("s r p c -> p s r c"), writes=[BT], sem_buf=BT, q="pool")
                        c.dma(st4[:], cpad_d[L, d_, ct * 4:(ct + 1) * 4].rearrange("s r p c -> p s r c"), writes=[st4], sem_buf=st4)
                        fr_b, fi_b, nfi_b = (prm[:, d_, k_, cs].unsqueeze(2).to_broadcast([128, 4, 128]) for k_ in (P_FR, P_FI, P_NFI))
                        ta, tb, tc2, td2 = TM[4], TM[5], TM[6], TM[7]
                        c.op("dve", lambda e: e.tensor_tensor(out=ta[:], in0=st4[:, :, 1, :], in1=fi_b, op=ALU.mult), reads=[st4, prm], writes=[ta])
                        c.op("dve", lambda e: e.tensor_tensor(out=tb[:], in0=st4[:, :, 0, :], in1=fr_b, op=ALU.mult), reads=[st4, prm], writes=[tb])
                        c.op("pool", lambda e: e.tensor_tensor(out=G[:, 0, :, 0, :], in0=tb[:], in1=ta[:], op=ALU.subtract), reads=[ta, tb], writes=[G])
                        c.op("dve", lambda e: e.tensor_tensor(out=tc2[:], in0=st4[:, :, 1, :], in1=fr_b, op=ALU.mult), reads=[st4, prm], writes=[tc2])
                        c.op("dve", lambda e: e.tensor_tensor(out=td2[:], in0=st4[:, :, 0, :], in1=nfi_b, op=ALU.mult), reads=[st4, prm], writes=[td2])
                        c.op("pool", lambda e: e.tensor_tensor(out=G[:, 0, :, 1, :], in0=td2[:], in1=tc2[:], op=ALU.subtract), reads=[tc2, td2], writes=[G])
                        tsc(nlpi[:, :, :], lp[:, d_, :, 1, cs], -1.0, ALU.mult, rd=(lp,), wr=(nlpi,))
                        idb = ident[:].unsqueeze(1).to_broadcast([128, 4, 128])
                        for k in range(0, 9):
                            dset = dg[k % 2]
                            if k >= 1:
                                for q_, src_ in ((0, lp[:, d_, k, 0, cs]), (1, lp[:, d_, k, 1, cs]), (2, nlpi[:, k, :])):
                                    c.op("dve", lambda e, q_=q_, src_=src_, dset=dset: e.tensor_tensor(
                                        out=dset[:, q_], in0=idb, in1=src_.unsqueeze(2).to_broadcast([128, 4, 128]), op=ALU.mult),
                                        reads=[ident, lp, nlpi], writes=[dset])
                                bkr, bki = bank(), bank()
                                for st_ in range(4):
                                    sl_ = slice(st_ * 128, (st_ + 1) * 128)
                                    c.op("pe", lambda e, st_=st_, sl_=sl_: e.matmul(bkr[:, sl_], lhsT=dset[:, 0, st_, :], rhs=G[:, 0, st_, 0, :],
                                                                                   start=True, stop=False), reads=[dset, G], writes=[bkr])
                                    c.op("pe", lambda e, st_=st_, sl_=sl_: e.matmul(bkr[:, sl_], lhsT=dset[:, 1, st_, :], rhs=G[:, 0, st_, 1, :],
                                                                                   start=False, stop=True), reads=[dset, G], writes=[bkr])
                                for st_ in range(4):
                                    sl_ = slice(st_ * 128, (st_ + 1) * 128)
                                    c.op("pe", lambda e, st_=st_, sl_=sl_: e.matmul(bki[:, sl_], lhsT=dset[:, 0, st_, :], rhs=G[:, 0, st_, 1, :],
                                                                                   start=True, stop=False), reads=[dset, G], writes=[bki])
                                    c.op("pe", lambda e, st_=st_, sl_=sl_: e.matmul(bki[:, sl_], lhsT=dset[:, 2, st_, :], rhs=G[:, 0, st_, 0, :],
                                                                                   start=False, stop=True), reads=[dset, G], writes=[bki])
                                c.op("act", lambda e, k=k: e.activation(out=G[:, k, :, 0, :], in_=bkr[:].rearrange("p (a b) -> p a b", a=4),
                                                                        func=AF.Identity), reads=[bkr], writes=[G])
                                c.op("act", lambda e, k=k: e.activation(out=G[:, k, :, 1, :], in_=bki[:].rearrange("p (a b) -> p a b", a=4),
                                                                        func=AF.Identity), reads=[bki], writes=[G])
                            if k <= 7:
                                i = 7 - k
                                bkr, bki = bank(), bank()
                                for st_ in range(4):
                                    sl_ = slice(st_ * 128, (st_ + 1) * 128)
                                    if k == 0:
                                        c.op("pe", lambda e, st_=st_, sl_=sl_: e.matmul(bkr[:, sl_], lhsT=BT[:, st_, 0, :], rhs=identb[:],
                                                                                       start=True, stop=True), reads=[BT, identb], writes=[bkr])
                                    else:
                                        c.op("pe", lambda e, st_=st_, sl_=sl_: e.matmul(bkr[:, sl_], lhsT=BT[:, st_, 0, :], rhs=dset[:, 0, st_, :],
                                                                                       start=True, stop=False), reads=[BT, dset], writes=[bkr])
                                        c.op("pe", lambda e, st_=st_, sl_=sl_: e.matmul(bkr[:, sl_], lhsT=BT[:, st_, 1, :], rhs=dset[:, 2, st_, :],
                                                                                       start=False, stop=True), reads=[BT, dset], writes=[bkr])
                                for st_ in range(4):
                                    sl_ = slice(st_ * 128, (st_ + 1) * 128)
                                    if k == 0:
                                        c.op("pe", lambda e, st_=st_, sl_=sl_: e.matmul(bki[:, sl_], lhsT=BT[:, st_, 1, :], rhs=identb[:],
                                                                                       start=True, stop=True), reads=[BT, identb], writes=[bki])
                                    else:
                                        c.op("pe", lambda e, st_=st_, sl_=sl_: e.matmul(bki[:, sl_], lhsT=BT[:, st_, 0, :], rhs=dset[:, 1, st_, :],
                                                                                       start=True, stop=False), reads=[BT, dset], writes=[bki])
                                        c.op("pe", lambda e, st_=st_, sl_=sl_: e.matmul(bki[:, sl_], lhsT=BT[:, st_, 1, :], rhs=dset[:, 0, st_, :],
                                                                                       start=False, stop=True), reads=[BT, dset], writes=[bki])
                                c.op("act", lambda e, i=i: e.activation(out=WI[:, i, 0].rearrange("p a b -> p (a b)"), in_=bkr[:], func=AF.Identity),
                                     reads=[bkr], writes=[WI])
                                c.op("act", lambda e, i=i: e.activation(out=WI[:, i, 1].rearrange("p a b -> p (a b)"), in_=bki[:], func=AF.Identity),
                                     reads=[bki], writes=[WI])
                        for hb in range(2):
                            bk = bank()
                            for tq in range(4):
                                tau = hb * 4 + tq
                                i_ = 0
                                for st_ in range(4):
                                    for ri in range(2):
                                        c.op("pe", lambda e, tq=tq, tau=tau, st_=st_, ri=ri, i_=i_, bk=bk: e.matmul(
                                            bk[:, tq * 128:(tq + 1) * 128], lhsT=BT[:, st_, ri, :], rhs=G[:, tau, st_, ri, :],
                                            start=(i_ == 0), stop=(i_ == 7)), reads=[BT, G], writes=[bk])
                                        i_ += 1
                            c.op("act", lambda e, bk=bk, hb=hb: e.activation(
                                out=KT[:, hb * 4:hb * 4 + 4, :], in_=bk[:].rearrange("p (a b) -> p a b", a=4), func=AF.Identity),
                                reads=[bk], writes=[KT])
                        if bstage is not None and bstage < 3:
                            continue
                        c.op("pool", lambda e: e.memset(carr[:], 0.0), writes=[carr])
                        c.op("pool", lambda e: e.memset(SBr[:, :, 0:1], 0.0), writes=[SBr])
                        c.op("pool", lambda e: e.memset(SBi[:, :, 0:1], 0.0), writes=[SBi])
                        X_ = XD[d_]
                        for u_ in range(4):
                            m0 = u_ * 128
                            PSr, PSi = bank(), bank()
                            for ri, PS_ in ((0, PSr), (1, PSi)):
                                for st_ in range(4):
                                    for i in range(8):
                                        c.op("pe", lambda e, ri=ri, PS_=PS_, st_=st_, i=i: e.matmul(
                                            PS_[:, st_ * 128:(st_ + 1) * 128], lhsT=WI[:, i, ri, st_, :], rhs=X_[:, i, m0:m0 + 128],
                                            start=(i == 0), stop=(i == 7)), reads=[WI, X_], writes=[PS_])
                            pr = PSr[:].rearrange("p (s j) -> p s j", s=4)
                            pi_ = PSi[:].rearrange("p (s j) -> p s j", s=4)
                            T1, T2, T3, T4, BR, BI, SR, SI = TM
                            c.op("dve", lambda e: e.tensor_tensor(out=T1[:], in0=pr, in1=ctab[:], op=ALU.mult), reads=[PSr, ctab], writes=[T1])
                            c.op("dve", lambda e: e.tensor_tensor(out=T2[:], in0=pi_, in1=stab[:], op=ALU.mult), reads=[PSi, stab], writes=[T2])
                            c.op("pool", lambda e: e.tensor_tensor(out=BR[:], in0=T1[:], in1=T2[:], op=ALU.add), reads=[T1, T2], writes=[BR])
                            c.op("dve", lambda e: e.tensor_tensor(out=T3[:], in0=pi_, in1=ctab[:], op=ALU.mult), reads=[PSi, ctab], writes=[T3])
                            c.op("dve", lambda e: e.tensor_tensor(out=T4[:], in0=pr, in1=stab[:], op=ALU.mult), reads=[PSr, stab], writes=[T4])
                            c.op("pool", lambda e: e.tensor_tensor(out=BI[:], in0=T3[:], in1=T4[:], op=ALU.subtract), reads=[T3, T4], writes=[BI])
                            for (B_, S_, ci) in ((BR, SR, 0), (BI, SI, 1)):
                                for st_ in range(4):
                                    c.op("dve", lambda e, B_=B_, S_=S_, st_=st_, ci=ci: e.tensor_tensor_scan(
                                        out=S_[:, st_, :], data0=prm[:, d_, P_RHO8, ct * 4 + st_:ct * 4 + st_ + 1].to_broadcast([128, 128]), data1=B_[:, st_, :],
                                        initial=carr[:, ci, st_:st_ + 1], op0=ALU.mult, op1=ALU.add),
                                        reads=[B_, prm, carr], writes=[S_])
                            lr_, li_ = SR[:, :, 127], SI[:, :, 127]
                            c7, s7 = ck[:, d_, 10, cs], sk[:, d_, 10, cs]
                            c.op("dve", lambda e: e.tensor_tensor(out=ctmp[:, 0, :], in0=lr_, in1=c7, op=ALU.mult), reads=[SR, ck], writes=[ctmp])
                            c.op("dve", lambda e: e.tensor_tensor(out=ctmp[:, 1, :], in0=li_, in1=s7, op=ALU.mult), reads=[SI, sk], writes=[ctmp])
                            c.op("dve", lambda e: e.tensor_tensor(out=ctmp[:, 2, :], in0=lr_, in1=s7, op=ALU.mult), reads=[SR, sk], writes=[ctmp])
                            c.op("dve", lambda e: e.tensor_tensor(out=ctmp[:, 3, :], in0=li_, in1=c7, op=ALU.mult), reads=[SI, ck], writes=[ctmp])
                            c.op("dve", lambda e: e.tensor_tensor(out=carr[:, 0, :], in0=ctmp[:, 0, :], in1=ctmp[:, 1, :], op=ALU.subtract),
                                 reads=[ctmp], writes=[carr])
                            c.op("dve", lambda e: e.tensor_tensor(out=carr[:, 1, :], in0=ctmp[:, 2, :], in1=ctmp[:, 3, :], op=ALU.add),
                                 reads=[ctmp], writes=[carr])
                            R1, R2, R3, R4 = T1, T2, T3, T4
                            c.op("pool", lambda e: e.tensor_tensor(out=R1[:], in0=SR[:], in1=ctab[:], op=ALU.mult), reads=[SR, ctab], writes=[R1])
                            c.op("pool", lambda e: e.tensor_tensor(out=R2[:], in0=SI[:], in1=stab[:], op=ALU.mult), reads=[SI, stab], writes=[R2])
                            c.op("dve", lambda e: e.tensor_tensor(out=SBr[:, :, 1 + m0:1 + m0 + 128], in0=R1[:], in1=R2[:], op=ALU.subtract),
                                 reads=[R1, R2], writes=[SBr])
                            c.op("pool", lambda e: e.tensor_tensor(out=R3[:], in0=SR[:], in1=stab[:], op=ALU.mult), reads=[SR, stab], writes=[R3])
                            c.op("dve", lambda e: e.tensor_tensor(out=R4[:], in0=SI[:], in1=ctab[:], op=ALU.mult), reads=[SI, ctab], writes=[R4])
                            c.op("pool", lambda e: e.tensor_tensor(out=SBi[:, :, 1 + m0:1 + m0 + 128], in0=R3[:], in1=R4[:], op=ALU.add),
                                 reads=[R3, R4], writes=[SBi])
                        if bstage is not None and bstage < 4:
                            continue
                        yv = YACC[:] if d_ == 0 else YACC[:, ::-1]
                        for j in range(8):
                            PY = bank()
                            tot = (j + 1) + 8
                            i_ = 0
                            for i in range(j + 1):
                                c.op("pe", lambda e, i=i, j=j, i_=i_, PY=PY: e.matmul(PY[:, 0:512], lhsT=KT[:, j - i, :], rhs=X_[:, i, :],
                                                                                   start=(i_ == 0), stop=(i_ == tot - 1)),
                                     reads=[KT, X_], writes=[PY])
                                i_ += 1
                            for st_ in range(4):
                                for ri, SB_ in ((0, SBr), (1, SBi)):
                                    c.op("pe", lambda e, st_=st_, ri=ri, SB_=SB_, j=j, i_=i_, PY=PY: e.matmul(
                                        PY[:, 0:512], lhsT=G[:, j + 1, st_, ri, :], rhs=SB_[:, st_, 0:512],
                                        start=(i_ == 0), stop=(i_ == tot - 1)), reads=[G, SB_], writes=[PY])
                                    i_ += 1
                            c.op("dve", lambda e, j=j, PY=PY: e.tensor_tensor(out=yv[:, j::8], in0=PY[:, 0:512], in1=yv[:, j::8], op=ALU.add),
                                 reads=[PY, YACC], writes=[YACC])
                    for cc in range(8):
                        ysl = YACC[:, cc * 512:(cc + 1) * 512]
                        g_ = TM[cc % 4][:].rearrange("p a b -> p (a b)")
                        gb_ = TM[cc % 4]
                        c.op("act", lambda e, ysl=ysl, g_=g_: e.activation(out=g_, in_=ysl, func=AF.Square), reads=[YACC], writes=[gb_])
                        c.op("dve", lambda e, g_=g_: e.tensor_scalar(out=g_, in0=g_, scalar1=0.044715, scalar2=1.0, op0=ALU.mult, op1=ALU.add),
                             reads=[gb_], writes=[gb_])
                        c.op("pool", lambda e, ysl=ysl, g_=g_: e.tensor_tensor(out=g_, in0=g_, in1=ysl, op=ALU.mult), reads=[gb_, YACC], writes=[gb_])
                        c.op("act", lambda e, g_=g_: e.activation(out=g_, in_=g_, func=AF.Sigmoid, scale=1.5957691216057308), reads=[gb_], writes=[gb_])
                        c.op("dve", lambda e, ysl=ysl, g_=g_: e.tensor_tensor(out=ysl, in0=ysl, in1=g_, op=ALU.mult), reads=[gb_, YACC], writes=[YACC])
                    c.op("pool", lambda e: e.tensor_copy(out=YGB[:], in_=YACC[:]), reads=[YACC], writes=[YGB])
                    c.dma(ygT_d[ct * 128:(ct + 1) * 128, :], YACC[:], reads=[YACC], writes=[dram_misc], sem_buf=YACC, store=True)
                    c.dma(ygb_d[ct * 128:(ct + 1) * 128, :], YGB[:], reads=[YGB], writes=[dram_misc], sem_buf=YGB, store=True)
                c.barrier()
                c.release([wc_, spm, prm, ck, sk, lp, hpi, ident, identb, nlpi, st4, ctab, stab, ttmp, BT, G, WI, KT, YACC,
                           SBr, SBi, carr, ctmp, YGB] + XD + TM + dg)
            if 'C' not in skip:
                with ExitStack() as ph:
                    ygb = c.sbuf("ygb", [128, 4, T], BF16, ph)
                    wgl = c.sbuf("wgl", [128, 4, 128], BF16, ph)
                    wcg = c.sbuf("wcg", [128, 8, 128], BF16, ph)
                    SGL = [c.sbuf("SGL%d" % i, [128, 1024], F32, ph) for i in range(2)]
                    SGC = [c.sbuf("SGC%d" % i, [128, 1024], F32, ph) for i in range(2)]
                    YGS = [c.sbuf("YGS%d" % i, [128, 1024], F32, ph) for i in range(2)]
                    YSC = [c.sbuf("YSC%d" % i, [128, 1024], BF16, ph) for i in range(2)]
                    c.dma(ygb[:], ygb_d.rearrange("(k p) t -> p k t", p=128), reads=[dram_misc], writes=[ygb], sem_buf=ygb)
                    for ct in range(4):
                        load_w(wgl, wgl[:], gluw_d[L, :, ct * 128:(ct + 1) * 128].rearrange("(kt p) n -> p kt n", p=128), 4, 128)
                        load_w(wcg, wcg[:], win_cols(L, 7 * W + ct * 128, 128), 8, 128, fold_g=True)
                        for si in range(4):
                            t0 = si * 1024
                            sgl, sgc, ygs, ysc = SGL[si % 2], SGC[si % 2], YGS[si % 2], YSC[si % 2]
                            c.dma(ygs[:], ygT_d[ct * 128:(ct + 1) * 128, t0:t0 + 1024], reads=[dram_misc], writes=[ygs], sem_buf=ygs)

                            def ev_l(bk, a, n, sgl=sgl, t0=t0):
                                c.op("act", lambda e: e.activation(out=sgl[:, a - t0:a - t0 + n], in_=bk[:, 0:n], func=AF.Sigmoid,
                                                                   bias=chp[:, V_GLUB, ct:ct + 1]), reads=[bk, chp], writes=[sgl])

                            def ev_c(bk, a, n, sgc=sgc, t0=t0):
                                c.op("act", lambda e: e.activation(out=sgc[:, a - t0:a - t0 + n], in_=bk[:, 0:n], func=AF.Silu),
                                     reads=[bk], writes=[sgc])
                            mm_chunks([(wgl, (lambda kt: wgl[:, kt, :]), 4, (lambda kt, a, n: ygb[:, kt, a:a + n]), ygb)], t0, 1024, ev_l)
                            mm_chunks([(wcg, (lambda kt: wcg[:, kt, :]), 8, hn_rhs(0), hn)], t0, 1024, ev_c)
                            c.op("dve", lambda e, ygs=ygs, sgl=sgl: e.tensor_tensor(out=ygs[:], in0=ygs[:], in1=sgl[:], op=ALU.mult),
                                 reads=[ygs, sgl], writes=[ygs])
                            if dbg:
                                c.dma(dbg_d[2, ct * 128:(ct + 1) * 128, t0:t0 + 1024], ygs[:], reads=[ygs], writes=[dram_misc],
                                      sem_buf=ygs, store=True)
                            c.op("pool", lambda e, ygs=ygs, sgc=sgc, ysc=ysc: e.tensor_tensor(out=ysc[:], in0=ygs[:], in1=sgc[:], op=ALU.mult),
                                 reads=[ygs, sgc], writes=[ysc])
                            c.dma(ysT_d[2, ct * 128:(ct + 1) * 128, t0:t0 + 1024], ysc[:], reads=[ysc], writes=[dram_ys],
                                  sem_buf=ysc, store=True)
                    c.barrier()
                    c.release([ygb, wgl, wcg] + SGL + SGC + YGS + YSC)
            if stop_after == 'C':
                break
            with ExitStack() as ph:
                wqs = [c.sbuf("wq%d" % i, [128, 8, 128], BF16, ph) for i in range(2)]
                wks = [c.sbuf("wk%d" % i, [128, 8, 128], BF16, ph) for i in range(2)]
                wvs = [c.sbuf("wv%d" % i, [128, 8, 128], BF16, ph) for i in range(2)]
                wbgs = [c.sbuf("wbg%d" % i, [128, 8, 128], BF16, ph) for i in range(2)]

                def loadB(h_):
                    for lst, col in ((wqs, 2), (wks, 3), (wvs, 4), (wbgs, 5)):
                        load_w(lst[h_ % 2], lst[h_ % 2][:], win_cols(L, col * W + h_ * 128, 128), 8, 128)
                if 'B' not in skip:
                    loadB(0)
                qkg = c.sbuf("qkg", [128, 2], F32, ph)
                eps2 = c.sbuf("eps2", [128, 1], F32, ph)
                QZ = c.sbuf("QZ", [128, 64, 2, 64], BF16, ph)
                KN = c.sbuf("KN", [128, T], BF16, ph)
                Zf = c.sbuf("Zf", [128, 1024], F32, ph)
                SQb = c.sbuf("SQb", [128, 1024], BF16, ph)
                RS = c.sbuf("RS", [128, 1024], F32, ph)
                Vs = [c.sbuf("VA%d" % i, [128, 32, 2, 65], BF16, ph) for i in range(2)]
                identf = c.sbuf("identf", [128, 128], F32, ph)
                Yn = [c.sbuf("Yn%d" % i, [64, 2, 64], F32, ph) for i in range(2)]
                rec = [c.sbuf("rec%d" % i, [64, 2], F32, ph) for i in range(2)]
                bmk = c.sbuf("bmk", [128, 8, 512], F32, ph)
                ES = [c.sbuf("ES%d" % i, [128, 512], F32, ph) for i in range(2)]
                EB = [c.sbuf("EB%d" % i, [128, 512], BF16, ph) for i in range(2)]
                RD = c.sbuf("RD", [128, 512], F32, ph)
                SGF = c.sbuf("SGF", [128, T], F32, ph)
                YB = [c.sbuf("YB%d" % i, [128, 512], F32, ph) for i in range(2)]
                SGb = [c.sbuf("SGb%d" % i, [128, 512], F32, ph) for i in range(2)]
                YSb = [c.sbuf("YSb%d" % i, [128, 512], BF16, ph) for i in range(2)]
                for hp in (range(4) if 'B' not in skip else []):
                    wq, wk, wv, wbg = wqs[hp % 2], wks[hp % 2], wvs[hp % 2], wbgs[hp % 2]
                    if hp + 1 < 4:
                        loadB(hp + 1)
                    c.dma(qkg[:], qkg_d[L], writes=[qkg], sem_buf=qkg)
                    c.dma(bmk[:], bm_d[L, hp].rearrange("p c h x -> p c (h x)"), writes=[bmk], sem_buf=bmk)
                    c.op("pool", lambda e: e.memset(eps2[:], 1e-6), writes=[eps2])
                    c.op("dve", lambda e: e.tensor_scalar(out=qkg[:, 0:1], in0=qkg[:, 0:1], scalar1=0.125, scalar2=None, op0=ALU.mult),
                         reads=[qkg], writes=[qkg])
                    c.op("pool", lambda e: e.memset(QZ[64:128, :, 0, :], 0.0), writes=[QZ])
                    c.op("pool", lambda e: e.memset(QZ[0:64, :, 1, :], 0.0), writes=[QZ])
                    for (w_, gi_) in ((wq, 0), (wk, 1)):
                        for si in range(4):
                            t0 = si * 1024

                            def ev_z(bk, a, n, t0=t0):
                                o = a - t0
                                c.op("act", lambda e: e.activation(out=Zf[:, o:o + n], in_=bk[:, 0:n], func=AF.Identity),
                                     reads=[bk], writes=[Zf])
                                c.op("act", lambda e: e.activation(out=SQb[:, o:o + n], in_=bk[:, 0:n], func=AF.Square),
                                     reads=[bk], writes=[SQb])
                            mm_chunks([(w_, (lambda kt, w_=w_: w_[:, kt, :]), 8, hn_rhs(0), hn)], t0, 1024, ev_z)

                            def ev_r(bk, a, n, t0=t0):
                                o = a - t0
                                c.op("act", lambda e: e.activation(out=RS[:, o:o + n], in_=bk[:, 0:n], func=AF.Ln, scale=1.0 / 64,
                                                                   bias=eps2[:, 0:1]), reads=[bk, eps2], writes=[RS])
                            mm_chunks([(blk1, (lambda kt: blk1[:]), 1, (lambda kt, a, n, t0=t0: SQb[:, a - t0:a - t0 + n]), SQb)],
                                      t0, 1024, ev_r)
                            c.op("act", lambda e: e.activation(out=RS[:], in_=RS[:], func=AF.Exp, scale=-0.5), reads=[RS], writes=[RS])
                            if gi_ == 1:
                                c.op("dve", lambda e, t0=t0: e.scalar_tensor_tensor(
                                    out=KN[:, t0:t0 + 1024], in0=Zf[:], scalar=qkg[:, 1:2], in1=RS[:], op0=ALU.mult, op1=ALU.mult),
                                    reads=[Zf, qkg, RS], writes=[KN])
                            else:
                                for h2 in range(2):
                                    ps_ = slice(h2 * 64, (h2 + 1) * 64)
                                    c.op("dve", lambda e, t0=t0, h2=h2, ps_=ps_: e.scalar_tensor_tensor(
                                        out=QZ[ps_, t0 // 64:t0 // 64 + 16, h2, :], in0=Zf[ps_, :].rearrange("p (r q) -> p r q", q=64),
                                        scalar=qkg[ps_, 0:1], in1=RS[ps_, :].rearrange("p (r q) -> p r q", q=64),
                                        op0=ALU.mult, op1=ALU.mult), reads=[Zf, qkg, RS], writes=[QZ])
                    if hp == 0:
                        c.dma(identf[:], ident_d, writes=[identf], sem_buf=identf)
                        for par in range(2):
                            c.op("pool", lambda e, par=par: e.memset(Vs[par][:, :, :, 64:65], 1.0), writes=[Vs[par]])
                    for par in range(2):
                        ntile = 32 - par
                        for tg in range(0, ntile, 4):
                            bk = bank()
                            nt_ = min(4, ntile - tg)
                            for j in range(nt_):
                                tt_ = tg + j
                                for kt in range(8):
                                    c.op("pe", lambda e, j=j, tt_=tt_, kt=kt, bk=bk: e.matmul(
                                        bk[:, j * 128:(j + 1) * 128], lhsT=hn[:, kt, 2 + 64 * par + tt_ * 128:2 + 64 * par + (tt_ + 1) * 128],
                                        rhs=wv[:, kt, :], start=(kt == 0), stop=(kt == 7)), reads=[hn, wv], writes=[bk])
                            c.op("act", lambda e, bk=bk, tg=tg, nt_=nt_: e.activation(
                                out=Vs[par][:, tg:tg + nt_, :, 0:64], in_=bk[:, 0:nt_ * 128].rearrange("p (a h d) -> p a h d", a=nt_, h=2),
                                func=AF.Identity), reads=[bk], writes=[Vs[par]])
                    def ev_bg(bk, a, n):
                        c.op("act", lambda e: e.activation(out=SGF[:, a:a + n], in_=bk[:, 0:n], func=AF.Silu), reads=[bk], writes=[SGF])
                    mm_chunks([(wbg, (lambda kt: wbg[:, kt, :]), 8, hn_rhs(0), hn)], 0, T, ev_bg)
                    state["nb"] = 6
                    nrows = 64 if bstage is None else 8 * bstage

                    def emit_qk(r):
                        rs_ = min(max(r - 4, 0), 56)
                        cfg = r - rs_
                        par = rs_ % 2
                        tt0 = (rs_ - par) // 2
                        k0 = 64 * rs_
                        bS = bank()
                        for kt in range(4):
                            c.op("pe", lambda e, kt=kt: e.matmul(
                                bS[:, kt * 128:(kt + 1) * 128], lhsT=KN[:, k0 + kt * 128:k0 + (kt + 1) * 128],
                                rhs=QZ[:, r, :, :].rearrange("p h q -> p (h q)"), start=True, stop=True),
                                reads=[KN, QZ], writes=[bS])
                        es_, eb_ = ES[r % 2], EB[r % 2]
                        c.op("dve", lambda e: e.tensor_tensor(out=es_[:], in0=bS[:], in1=bmk[:, cfg, :], op=ALU.add),
                             reads=[bS, bmk], writes=[es_])
                        c.op("act", lambda e: e.activation(out=eb_[:], in_=es_[:], func=AF.Exp), reads=[es_], writes=[eb_])
                        return eb_, par, tt0

                    def emit_tr(r):
                        rg, rl = r // 8, r % 8
                        bT = pb[6 + rg % 2]
                        yn_ = Yn[r % 2]
                        c.op("pe", lambda e: e.transpose(bT[:, rl * 64:(rl + 1) * 64], yn_[:].rearrange("p h d -> p (h d)"), identf[0:64, 0:64]),
                             reads=[yn_, identf], writes=[bT])
                        if rl != 7:
                            return
                        t0 = rg * 512
                        yb_, ys_ = YB[rg % 2], YSb[rg % 2]
                        if dbg:
                            c.op("act", lambda e: e.activation(out=yb_[:], in_=bT[:], func=AF.Identity), reads=[bT], writes=[yb_])
                            c.dma(dbg_d[1, hp * 128:(hp + 1) * 128, t0:t0 + 512], yb_[:], reads=[yb_], writes=[dram_misc],
                                  sem_buf=yb_, store=True)
                        c.op("dve", lambda e: e.tensor_tensor(out=ys_[:], in0=bT[:], in1=SGF[:, t0:t0 + 512], op=ALU.mult),
                             reads=[bT, SGF], writes=[ys_])
                        c.dma(ysT_d[1, hp * 128:(hp + 1) * 128, t0:t0 + 512], ys_[:], reads=[ys_], writes=[dram_ys],
                              sem_buf=ys_, store=True)

                    pend = emit_qk(0) if nrows else None
                    prev = None
                    for r in range(nrows):
                        eb_, par, tt0 = pend
                        if r + 1 < nrows:
                            pend = emit_qk(r + 1)
                        bY = bank()
                        for h2 in range(2):
                            for kt in range(4):
                                c.op("pe", lambda e, h2=h2, kt=kt: e.matmul(
                                    bY[0:64, h2 * 65:(h2 + 1) * 65], lhsT=eb_[:, (kt * 2 + h2) * 64:(kt * 2 + h2 + 1) * 64],
                                    rhs=Vs[par][:, tt0 + kt, h2, :], start=(kt == 0), stop=(kt == 3)),
                                    reads=[Vs[par], eb_], writes=[bY])
                        byv = bY[0:64, 0:130].rearrange("p (h c) -> p h c", c=65)
                        rc_, yn_ = rec[r % 2], Yn[r % 2]
                        c.op("dve", lambda e: e.reciprocal(out=rc_[:], in_=byv[:, :, 64]), reads=[bY], writes=[rc_])
                        c.op("dve", lambda e: e.tensor_tensor(out=yn_[:], in0=byv[:, :, 0:64], in1=rc_[:].unsqueeze(2).to_broadcast([64, 2, 64]),
                                                              op=ALU.mult), reads=[bY, rc_], writes=[yn_])
                        if prev is not None:
                            emit_tr(prev)
                        prev = r
                    if prev is not None:
                        emit_tr(prev)
                    state["nb"] = 8
                c.barrier()
                c.release(wqs + wks + wvs + wbgs + [qkg, eps2, KN, Zf, SQb, RS, bmk, RD, SGF, identf] + [QZ] + Vs + ES + EB + YB + SGb + YSb + Yn + rec)
            if stop_after == 'B':
                break
            if stop_after == 'mix':
                break
            TB = 512
            with ExitStack() as ph:
                wmg = c.sbuf("wmg", [128, 8, 4 * D], BF16, ph)
                wbs = c.sbuf("wbs", [128, 4, 4, D], BF16, ph)
                ysb = c.sbuf("ysb", [128, 16, TB], BF16, ph)
                mgb = [c.sbuf("mgb%d" % i, [128, 8, TB], BF16, ph) for i in range(1)]
                acc = c.sbuf("acc", [128, TB], F32, ph)
                sgm = [c.sbuf("sgm%d" % i, [128, TB], F32, ph) for i in range(2)]
                tmpm = [c.sbuf("tmpm%d" % i, [128, TB], F32, ph) for i in range(2)]
                wmg_t = [Buf("wmg_t%d" % i, wmg.t) for i in range(8)]
                wbs_t = [Buf("wbs_t%d" % i, wbs.t) for i in range(8)]
                ysb_t = [Buf("ysb_t%d" % i, ysb.t) for i in range(4)]
                for dt in range(8):
                    for n in range(4):
                        cb = n * 8 + dt
                        load_w(wmg_t[dt], wmg[:, :, cb * 128:(cb + 1) * 128], win_cols(L, 10 * W + cb * 128, 128), 8, 128)
                        load_w(wbs_t[dt], wbs[:, n, :, dt * 128:(dt + 1) * 128],
                               wbr_d[L, n, :, dt * 128:(dt + 1) * 128].rearrange("(kt p) n -> p kt n", p=128), 4, 128)
                for blk in range(T // TB):
                    t0 = blk * TB
                    mg_ = mgb[0]
                    for n in range(4):
                        c.dma(ysb[:, n * 4:(n + 1) * 4, :], ysT_d[n, :, t0:t0 + TB].rearrange("(k p) t -> p k t", p=128),
                              reads=[dram_ys], writes=[ysb_t[n]], sem_buf=ysb_t[n])
                    for dt in range(8):
                        for n in range(4):
                            b1 = bank()
                            for kt in range(8):
                                c.op("pe", lambda e, kt=kt, b1=b1, n=n, dt=dt: e.matmul(
                                    b1[:, 0:TB], lhsT=wmg[:, kt, n * D + dt * 128:n * D + (dt + 1) * 128],
                                    rhs=hn[:, kt, 2 + t0:2 + t0 + TB], start=(kt == 0), stop=(kt == 7)),
                                    reads=[wmg_t[dt], hn], writes=[b1])
                            b2 = bank()
                            for kt in range(4):
                                c.op("pe", lambda e, kt=kt, b2=b2, n=n, dt=dt: e.matmul(
                                    b2[:, 0:TB], lhsT=wbs[:, n, kt, dt * 128:(dt + 1) * 128], rhs=ysb[:, n * 4 + kt, :],
                                    start=(kt == 0), stop=(kt == 3)), reads=[wbs_t[dt], ysb_t[n]], writes=[b2])
                            sg_ = sgm[n % 2]
                            c.op("act", lambda e, sg_=sg_, b1=b1: e.activation(out=sg_[:], in_=b1[:, 0:TB], func=AF.Sigmoid),
                                 reads=[b1], writes=[sg_])
                            if n == 0:
                                c.op("dve", lambda e, sg_=sg_, b2=b2: e.tensor_tensor(out=acc[:], in0=b2[:, 0:TB], in1=sg_[:], op=ALU.mult),
                                     reads=[b2, sg_], writes=[acc])
                            else:
                                tm_ = tmpm[n % 2]
                                c.op("dve", lambda e, sg_=sg_, b2=b2, tm_=tm_: e.tensor_tensor(out=tm_[:], in0=b2[:, 0:TB], in1=sg_[:], op=ALU.mult),
                                     reads=[b2, sg_], writes=[tm_])
                                if n < 3:
                                    c.op("pool", lambda e, tm_=tm_: e.tensor_tensor(out=acc[:], in0=acc[:], in1=tm_[:], op=ALU.add),
                                         reads=[acc, tm_], writes=[acc])
                                else:
                                    c.op("pool", lambda e, tm_=tm_, dt=dt, mg_=mg_: e.tensor_tensor(out=mg_[:, dt, :], in0=acc[:], in1=tm_[:], op=ALU.add),
                                         reads=[acc, tm_], writes=[mg_])
                    c.dma(mgT_d[:, t0:t0 + TB].rearrange("(k p) t -> p k t", p=128), mg_[:], reads=[mg_], writes=[dram_mg],
                          sem_buf=mg_, store=True)
                c.barrier()
                c.release([wmg, wbs, ysb, acc] + mgb + sgm + tmpm + wmg_t + wbs_t + ysb_t)
            with ExitStack() as ph:
                wo_sb = c.sbuf("wo_sb", [128, 8, D], BF16, ph)
                pg_sb = c.sbuf("pg_sb", [128, 8, D], BF16, ph)
                pp_sb = c.sbuf("pp_sb", [128, 2, D], BF16, ph)
                mgi = [c.sbuf("mgi%d" % i, [128, 8, TB], BF16, ph) for i in range(2)]
                xk8 = [c.sbuf("xk8%d" % i, [128, 8, TB], F32, ph) for i in range(2)]
                x1b = c.sbuf("x1b", [128, 8, TB], BF16, ph)
                ptf = c.sbuf("ptf", [128, 2, TB], F32, ph)
                ptb = c.sbuf("ptb", [128, 2, TB], BF16, ph)
                sg2 = [c.sbuf("sg2%d" % i, [128, TB], F32, ph) for i in range(2)]
                wo_t = [Buf("wo_t%d" % i, wo_sb.t) for i in range(8)]
                pg_t = [Buf("pg_t%d" % i, pg_sb.t) for i in range(8)]
                pp_t = [Buf("pp_t%d" % i, pp_sb.t) for i in range(8)]
                x1bs = [x1b, c.sbuf("x1b2", [128, 8, TB], BF16, ph)]
                ptbs = [ptb, c.sbuf("ptb2", [128, 2, TB], BF16, ph)]
                for db in range(8):
                    sl = slice(db * 128, (db + 1) * 128)
                    load_w(wo_t[db], wo_sb[:, :, sl], wout_d[L, :, sl].rearrange("(kt p) n -> p kt n", p=128), 8, 128)
                for db in range(8):
                    sl = slice(db * 128, (db + 1) * 128)
                    load_w(pg_t[db], pg_sb[:, :, sl], pgate_d[L, :, sl].rearrange("(kt p) n -> p kt n", p=128), 8, 128)
                    load_w(pp_t[db], pp_sb[:, :, sl], pproj_d[L, :, sl].rearrange("(kt p) n -> p kt n", p=128), 2, 128)
                nblk = T // TB

                def m2_stage1(blk):
                    t0 = blk * TB
                    mg_, x1, x1b_, ptb_ = mgi[blk % 2], xk8[blk % 2], x1bs[blk % 2], ptbs[blk % 2]
                    c.dma(mg_[:], mgT_d[:, t0:t0 + TB].rearrange("(k p) t -> p k t", p=128), reads=[dram_mg], writes=[mg_], sem_buf=mg_)
                    c.dma(ptf[:], pT_d[L, :, t0:t0 + TB].rearrange("(k p) t -> p k t", p=128), writes=[ptf], sem_buf=ptf)
                    c.dma(x1[:], xsrc_d[:, t0:t0 + TB].rearrange("(k p) t -> p k t", p=128), reads=([dram_x1] if L > 0 else []), writes=[x1], sem_buf=x1)
                    c.op("pool", lambda e: e.tensor_copy(out=ptb_[:], in_=ptf[:]), reads=[ptf], writes=[ptb_])
                    for dt in range(8):
                        b1 = bank()
                        for kt in range(8):
                            c.op("pe", lambda e, kt=kt, b1=b1, dt=dt: e.matmul(
                                b1[:, 0:TB], lhsT=wo_sb[:, kt, dt * 128:(dt + 1) * 128], rhs=mg_[:, kt, :],
                                start=(kt == 0), stop=(kt == 7)), reads=[wo_t[dt], mg_], writes=[b1])
                        c.op("dve", lambda e, dt=dt, b1=b1: e.tensor_tensor(out=x1[:, dt, :], in0=b1[:, 0:TB], in1=x1[:, dt, :], op=ALU.add),
                             reads=[b1, x1], writes=[x1])
                        c.op("pool", lambda e, dt=dt: e.tensor_copy(out=x1b_[:, dt, :], in_=x1[:, dt, :]), reads=[x1], writes=[x1b_])

                def m2_stage2(blk):
                    t0 = blk * TB
                    x1, x1b_, ptb_ = xk8[blk % 2], x1bs[blk % 2], ptbs[blk % 2]
                    for dt in range(8):
                        b1 = bank()
                        for kt in range(8):
                            c.op("pe", lambda e, kt=kt, b1=b1, dt=dt: e.matmul(
                                b1[:, 0:TB], lhsT=pg_sb[:, kt, dt * 128:(dt + 1) * 128], rhs=x1b_[:, kt, :],
                                start=(kt == 0), stop=(kt == 7)), reads=[pg_t[dt], x1b_], writes=[b1])
                        b2 = bank()
                        for kt in range(2):
                            c.op("pe", lambda e, kt=kt, b2=b2, dt=dt: e.matmul(
                                b2[:, 0:TB], lhsT=pp_sb[:, kt, dt * 128:(dt + 1) * 128], rhs=ptb_[:, kt, :],
                                start=(kt == 0), stop=(kt == 1)), reads=[pp_t[dt], ptb_], writes=[b2])
                        sg_ = sg2[dt % 2]
                        c.op("act", lambda e, sg_=sg_, b1=b1: e.activation(out=sg_[:], in_=b1[:, 0:TB], func=AF.Sigmoid),
                             reads=[b1], writes=[sg_])
                        c.op("dve", lambda e, sg_=sg_, b2=b2: e.tensor_tensor(out=sg_[:], in0=b2[:, 0:TB], in1=sg_[:], op=ALU.mult),
                             reads=[b2, sg_], writes=[sg_])
                        c.op("pool", lambda e, sg_=sg_, dt=dt: e.tensor_tensor(out=x1[:, dt, :], in0=sg_[:], in1=x1[:, dt, :], op=ALU.add),
                             reads=[sg_, x1], writes=[x1])
                    c.dma(xdst_d[:, t0:t0 + TB].rearrange("(k p) t -> p k t", p=128), x1[:], reads=[x1],
                          writes=[dram_out if L == NL - 1 else dram_x1], sem_buf=x1, store=True)

                m2_stage1(0)
                for blk in range(nblk):
                    if blk + 1 < nblk:
                        m2_stage1(blk + 1)
                    m2_stage2(blk)
                c.barrier()
                c.release([wo_sb, pg_sb, pp_sb, ptf] + x1bs + ptbs + mgi + xk8 + sg2 + wo_t + pg_t + pp_t)
            if stop_after == 'L0':
                break

        c.barrier()
    return nc


def _prep_shared(inp):
    f = np.float32
    sh = {}
    sh["gl"] = np.ascontiguousarray(inp["norm_scale"].reshape(NL, 8, 128).transpose(0, 2, 1)).astype(f)
    sh["w_in"] = np.ascontiguousarray(inp["w_in"]).astype(f)
    sh["cwbc"] = np.ascontiguousarray(np.broadcast_to(inp["lru_conv_w"][:, None], (NL, 128, 4, W))).astype(f)

    def pc(v):
        return v.reshape(NL, 4, 128).transpose(0, 2, 1)
    vecs = [inp["lru_conv_b"], inp["lru_b_r"][:, 0], inp["lru_b_r"][:, 1], inp["lru_b_i"][:, 0], inp["lru_b_i"][:, 1],
            inp["lru_lambda"][:, 0], inp["lru_lambda"][:, 1], inp["ssm_d"], inp["ssm_glu_b"], inp["pool_scale"]]
    vecs += [inp["lru_conv_w"][:, k] for k in range(4)]
    sh["chp"] = np.ascontiguousarray(np.stack([pc(v) for v in vecs], axis=2)).astype(f)
    bd = np.zeros((NL, 2, 2, 4, 128, 128), f)
    for gt, key in enumerate(("lru_w_r", "lru_w_i")):
        w = inp[key]
        for ct in range(4):
            for j in range(2):
                bd[:, :, gt, ct, j * 64:(j + 1) * 64, j * 64:(j + 1) * 64] = w[:, :, 2 * ct + j]
    sh["bd"] = bd
    qkg = np.zeros((NL, 128, 2), f)
    qkg[:, :, 0] = np.tile(inp["na_q_gain"], (1, 2))
    qkg[:, :, 1] = np.tile(inp["na_k_gain"], (1, 2))
    sh["qkg"] = qkg
    rpb = inp["na_rel_bias"]
    kk = np.arange(512)
    ki = kk // 64
    kc = kk % 64
    qc = np.arange(64)
    win = np.clip(qc - 8, 0, 48)
    valid = (kc[:, None] >= win[None, :]) & (kc[:, None] < win[None, :] + 16)
    dc = np.clip(kc[:, None] - qc[None, :], -15, 15) + 15
    bm = np.empty((NL, 4, 128, 8, 2, 256), f)
    for cfg in range(8):
        dr = ki - cfg + 7
        ok = valid & (dr[:, None] >= 0) & (dr[:, None] <= 14)
        drc = np.clip(dr, 0, 14)
        g = rpb[:, :, drc[:, None], dc]
        g = np.where(ok[None, None], g, f(-1e30)).astype(f)
        g = g.reshape(NL, 4, 2, 4, 128, 64)
        bm[:, :, :, cfg] = g.transpose(0, 1, 4, 3, 2, 5).reshape(NL, 4, 128, 2, 256)
    sh["bm"] = bm
    def sl(a):
        return a.reshape(NL, 2, 16, 2, 64).transpose(0, 1, 3, 4, 2).reshape(NL, 2, 128, 16)
    ldt = np.broadcast_to(inp["ssm_log_dt"][..., None], (NL, 2, 32, 64))
    sh["ssmp"] = np.ascontiguousarray(np.stack([sl(inp["ssm_a_re"]), sl(inp["ssm_a_im"]), sl(ldt)], axis=3)).astype(f)
    bpad = np.zeros((NL, 2, 16, 2, 128, 128), f)
    cpad = np.zeros((NL, 2, 16, 2, 128, 128), f)
    for ri, (bk, ck) in enumerate((("ssm_b_re", "ssm_c_re"), ("ssm_b_im", "ssm_c_im"))):
        B = inp[bk]
        C = inp[ck]
        for stg in range(16):
            q = stg % 4
            for gl in range(2):
                g_ = 2 * stg + gl
                r0 = 32 * q + gl * 16
                bpad[:, :, stg, ri, r0:r0 + 16, gl * 64:(gl + 1) * 64] = B[:, :, g_].transpose(0, 1, 3, 2)
                cpad[:, :, stg, ri, gl * 64:(gl + 1) * 64, r0:r0 + 16] = C[:, :, g_].transpose(0, 1, 3, 2)
    sh["bpad"] = bpad
    sh["cpad"] = cpad
    sh["bpadT"] = np.ascontiguousarray(bpad.transpose(0, 1, 2, 3, 5, 4))
    sh["ident"] = np.eye(128, dtype=f)
    sh["gluw"] = np.ascontiguousarray(inp["ssm_glu_w"]).astype(f)
    sh["poolw"] = np.ascontiguousarray(inp["pool_w"]).astype(f)
    pe = np.zeros((128, 4, 16), f)
    for gi, w in enumerate((2, 4, 8, 16)):
        t = np.concatenate([np.arange(8), np.arange(T - 8, T)])
        lo = np.clip(t - w // 2, 0, T)
        hi = np.clip(t - w // 2 + w, 0, T)
        pe[:, gi, :] = (1.0 / (hi - lo)).astype(f)[None]
    sh["pedge"] = pe
    sh["wbr"] = np.ascontiguousarray(inp["w_branch"]).astype(f)
    sh["wout"] = np.ascontiguousarray(inp["w_out"]).astype(f)
    sh["pproj"] = np.ascontiguousarray(inp["ple_proj"]).astype(f)
    sh["pgate"] = np.ascontiguousarray(inp["ple_gate"]).astype(f)
    return sh


def _in_maps(inp, cores):
    sh = _prep_shared(inp)
    maps = []
    for b in cores:
        m = dict(sh)
        m["xT"] = np.ascontiguousarray(inp["x"][b].T).astype(np.float32)
        m["pT"] = np.ascontiguousarray(inp["p"][:, b].transpose(0, 2, 1)).astype(np.float32)
        maps.append(m)
    return maps


def kernel(**inputs):
    inp = {k: np.asarray(v) for k, v in inputs.items()}
    nc = build()
    res = run_bass_kernel_spmd(nc, _in_maps(inp, range(8)), core_ids=list(range(8)))
    out = np.stack([np.ascontiguousarray(r["outT"].T) for r in res.results], axis=0)
    return out.astype(np.float32)
```

```python
import numpy as np
from contextlib import ExitStack
import concourse.bass as bass
import concourse.mybir as mybir
from concourse.bass_utils import run_bass_kernel_spmd

F32 = mybir.dt.float32
BF16 = mybir.dt.bfloat16
AF = mybir.ActivationFunctionType
ALU = mybir.AluOpType

T = 4096
D = 1024
W = 512
NIN = 9216
NL = 2
V_CONVB, V_BR0, V_BR1, V_BI0, V_BI1, V_LAM0, V_LAM1, V_SSMD, V_GLUB, V_PSCALE, V_CW0 = range(11)
NV = 14


class Sem:
    def __init__(self, handle, is_dma):
        self.h = handle
        self.is_dma = is_dma
        self.total = 0


class Buf:
    def __init__(self, name, t=None):
        self.name = name
        self.t = t
        self.last_w = {}
        self.readers = {}
        self.dsems = {}
        self.is_psum = False

    def __getitem__(self, k):
        return self.t[k]


class Eng:
    def __init__(self, name, e, sem):
        self.name = name
        self.e = e
        self.sem = sem
        self.waited = {}


class Ctx:
    def __init__(self, nc, stack):
        self.nc = nc
        self.stack = stack
        self.nsem = 0
        self.sems = []
        self.engs = {}
        for name, e in (("pe", nc.tensor), ("act", nc.scalar), ("dve", nc.vector),
                        ("pool", nc.gpsimd), ("sp", nc.sync)):
            self.engs[name] = Eng(name, e, self.new_sem(name, False))
        self.ninstr = 0
        self.free_dma_sems = {"hw": [], "sw": []}

    def new_sem(self, name, is_dma):
        self.nsem += 1
        h = self.stack.enter_context(self.nc.semaphore("s%d_%s" % (self.nsem, name)))
        s = Sem(h, is_dma)
        self.sems.append(s)
        return s

    def dma_sem(self, kind):
        if self.free_dma_sems[kind]:
            return self.free_dma_sems[kind].pop()
        return self.new_sem("dma" + kind, True)

    def sbuf(self, name, shape, dtype, stack=None):
        self.ninstr += 1
        name = "sb%d_%s" % (self.ninstr, name)
        t = (stack or self.stack).enter_context(self.nc.sbuf_tensor(name, list(shape), dtype))
        return Buf(name, t)

    def psum(self, name, shape, dtype):
        t = self.stack.enter_context(self.nc.psum_tensor(name, list(shape), dtype))
        b = Buf(name, t)
        b.is_psum = True
        return b

    def _wait(self, eng, deps):
        for sem, val in deps.items():
            if sem.is_dma:
                val = sem.total
            if sem is eng.sem and eng.name == "pe":
                continue
            if eng.waited.get(sem, 0) >= val:
                continue
            eng.e.wait_ge(sem.h, val)
            eng.waited[sem] = val

    @staticmethod
    def _merge(d, sem, val):
        if d.get(sem, 0) < val:
            d[sem] = val

    def _deps(self, reads, writes):
        deps = {}
        for b in reads:
            for s, v in b.last_w.items():
                self._merge(deps, s, v)
            if b.is_psum:
                for s, v in b.readers.items():
                    self._merge(deps, s, v)
        for b in writes:
            for s, v in b.last_w.items():
                self._merge(deps, s, v)
            for s, v in b.readers.items():
                self._merge(deps, s, v)
        return deps

    def op(self, engname, fn, reads=(), writes=()):
        eng = self.engs[engname]
        self._wait(eng, self._deps(reads, writes))
        ins = fn(eng.e)
        eng.sem.total += 1
        ins.then_inc(eng.sem.h, 1)
        for b in writes:
            b.last_w = {eng.sem: eng.sem.total}
            b.readers = {}
        for b in reads:
            if b not in writes:
                self._merge(b.readers, eng.sem, eng.sem.total)
        self.ninstr += 1
        return ins

    def dma(self, out, in_, reads=(), writes=(), sem_buf=None, store=False, q="sp"):
        eng = self.engs[q]
        kind = "sw" if q == "pool" else "hw"
        key = ("st" if store else "ld", kind)
        if key not in sem_buf.dsems:
            sem_buf.dsems[key] = self.dma_sem(kind)
        sem = sem_buf.dsems[key]
        self._wait(eng, self._deps(reads, writes))
        ins = eng.e.dma_start(out=out, in_=in_)
        sem.total += 16
        ins.then_inc(sem.h, 16)
        for b in writes:
            b.last_w = {sem: sem.total}
            b.readers = {}
        for b in reads:
            if b not in writes:
                self._merge(b.readers, sem, sem.total)
        self.ninstr += 1
        return ins

    def barrier(self):
        allv = {s: s.total for s in self.sems if s.total > 0}
        for eng in self.engs.values():
            self._wait(eng, dict(allv))

    def release(self, bufs):
        for b in bufs:
            for (_, kind), sm in b.dsems.items():
                self.free_dma_sems[kind].append(sm)
            b.dsems = {}


def build(dbg=False, stop_after=None, skip=(), bstage=None):
    nc = bass.Bass("TRN2", target_bir_lowering=False)

    def din(name, shape, dt=F32):
        return nc.dram_tensor(name, list(shape), dt, kind="ExternalInput").ap()

    def dscr(name, shape, dt=F32):
        return nc.dram_tensor(name, list(shape), dt).ap()

    xT_d = din("xT", [D, T])
    pT_d = din("pT", [NL, 256, T])
    g_d = din("gl", [NL, 128, 8])
    win_d = din("w_in", [NL, D, NIN])
    cwbc_d = din("cwbc", [NL, 128, 4, W])
    chp_d = din("chp", [NL, 128, NV, 4])
    bd_d = din("bd", [NL, 2, 2, 4, 128, 128])
    qkg_d = din("qkg", [NL, 128, 2])
    bm_d = din("bm", [NL, 4, 128, 8, 2, 256])
    sp_d = din("ssmp", [NL, 2, 128, 3, 16])
    bpad_d = din("bpad", [NL, 2, 16, 2, 128, 128])
    cpad_d = din("cpad", [NL, 2, 16, 2, 128, 128])
    bpadT_d = din("bpadT", [NL, 2, 16, 2, 128, 128])
    ident_d = din("ident", [128, 128])
    gluw_d = din("gluw", [NL, W, W])
    poolw_d = din("poolw", [NL, 4, 128, 128])
    pedge_d = din("pedge", [128, 4, 16])
    wbr_d = din("wbr", [NL, 4, W, D])
    wout_d = din("wout", [NL, D, D])
    pproj_d = din("pproj", [NL, 256, D])
    pgate_d = din("pgate", [NL, D, D])
    outT_d = nc.dram_tensor("outT", [D, T], F32, kind="ExternalOutput").ap()
    x1T_d = nc.dram_tensor("x1T", [D, T], F32, kind="ExternalOutput").ap() if dbg else dscr("x1T", [D, T])
    ysT_d = dscr("ysT", [4, W, T], BF16)
    hfT_d = dscr("hfT", [W, T])
    ygT_d = dscr("ygT", [W, T])
    ygb_d = dscr("ygb", [W, T], BF16)
    mgT_d = dscr("mgT", [D, T], BF16)
    if dbg:
        dbg_d = nc.dram_tensor("dbg", [4, W, T], F32, kind="ExternalOutput").ap()

    with ExitStack() as st:
        c = Ctx(nc, st)
        pb = [c.psum("pb%d" % i, [128, 512], F32) for i in range(8)]
        state = {"bank": 0, "cast": 0, "nb": 8}

        def bank():
            b = pb[state["bank"] % state["nb"]]
            state["bank"] += 1
            return b

        hn = c.sbuf("hn", [128, 8, T + 4], BF16)
        gsb = c.sbuf("gsb", [128, 8], F32)
        chp = c.sbuf("chp", [128, NV, 4], F32)
        onesf = c.sbuf("onesf", [128, 128], F32)
        onesb = c.sbuf("onesb", [128, 128], BF16)
        blk1 = c.sbuf("blk1", [128, 128], BF16)
        dram_x1 = Buf("x1T_dram")
        dram_ys = Buf("ys_dram")
        dram_out = Buf("out_dram")
        dram_misc = Buf("misc_dram")
        dram_mg = Buf("mg_dram")

        c.op("pool", lambda e: e.memset(onesf[:], 1.0), writes=[onesf])
        c.op("pool", lambda e: e.memset(onesb[:], 1.0), writes=[onesb])
        c.op("pool", lambda e: e.memset(blk1[:], 0.0), writes=[blk1])
        c.op("pool", lambda e: e.memset(blk1[0:64, 0:64], 1.0), writes=[blk1])
        c.op("pool", lambda e: e.memset(blk1[64:128, 64:128], 1.0), writes=[blk1])
        c.op("pool", lambda e: e.memset(hn[:, :, 0:2], 0.0), writes=[hn])
        c.op("pool", lambda e: e.memset(hn[:, :, T + 2:T + 4], 0.0), writes=[hn])

        def cast_eng():
            state["cast"] += 1
            return ("dve", "pool")[state["cast"] % 2]

        def load_w(dst, dst_ap, src_ap, nkt, ncols, fold_g=False, eng=None):
            c.dma(dst_ap, src_ap, writes=[dst], sem_buf=dst, q="pool")

        def win_cols(L, c0, n):
            return win_d[L, :, c0:c0 + n].rearrange("(kt p) n -> p kt n", p=128)

        def mm_chunks(wlist, t0, ntok, evac):
            for a in range(t0, t0 + ntok, 512):
                n = min(512, t0 + ntok - a)
                bk = bank()
                tot = sum(w_[2] for w_ in wlist)
                i = 0
                for (wbuf, lf, nkt, rf, rbuf) in wlist:
                    for kt in range(nkt):
                        c.op("pe", lambda e, lf=lf, rf=rf, kt=kt, i=i: e.matmul(
                            bk[:, 0:n], lhsT=lf(kt), rhs=rf(kt, a, n), start=(i == 0), stop=(i == tot - 1)),
                            reads=[wbuf, rbuf], writes=[bk])
                        i += 1
                evac(bk, a, n)

        def hn_rhs(shift=0):
            return lambda kt, a, n: hn[:, kt, 2 + a + shift:2 + a + shift + n]

        for L in range(NL):
            xsrc_d = xT_d if L == 0 else x1T_d
            xdst_d = x1T_d if L == 0 else outT_d
            c.dma(gsb[:], g_d[L], writes=[gsb], sem_buf=gsb)
            c.dma(chp[:], chp_d[L], writes=[chp], sem_buf=chp)
            with ExitStack() as ph:
                xk = [c.sbuf("xk%d" % i, [128, T], F32, ph) for i in range(2)]
                sq = [c.sbuf("sq%d" % i, [128, T], F32, ph) for i in range(2)]
                rstd = c.sbuf("rstd", [128, T], F32, ph)
                for kt in range(8):
                    xb = xk[kt % 2]
                    sb_ = sq[kt % 2]
                    c.dma(xb[:], xsrc_d[kt * 128:(kt + 1) * 128, :], reads=([dram_x1] if L > 0 else []), writes=[xb], sem_buf=xb,
                          q=("sp", "pool")[kt % 2])
                    c.op("act", lambda e, xb=xb, sb_=sb_: e.activation(out=sb_[:], in_=xb[:], func=AF.Square),
                         reads=[xb], writes=[sb_])
                    for ch in range(8):
                        c.op("pe", lambda e, ch=ch, sb_=sb_, kt=kt: e.matmul(
                            pb[ch][:], lhsT=onesf[:], rhs=sb_[:, ch * 512:(ch + 1) * 512], start=(kt == 0), stop=(kt == 7)),
                            reads=[onesf, sb_], writes=[pb[ch]])
                epsb = c.sbuf("epsb", [128, 1], F32, ph)
                c.op("pool", lambda e: e.memset(epsb[:], 1e-6), writes=[epsb])
                for ch in range(8):
                    sl = slice(ch * 512, (ch + 1) * 512)
                    c.op("act", lambda e, ch=ch, sl=sl: e.activation(out=rstd[:, sl], in_=pb[ch][:], func=AF.Ln,
                                                                     scale=1.0 / D, bias=epsb[:, 0:1]),
                         reads=[pb[ch], epsb], writes=[rstd])
                c.op("act", lambda e: e.activation(out=rstd[:], in_=rstd[:], func=AF.Exp, scale=-0.5), reads=[rstd], writes=[rstd])
                for kt in range(8):
                    xb = xk[kt % 2]
                    c.dma(xb[:], xsrc_d[kt * 128:(kt + 1) * 128, :], reads=([dram_x1] if L > 0 else []), writes=[xb], sem_buf=xb,
                          q=("sp", "pool")[kt % 2])
                    c.op("dve", lambda e, xb=xb, kt=kt: e.scalar_tensor_tensor(
                        out=hn[:, kt, 2:2 + T], in0=xb[:], scalar=gsb[:, kt:kt + 1], in1=rstd[:], op0=ALU.mult, op1=ALU.mult),
                        reads=[xb, rstd, gsb], writes=[hn])
                c.barrier()
                c.release(xk + sq + [rstd, epsb])

            TS = 1024
            with ExitStack() as ph:
                wzs = [c.sbuf("wz%d" % i, [128, 8, 128], BF16, ph) for i in range(2)]
                wgs = [c.sbuf("wg%d" % i, [128, 8, 128], BF16, ph) for i in range(2)]
                bdws = [c.sbuf("bdw%d" % i, [128, 4, 128], BF16, ph) for i in range(2)]

                def loadA(ct_):
                    load_w(wzs[ct_ % 2], wzs[ct_ % 2][:], win_cols(L, 0 * W + ct_ * 128, 128), 8, 128)
                    load_w(wgs[ct_ % 2], wgs[ct_ % 2][:], win_cols(L, 1 * W + ct_ * 128, 128), 8, 128)
                    for d__ in range(2):
                        for gt_ in range(2):
                            c.dma(bdws[ct_ % 2][:, d__ * 2 + gt_, :], bd_d[L, d__, gt_, ct_], writes=[bdws[ct_ % 2]],
                                  sem_buf=bdws[ct_ % 2], q="pool")
                if 'A' not in skip:
                    loadA(0)
                cp = c.sbuf("cp", [128, 4], F32, ph)
                XCF = c.sbuf("XCF", [128, T], F32, ph)
                XCBF = c.sbuf("XCBF", [128, T], BF16, ph)
                R = c.sbuf("R", [128, T], F32, ph)
                GI = c.sbuf("GI", [128, T], F32, ph)
                A2 = c.sbuf("A2", [128, T + 4], F32, ph)
                HF = c.sbuf("HF", [128, T], F32, ph)
                H = c.sbuf("H", [128, TS], F32, ph)
                SG = c.sbuf("SG", [128, TS], F32, ph)
                YS = c.sbuf("YS", [128, TS], BF16, ph)
                carry = c.sbuf("carry", [128, 1], F32, ph)
                for ct in (range(4) if 'A' not in skip else []):
                    wz, wg, bdw = wzs[ct % 2], wgs[ct % 2], bdws[ct % 2]
                    if ct + 1 < 4:
                        loadA(ct + 1)
                    for d_ in range(2):
                        c.op("act", lambda e, d_=d_: e.activation(out=cp[:, 2 * d_:2 * d_ + 1], in_=chp[:, V_LAM0 + d_, ct:ct + 1],
                                                                  func=AF.Exp, scale=-1.0), reads=[chp], writes=[cp])
                        c.op("act", lambda e, d_=d_: e.activation(out=cp[:, 2 * d_:2 * d_ + 1], in_=cp[:, 2 * d_:2 * d_ + 1],
                                                                  func=AF.Ln, bias=1.0, scale=1.0), reads=[cp], writes=[cp])
                        c.op("dve", lambda e, d_=d_: e.tensor_scalar(out=cp[:, 2 * d_ + 1:2 * d_ + 2], in0=cp[:, 2 * d_:2 * d_ + 1],
                                                                     scalar1=-16.0, scalar2=None, op0=ALU.mult), reads=[cp], writes=[cp])
                        c.op("dve", lambda e, d_=d_: e.tensor_scalar(out=cp[:, 2 * d_:2 * d_ + 1], in0=cp[:, 2 * d_:2 * d_ + 1],
                                                                     scalar1=-8.0, scalar2=None, op0=ALU.mult), reads=[cp], writes=[cp])
                    Z = A2
                    c.op("pool", lambda e: e.memset(Z[:, 0:2], 0.0), writes=[Z])
                    c.op("pool", lambda e: e.memset(Z[:, T + 2:T + 4], 0.0), writes=[Z])

                    def ev_z0(bk, a, n):
                        c.op("act", lambda e: e.activation(out=Z[:, 2 + a:2 + a + n], in_=bk[:, 0:n], func=AF.Identity), reads=[bk], writes=[Z])
                    mm_chunks([(wz, (lambda kt: wz[:, kt, :]), 8, hn_rhs(0), hn)], 0, T, ev_z0)
                    cwv = lambda k: chp[:, V_CW0 + k, ct:ct + 1]
                    c.op("dve", lambda e: e.tensor_scalar(out=XCF[:], in0=Z[:, 0:T], scalar1=cwv(0), scalar2=chp[:, V_CONVB, ct:ct + 1],
                                                          op0=ALU.mult, op1=ALU.add), reads=[Z, chp], writes=[XCF])
                    for k in range(1, 4):
                        c.op("dve", lambda e, k=k: e.scalar_tensor_tensor(
                            out=XCF[:], in0=Z[:, k:k + T], scalar=cwv(k), in1=XCF[:], op0=ALU.mult, op1=ALU.add),
                            reads=[Z, chp, XCF], writes=[XCF])
                    c.op("act", lambda e: e.activation(out=XCBF[:], in_=XCF[:], func=AF.Identity), reads=[XCF], writes=[XCBF])

                    for d_ in range(2):
                        def ev_gate(dst, vidx):
                            def f(bk, a, n):
                                c.op("act", lambda e: e.activation(out=dst[:, a:a + n], in_=bk[:, 0:n], func=AF.Sigmoid,
                                                                   bias=chp[:, vidx, ct:ct + 1], scale=1.0),
                                     reads=[bk, chp], writes=[dst])
                            return f
                        for gt, dst, vidx in ((0, R, V_BR0 + d_), (1, GI, V_BI0 + d_)):
                            mm_chunks([(bdw, (lambda kt, gt=gt: bdw[:, d_ * 2 + gt, :]), 1,
                                        (lambda kt, a, n: XCBF[:, a:a + n]), XCBF)], 0, T, ev_gate(dst, vidx))
                        c.op("act", lambda e: e.activation(out=A2[:, 0:T], in_=R[:], func=AF.Exp, scale=cp[:, 2 * d_ + 1:2 * d_ + 2]),
                             reads=[R, cp], writes=[A2])
                        c.op("act", lambda e: e.activation(out=R[:], in_=R[:], func=AF.Exp, scale=cp[:, 2 * d_:2 * d_ + 1]),
                             reads=[R, cp], writes=[R])
                        c.op("act", lambda e: e.activation(out=A2[:, 0:T], in_=A2[:, 0:T], func=AF.Sqrt, scale=-1.0, bias=1.0),
                             reads=[A2], writes=[A2])
                        c.op("dve", lambda e: e.tensor_tensor(out=GI[:], in0=GI[:], in1=XCF[:], op=ALU.mult), reads=[GI, XCF], writes=[GI])
                        c.op("dve", lambda e: e.tensor_tensor(out=GI[:], in0=GI[:], in1=A2[:, 0:T], op=ALU.mult), reads=[GI, A2], writes=[GI])
                        c.op("pool", lambda e: e.memset(carry[:], 0.0), writes=[carry])
                        if d_ == 0:
                            for si in range(T // TS):
                                sl = slice(si * TS, (si + 1) * TS)
                                c.op("dve", lambda e, sl=sl: e.tensor_tensor_scan(out=HF[:, sl], data0=R[:, sl], data1=GI[:, sl],
                                                                                  initial=carry[:, 0:1], op0=ALU.mult, op1=ALU.add),
                                     reads=[R, GI, carry], writes=[HF])
                                c.op("dve", lambda e, sl=sl: e.tensor_copy(out=carry[:], in_=HF[:, sl.stop - 1:sl.stop]), reads=[HF], writes=[carry])
                        else:
                            for si in reversed(range(T // TS)):
                                t0 = si * TS
                                sl = slice(t0, t0 + TS)
                                c.op("dve", lambda e, sl=sl: e.tensor_tensor_scan(out=H[:, ::-1], data0=R[:, sl][:, ::-1], data1=GI[:, sl][:, ::-1],
                                                                                  initial=carry[:, 0:1], op0=ALU.mult, op1=ALU.add),
                                     reads=[R, GI, carry], writes=[H])
                                c.op("dve", lambda e: e.tensor_copy(out=carry[:], in_=H[:, 0:1]), reads=[H], writes=[carry])

                                def ev_sg(bk, a, n, t0=t0):
                                    o = a - t0
                                    c.op("act", lambda e: e.activation(out=SG[:, o:o + n], in_=bk[:, 0:n], func=AF.Silu), reads=[bk], writes=[SG])
                                mm_chunks([(wg, (lambda kt: wg[:, kt, :]), 8, hn_rhs(0), hn)], t0, TS, ev_sg)
                                c.op("pool", lambda e, sl=sl: e.tensor_tensor(out=H[:], in0=HF[:, sl], in1=H[:], op=ALU.add), reads=[HF, H], writes=[H])
                                if dbg:
                                    c.dma(dbg_d[0, ct * 128:(ct + 1) * 128, t0:t0 + TS], H[:], reads=[H], writes=[dram_misc], sem_buf=H, store=True)
                                c.op("dve", lambda e: e.tensor_tensor(out=YS[:], in0=H[:], in1=SG[:], op=ALU.mult), reads=[H, SG], writes=[YS])
                                c.dma(ysT_d[0, ct * 128:(ct + 1) * 128, t0:t0 + TS], YS[:], reads=[YS], writes=[dram_ys], sem_buf=YS, store=True)
                c.barrier()
                c.release(wzs + wgs + bdws + [cp, XCF, XCBF, R, GI, A2, HF, H, SG, YS, carry])

            if stop_after == 'A':
                break
            with ExitStack() as ph:
                wds = [c.sbuf("wd%d" % i, [128, 8, 128], BF16, ph) for i in range(2)]
                wgds = [c.sbuf("wgd%d" % i, [128, 8, 128], BF16, ph) for i in range(2)]
                wps = [c.sbuf("wp%d" % i, [128, 128], BF16, ph) for i in range(2)]

                def loadD(g_):
                    load_w(wds[g_ % 2], wds[g_ % 2][:], win_cols(L, 8 * W + g_ * 128, 128), 8, 128)
                    load_w(wgds[g_ % 2], wgds[g_ % 2][:], win_cols(L, 9 * W + g_ * 128, 128), 8, 128)
                    c.dma(wps[g_ % 2][:], poolw_d[L, g_], writes=[wps[g_ % 2]], sem_buf=wps[g_ % 2], q="pool")
                if 'D' not in skip:
                    loadD(0)
                pedge = c.sbuf("pedge", [128, 4, 16], F32, ph)
                U = c.sbuf("U", [128, T + 16], F32, ph)
                S1 = c.sbuf("S1", [128, T + 16], F32, ph)
                S2 = c.sbuf("S2", [128, T + 16], F32, ph)
                PB = c.sbuf("PB", [128, T], BF16, ph)
                YD = [c.sbuf("YD%d" % i, [128, 1024], F32, ph) for i in range(2)]
                SGd = [c.sbuf("SGd%d" % i, [128, 1024], F32, ph) for i in range(2)]
                YSd = [c.sbuf("YSd%d" % i, [128, 1024], BF16, ph) for i in range(2)]
                for gi in (range(4) if 'D' not in skip else []):
                    win_ = (2, 4, 8, 16)[gi]
                    wd, wgd, wp = wds[gi % 2], wgds[gi % 2], wps[gi % 2]
                    if gi + 1 < 4:
                        loadD(gi + 1)
                    c.dma(pedge[:], pedge_d, writes=[pedge], sem_buf=pedge)
                    c.op("pool", lambda e: e.memset(U[:, 0:8], 0.0), writes=[U])
                    c.op("pool", lambda e: e.memset(U[:, T + 8:T + 16], 0.0), writes=[U])

                    def ev_u(bk, a, n):
                        c.op("act", lambda e: e.activation(out=U[:, 8 + a:8 + a + n], in_=bk[:, 0:n], func=AF.Identity),
                             reads=[bk], writes=[U])
                    mm_chunks([(wd, (lambda kt: wd[:, kt, :]), 8, hn_rhs(0), hn)], 0, T, ev_u)
                    c.op("dve", lambda e: e.tensor_tensor(out=S1[:, 1:T + 16], in0=U[:, 0:T + 15], in1=U[:, 1:T + 16], op=ALU.add),
                         reads=[U], writes=[S1])
                    FB, OB = S1, S2
                    if win_ >= 4:
                        c.op("dve", lambda e: e.tensor_tensor(out=S2[:, 2:T + 15], in0=S1[:, 1:T + 14], in1=S1[:, 3:T + 16], op=ALU.add),
                             reads=[S1], writes=[S2])
                        FB, OB = S2, S1
                    if win_ >= 8:
                        c.op("dve", lambda e: e.tensor_tensor(out=S1[:, 4:T + 13], in0=S2[:, 2:T + 11], in1=S2[:, 6:T + 15], op=ALU.add),
                             reads=[S2], writes=[S1])
                        FB, OB = S1, S2
                    if win_ >= 16:
                        c.op("dve", lambda e: e.tensor_tensor(out=S2[:, 8:T + 8], in0=S1[:, 4:T + 4], in1=S1[:, 12:T + 12], op=ALU.add),
                             reads=[S1], writes=[S2])
                        FB, OB = S2, S1
                    c.op("dve", lambda e: e.scalar_tensor_tensor(out=OB[:, 8:8 + T], in0=FB[:, 8:8 + T], scalar=1.0 / win_,
                                                                 in1=U[:, 8:8 + T], op0=ALU.mult, op1=ALU.subtract),
                         reads=[FB, U], writes=[OB])
                    for (i0, e0) in ((8, 0), (T, 8)):
                        c.op("dve", lambda e, i0=i0, e0=e0: e.tensor_tensor(out=OB[:, i0:i0 + 8], in0=FB[:, i0:i0 + 8],
                                                                            in1=pedge[:, gi, e0:e0 + 8], op=ALU.mult),
                             reads=[FB, pedge], writes=[OB])
                        c.op("dve", lambda e, i0=i0: e.tensor_tensor(out=OB[:, i0:i0 + 8], in0=OB[:, i0:i0 + 8],
                                                                     in1=U[:, i0:i0 + 8], op=ALU.subtract),
                             reads=[OB, U], writes=[OB])
                    c.op("act", lambda e: e.activation(out=PB[:], in_=OB[:, 8:8 + T], func=AF.Identity), reads=[OB], writes=[PB])
                    for si in range(4):
                        t0 = si * 1024
                        yd_, sg_, ys_ = YD[si % 2], SGd[si % 2], YSd[si % 2]

                        def ev_y(bk, a, n, yd_=yd_, t0=t0):
                            c.op("act", lambda e: e.activation(out=yd_[:, a - t0:a - t0 + n], in_=bk[:, 0:n], func=AF.Identity,
                                                               scale=chp[:, V_PSCALE, gi:gi + 1]), reads=[bk, chp], writes=[yd_])

                        def ev_g(bk, a, n, sg_=sg_, t0=t0):
                            c.op("act", lambda e: e.activation(out=sg_[:, a - t0:a - t0 + n], in_=bk[:, 0:n], func=AF.Silu),
                                 reads=[bk], writes=[sg_])
                        mm_chunks([(wp, (lambda kt: wp[:]), 1, (lambda kt, a, n: PB[:, a:a + n]), PB)], t0, 1024, ev_y)
                        mm_chunks([(wgd, (lambda kt: wgd[:, kt, :]), 8, hn_rhs(0), hn)], t0, 1024, ev_g)
                        if dbg:
                            c.dma(dbg_d[3, gi * 128:(gi + 1) * 128, t0:t0 + 1024], yd_[:], reads=[yd_], writes=[dram_misc],
                                  sem_buf=yd_, store=True)
                        c.op("dve", lambda e, yd_=yd_, sg_=sg_, ys_=ys_: e.tensor_tensor(out=ys_[:], in0=yd_[:], in1=sg_[:], op=ALU.mult),
                             reads=[yd_, sg_], writes=[ys_])
                        c.dma(ysT_d[3, gi * 128:(gi + 1) * 128, t0:t0 + 1024], ys_[:], reads=[ys_], writes=[dram_ys],
                              sem_buf=ys_, store=True)
                c.barrier()
                c.release(wds + wgds + wps + [pedge, U, S1, S2, PB] + YD + SGd + YSd)
            if stop_after == 'D':
                break
            if 'Z' in skip:
                with ExitStack() as ph:
                    zb = c.sbuf("zb", [128, T], BF16, ph)
                    c.op("pool", lambda e: e.memset(zb[:], 0.0), writes=[zb])
                    for n_ in (1, 2):
                        for ct in range(4):
                            c.dma(ysT_d[n_, ct * 128:(ct + 1) * 128, :], zb[:], reads=[zb], writes=[dram_ys], sem_buf=zb, store=True)
                    c.barrier()
                    c.release([zb])
            with ExitStack() as ph:
                wc_ = c.sbuf("wc_", [128, 8, 128], BF16, ph)
                spm = c.sbuf("spm", [128, 2, 3, 16], F32, ph)
                P_LR, P_DT, P_X1, P_TH, P_RHO, P_C, P_S, P_T1, P_T2, P_T3, P_AR, P_AI, P_NR, P_DEN, P_FR, P_FI, P_NFI, P_RHO8 = range(18)
                prm = c.sbuf("prm", [128, 2, 18, 16], F32, ph)
                ck = c.sbuf("ck", [128, 2, 11, 16], F32, ph)
                sk = c.sbuf("sk", [128, 2, 11, 16], F32, ph)
                lp = c.sbuf("lp", [128, 2, 9, 2, 16], F32, ph)
                hpi = c.sbuf("hpi", [128, 1], F32, ph)
                ident = c.sbuf("ident", [128, 128], F32, ph)
                ctab = c.sbuf("ctab", [128, 4, 128], F32, ph)
                stab = c.sbuf("stab", [128, 4, 128], F32, ph)
                ttmp = c.sbuf("ttmp", [128, 128], F32, ph)
                st4 = c.sbuf("st4", [128, 4, 2, 128], F32, ph)
                dg = [c.sbuf("dg%d" % i, [128, 3, 4, 128], BF16, ph) for i in range(2)]
                nlpi = c.sbuf("nlpi", [128, 9, 4], F32, ph)
                identb = c.sbuf("identb", [128, 128], BF16, ph)
                BT = c.sbuf("BT", [128, 4, 2, 128], BF16, ph)
                G = c.sbuf("G", [128, 9, 4, 2, 128], BF16, ph)
                WI = c.sbuf("WI", [128, 8, 2, 4, 128], BF16, ph)
                KT = c.sbuf("KT", [128, 8, 128], BF16, ph)
                YACC = c.sbuf("YACC", [128, T], F32, ph)
                XD = [c.sbuf("XD%d" % i, [128, 8, 512], BF16, ph) for i in range(2)]
                SBr = c.sbuf("SBr", [128, 4, 516], BF16, ph)
                SBi = c.sbuf("SBi", [128, 4, 516], BF16, ph)
                carr = c.sbuf("carr", [128, 2, 4], F32, ph)
                ctmp = c.sbuf("ctmp", [128, 4, 4], F32, ph)
                TM = [c.sbuf("TM%d" % i, [128, 4, 128], F32, ph) for i in range(8)]
                YGB = c.sbuf("YGB", [128, T], BF16, ph)
                for ct in (range(4) if 'C' not in skip else []):
                    load_w(wc_, wc_[:], win_cols(L, 6 * W + ct * 128, 128), 8, 128, fold_g=True)
                    c.op("pool", lambda e: e.memset(hpi[:], float(np.pi / 2)), writes=[hpi])
                    c.dma(ident[:], ident_d, writes=[ident], sem_buf=ident)
                    c.op("pool", lambda e: e.tensor_copy(out=identb[:], in_=ident[:]), reads=[ident], writes=[identb])
                    cs = slice(ct * 4, ct * 4 + 4)
                    if ct == 0:
                        for d_ in range(2):
                            c.dma(spm[:, d_], sp_d[L, d_], writes=[spm], sem_buf=spm)

                    def P(d_, k):
                        return prm[:, d_, k, :]

                    def tt(o, a, b, op, eng="dve", rd=(prm,), wr=(prm,)):
                        c.op(eng, lambda e: e.tensor_tensor(out=o, in0=a, in1=b, op=op), reads=list(rd), writes=list(wr))

                    def tsc(o, a, s1, op0, s2=None, op1=None, rd=(prm,), wr=(prm,)):
                        if op1 is None:
                            c.op("dve", lambda e: e.tensor_scalar(out=o, in0=a, scalar1=s1, scalar2=None, op0=op0),
                                 reads=list(rd), writes=list(wr))
                        else:
                            c.op("dve", lambda e: e.tensor_scalar(out=o, in0=a, scalar1=s1, scalar2=s2, op0=op0, op1=op1),
                                 reads=list(rd), writes=list(wr))

                    def ev_uc(bk, a, n):
                        cc = a // 512
                        c.op("act", lambda e: e.activation(out=YACC[:, a:a + n], in_=bk[:, 0:n], func=AF.Identity,
                                                           scale=chp[:, V_SSMD, ct:ct + 1]), reads=[bk, chp], writes=[YACC])
                        src = bk[:, 0:512].rearrange("p (m i) -> p i m", i=8)
                        c.op("act", lambda e: e.activation(out=XD[0][:, :, 64 * cc:64 * cc + 64], in_=src, func=AF.Identity),
                             reads=[bk], writes=[XD[0]])
                    mm_chunks([(wc_, (lambda kt: wc_[:, kt, :]), 8, hn_rhs(0), hn)], 0, T, ev_uc)
                    c.op("pool", lambda e: e.tensor_copy(out=XD[1][:], in_=XD[0][:, ::-1, ::-1]), reads=[XD[0]], writes=[XD[1]])

                    for d_ in ([slice(0, 2)] if ct == 0 else []):
                        tsc(P(d_, P_LR), spm[:, d_, 0, :], -1e-4, ALU.min, rd=(spm,))
                        c.op("act", lambda e: e.activation(out=P(d_, P_DT), in_=spm[:, d_, 2, :], func=AF.Exp), reads=[spm], writes=[prm])
                        tt(P(d_, P_X1), P(d_, P_LR), P(d_, P_DT), ALU.mult)
                        tt(P(d_, P_TH), spm[:, d_, 1, :], P(d_, P_DT), ALU.mult, rd=(prm, spm))
                        c.op("act", lambda e: e.activation(out=P(d_, P_RHO), in_=P(d_, P_X1), func=AF.Exp), reads=[prm], writes=[prm])
                        c.op("act", lambda e: e.activation(out=P(d_, P_RHO8), in_=P(d_, P_X1), func=AF.Exp, scale=8.0), reads=[prm], writes=[prm])
                        c.op("act", lambda e: e.activation(out=P(d_, P_S), in_=P(d_, P_TH), func=AF.Sin, scale=1.0 / 64), reads=[prm], writes=[prm])
                        c.op("act", lambda e: e.activation(out=P(d_, P_C), in_=P(d_, P_TH), func=AF.Sin, scale=1.0 / 64,
                                                           bias=hpi[:, 0:1]), reads=[prm, hpi], writes=[prm])
                        for it in range(6):
                            tt(P(d_, P_T1), P(d_, P_C), P(d_, P_S), ALU.mult)
                            tt(P(d_, P_T2), P(d_, P_C), P(d_, P_C), ALU.mult)
                            tt(P(d_, P_T3), P(d_, P_S), P(d_, P_S), ALU.mult)
                            tt(P(d_, P_C), P(d_, P_T2), P(d_, P_T3), ALU.subtract)
                            tsc(P(d_, P_S), P(d_, P_T1), 2.0, ALU.mult)
                        c.op("dve", lambda e: e.tensor_copy(out=ck[:, d_, 0, :], in_=P(d_, P_C)), reads=[prm], writes=[ck])
                        c.op("dve", lambda e: e.tensor_copy(out=sk[:, d_, 0, :], in_=P(d_, P_S)), reads=[prm], writes=[sk])
                        for k in range(1, 11):
                            tt(P(d_, P_T1), ck[:, d_, k - 1, :], sk[:, d_, k - 1, :], ALU.mult, rd=(ck, sk))
                            tt(P(d_, P_T2), ck[:, d_, k - 1, :], ck[:, d_, k - 1, :], ALU.mult, rd=(ck,))
                            tt(P(d_, P_T3), sk[:, d_, k - 1, :], sk[:, d_, k - 1, :], ALU.mult, rd=(sk,))
                            tt(ck[:, d_, k, :], P(d_, P_T2), P(d_, P_T3), ALU.subtract, wr=(ck,))
                            tsc(sk[:, d_, k, :], P(d_, P_T1), 2.0, ALU.mult, wr=(sk,))
                        tt(P(d_, P_AR), P(d_, P_RHO), P(d_, P_C), ALU.mult)
                        tt(P(d_, P_AI), P(d_, P_RHO), P(d_, P_S), ALU.mult)
                        tsc(P(d_, P_NR), P(d_, P_AR), -1.0, ALU.add)
                        tt(P(d_, P_T1), P(d_, P_LR), P(d_, P_LR), ALU.mult)
                        tt(P(d_, P_T2), spm[:, d_, 1, :], spm[:, d_, 1, :], ALU.mult, rd=(spm,))
                        tt(P(d_, P_DEN), P(d_, P_T1), P(d_, P_T2), ALU.add)
                        c.op("dve", lambda e: e.reciprocal(out=P(d_, P_DEN), in_=P(d_, P_DEN)), reads=[prm], writes=[prm])
                        tt(P(d_, P_T1), P(d_, P_NR), P(d_, P_LR), ALU.mult)
                        tt(P(d_, P_T2), P(d_, P_AI), spm[:, d_, 1, :], ALU.mult, rd=(prm, spm))
                        tt(P(d_, P_T1), P(d_, P_T1), P(d_, P_T2), ALU.add)
                        tt(P(d_, P_FR), P(d_, P_T1), P(d_, P_DEN), ALU.mult)
                        tt(P(d_, P_T1), P(d_, P_AI), P(d_, P_LR), ALU.mult)
                        tt(P(d_, P_T2), P(d_, P_NR), spm[:, d_, 1, :], ALU.mult, rd=(prm, spm))
                        tt(P(d_, P_T1), P(d_, P_T1), P(d_, P_T2), ALU.subtract)
                        tt(P(d_, P_FI), P(d_, P_T1), P(d_, P_DEN), ALU.mult)
                        tsc(P(d_, P_NFI), P(d_, P_FI), -1.0, ALU.mult)
                        c.op("pool", lambda e: e.memset(lp[:, d_, 0, 0, :], 1.0), writes=[lp])
                        c.op("pool", lambda e: e.memset(lp[:, d_, 0, 1, :], 0.0), writes=[lp])
                        c.op("dve", lambda e: e.tensor_copy(out=lp[:, d_, 1, 0, :], in_=P(d_, P_AR)), reads=[prm], writes=[lp])
                        c.op("dve", lambda e: e.tensor_copy(out=lp[:, d_, 1, 1, :], in_=P(d_, P_AI)), reads=[prm], writes=[lp])
                        for k in range(2, 9):
                            pr_, pi__ = lp[:, d_, k - 1, 0, :], lp[:, d_, k - 1, 1, :]
                            tt(P(d_, P_T1), pr_, P(d_, P_AR), ALU.mult, rd=(lp, prm))
                            tt(P(d_, P_T2), pi__, P(d_, P_AI), ALU.mult, rd=(lp, prm))
                            tt(lp[:, d_, k, 0, :], P(d_, P_T1), P(d_, P_T2), ALU.subtract, wr=(lp,))
                            tt(P(d_, P_T1), pr_, P(d_, P_AI), ALU.mult, rd=(lp, prm))
                            tt(P(d_, P_T2), pi__, P(d_, P_AR), ALU.mult, rd=(lp, prm))
                            tt(lp[:, d_, k, 1, :], P(d_, P_T1), P(d_, P_T2), ALU.add, wr=(lp,))
                    for d_ in (range(2) if (bstage is None or bstage >= 2) else []):
                        c.op("pool", lambda e: e.memset(ctab[:, :, 0:1], 1.0), writes=[ctab])
                        c.op("pool", lambda e: e.memset(stab[:, :, 0:1], 0.0), writes=[stab])
                        for k in range(7):
                            n = 1 << k
                            cs_b = ck[:, d_, k + 3, cs].unsqueeze(2).to_broadcast([128, 4, n])
                            ss_b = sk[:, d_, k + 3, cs].unsqueeze(2).to_broadcast([128, 4, n])
                            ta, tb = TM[4], TM[5]
                            c.op("dve", lambda e: e.tensor_tensor(out=ta[:, :, 0:n], in0=stab[:, :, 0:n], in1=ss_b, op=ALU.mult),
                                 reads=[stab, sk], writes=[ta])
                            c.op("dve", lambda e: e.tensor_tensor(out=tb[:, :, 0:n], in0=ctab[:, :, 0:n], in1=cs_b, op=ALU.mult),
                                 reads=[ctab, ck], writes=[tb])
                            c.op("dve", lambda e: e.tensor_tensor(out=ctab[:, :, n:2 * n], in0=tb[:, :, 0:n], in1=ta[:, :, 0:n], op=ALU.subtract),
                                 reads=[ta, tb], writes=[ctab])
                            tc2, td2 = TM[6], TM[7]
                            c.op("dve", lambda e: e.tensor_tensor(out=tc2[:, :, 0:n], in0=ctab[:, :, 0:n], in1=ss_b, op=ALU.mult),
                                 reads=[ctab, sk], writes=[tc2])
                            c.op("dve", lambda e: e.tensor_tensor(out=td2[:, :, 0:n], in0=stab[:, :, 0:n], in1=cs_b, op=ALU.mult),
                                 reads=[stab, ck], writes=[td2])
                            c.op("dve", lambda e: e.tensor_tensor(out=stab[:, :, n:2 * n], in0=td2[:, :, 0:n], in1=tc2[:, :, 0:n], op=ALU.add),
                                 reads=[tc2, td2], writes=[stab])
                        c.dma(BT[:], bpadT_d[L, d_, ct * 4:(ct + 1) * 4].rearrange("s r p c -> p s r c"), writes=[BT], sem_buf=BT, q="pool")
                        c.dma(st4[:], cpad_d[L, d_, ct * 4:(ct + 1) * 4].rearrange("s r p c -> p s r c"), writes=[st4], sem_buf=st4)
                        fr_b, fi_b, nfi_b = (prm[:, d_, k_, cs].unsqueeze(2).to_broadcast([128, 4, 128]) for k_ in (P_FR, P_FI, P_NFI))
                        ta, tb, tc2, td2 = TM[4], TM[5], TM[6], TM[7]
                        c.op("dve", lambda e: e.tensor_tensor(out=ta[:], in0=st4[:, :, 1, :], in1=fi_b, op=ALU.mult), reads=[st4, prm], writes=[ta])
                        c.op("dve", lambda e: e.tensor_tensor(out=tb[:], in0=st4[:, :, 0, :], in1=fr_b, op=ALU.mult), reads=[st4, prm], writes=[tb])
                        c.op("pool", lambda e: e.tensor_tensor(out=G[:, 0, :, 0, :], in0=tb[:], in1=ta[:], op=ALU.subtract), reads=[ta, tb], writes=[G])
                        c.op("dve", lambda e: e.tensor_tensor(out=tc2[:], in0=st4[:, :, 1, :], in1=fr_b, op=ALU.mult), reads=[st4, prm], writes=[tc2])
                        c.op("dve", lambda e: e.tensor_tensor(out=td2[:], in0=st4[:, :, 0, :], in1=nfi_b, op=ALU.mult), reads=[st4, prm], writes=[td2])
                        c.op("pool", lambda e: e.tensor_tensor(out=G[:, 0, :, 1, :], in0=td2[:], in1=tc2[:], op=ALU.subtract), reads=[tc2, td2], writes=[G])
                        tsc(nlpi[:, :, :], lp[:, d_, :, 1, cs], -1.0, ALU.mult, rd=(lp,), wr=(nlpi,))
                        idb = ident[:].unsqueeze(1).to_broadcast([128, 4, 128])
                        for k in range(0, 9):
                            dset = dg[k % 2]
                            if k >= 1:
                                for q_, src_ in ((0, lp[:, d_, k, 0, cs]), (1, lp[:, d_, k, 1, cs]), (2, nlpi[:, k, :])):
                                    c.op("dve", lambda e, q_=q_, src_=src_, dset=dset: e.tensor_tensor(
                                        out=dset[:, q_], in0=idb, in1=src_.unsqueeze(2).to_broadcast([128, 4, 128]), op=ALU.mult),
                                        reads=[ident, lp, nlpi], writes=[dset])
                                bkr, bki = bank(), bank()
                                for st_ in range(4):
                                    sl_ = slice(st_ * 128, (st_ + 1) * 128)
                                    c.op("pe", lambda e, st_=st_, sl_=sl_: e.matmul(bkr[:, sl_], lhsT=dset[:, 0, st_, :], rhs=G[:, 0, st_, 0, :],
                                                                                   start=True, stop=False), reads=[dset, G], writes=[bkr])
                                    c.op("pe", lambda e, st_=st_, sl_=sl_: e.matmul(bkr[:, sl_], lhsT=dset[:, 1, st_, :], rhs=G[:, 0, st_, 1, :],
                                                                                   start=False, stop=True), reads=[dset, G], writes=[bkr])
                                for st_ in range(4):
                                    sl_ = slice(st_ * 128, (st_ + 1) * 128)
                                    c.op("pe", lambda e, st_=st_, sl_=sl_: e.matmul(bki[:, sl_], lhsT=dset[:, 0, st_, :], rhs=G[:, 0, st_, 1, :],
                                                                                   start=True, stop=False), reads=[dset, G], writes=[bki])
                                    c.op("pe", lambda e, st_=st_, sl_=sl_: e.matmul(bki[:, sl_], lhsT=dset[:, 2, st_, :], rhs=G[:, 0, st_, 0, :],
                                                                                   start=False, stop=True), reads=[dset, G], writes=[bki])
                                c.op("act", lambda e, k=k: e.activation(out=G[:, k, :, 0, :], in_=bkr[:].rearrange("p (a b) -> p a b", a=4),
                                                                        func=AF.Identity), reads=[bkr], writes=[G])
                                c.op("act", lambda e, k=k: e.activation(out=G[:, k, :, 1, :], in_=bki[:].rearrange("p (a b) -> p a b", a=4),
                                                                        func=AF.Identity), reads=[bki], writes=[G])
                            if k <= 7:
                                i = 7 - k
                                bkr, bki = bank(), bank()
                                for st_ in range(4):
                                    sl_ = slice(st_ * 128, (st_ + 1) * 128)
                                    if k == 0:
                                        c.op("pe", lambda e, st_=st_, sl_=sl_: e.matmul(bkr[:, sl_], lhsT=BT[:, st_, 0, :], rhs=identb[:],
                                                                                       start=True, stop=True), reads=[BT, identb], writes=[bkr])
                                    else:
                                        c.op("pe", lambda e, st_=st_, sl_=sl_: e.matmul(bkr[:, sl_], lhsT=BT[:, st_, 0, :], rhs=dset[:, 0, st_, :],
                                                                                       start=True, stop=False), reads=[BT, dset], writes=[bkr])
                                        c.op("pe", lambda e, st_=st_, sl_=sl_: e.matmul(bkr[:, sl_], lhsT=BT[:, st_, 1, :], rhs=dset[:, 2, st_, :],
                                                                                       start=False, stop=True), reads=[BT, dset], writes=[bkr])
                                for st_ in range(4):
                                    sl_ = slice(st_ * 128, (st_ + 1) * 128)
                                    if k == 0:
                                        c.op("pe", lambda e, st_=st_, sl_=sl_: e.matmul(bki[:, sl_], lhsT=BT[:, st_, 1, :], rhs=identb[:],
                                                                                       start=True, stop=True), reads=[BT, identb], writes=[bki])
                                    else:
                                        c.op("pe", lambda e, st_=st_, sl_=sl_: e.matmul(bki[:, sl_], lhsT=BT[:, st_, 0, :], rhs=dset[:, 1, st_, :],
                                                                                       start=True, stop=False), reads=[BT, dset], writes=[bki])
                                        c.op("pe", lambda e, st_=st_, sl_=sl_: e.matmul(bki[:, sl_], lhsT=BT[:, st_, 1, :], rhs=dset[:, 0, st_, :],
                                                                                       start=False, stop=True), reads=[BT, dset], writes=[bki])
                                c.op("act", lambda e, i=i: e.activation(out=WI[:, i, 0].rearrange("p a b -> p (a b)"), in_=bkr[:], func=AF.Identity),
                                     reads=[bkr], writes=[WI])
                                c.op("act", lambda e, i=i: e.activation(out=WI[:, i, 1].rearrange("p a b -> p (a b)"), in_=bki[:], func=AF.Identity),
                                     reads=[bki], writes=[WI])
                        for hb in range(2):
                            bk = bank()
                            for tq in range(4):
                                tau = hb * 4 + tq
                                i_ = 0
                                for st_ in range(4):
                                    for ri in range(2):
                                        c.op("pe", lambda e, tq=tq, tau=tau, st_=st_, ri=ri, i_=i_, bk=bk: e.matmul(
                                            bk[:, tq * 128:(tq + 1) * 128], lhsT=BT[:, st_, ri, :], rhs=G[:, tau, st_, ri, :],
                                            start=(i_ == 0), stop=(i_ == 7)), reads=[BT, G], writes=[bk])
                                        i_ += 1
                            c.op("act", lambda e, bk=bk, hb=hb: e.activation(
                                out=KT[:, hb * 4:hb * 4 + 4, :], in_=bk[:].rearrange("p (a b) -> p a b", a=4), func=AF.Identity),
                                reads=[bk], writes=[KT])
                        if bstage is not None and bstage < 3:
                            continue
                        c.op("pool", lambda e: e.memset(carr[:], 0.0), writes=[carr])
                        c.op("pool", lambda e: e.memset(SBr[:, :, 0:1], 0.0), writes=[SBr])
                        c.op("pool", lambda e: e.memset(SBi[:, :, 0:1], 0.0), writes=[SBi])
                        X_ = XD[d_]
                        for u_ in range(4):
                            m0 = u_ * 128
                            PSr, PSi = bank(), bank()
                            for ri, PS_ in ((0, PSr), (1, PSi)):
                                for st_ in range(4):
                                    for i in range(8):
                                        c.op("pe", lambda e, ri=ri, PS_=PS_, st_=st_, i=i: e.matmul(
                                            PS_[:, st_ * 128:(st_ + 1) * 128], lhsT=WI[:, i, ri, st_, :], rhs=X_[:, i, m0:m0 + 128],
                                            start=(i == 0), stop=(i == 7)), reads=[WI, X_], writes=[PS_])
                            pr = PSr[:].rearrange("p (s j) -> p s j", s=4)
                            pi_ = PSi[:].rearrange("p (s j) -> p s j", s=4)
                            T1, T2, T3, T4, BR, BI, SR, SI = TM
                            c.op("dve", lambda e: e.tensor_tensor(out=T1[:], in0=pr, in1=ctab[:], op=ALU.mult), reads=[PSr, ctab], writes=[T1])
                            c.op("dve", lambda e: e.tensor_tensor(out=T2[:], in0=pi_, in1=stab[:], op=ALU.mult), reads=[PSi, stab], writes=[T2])
                            c.op("pool", lambda e: e.tensor_tensor(out=BR[:], in0=T1[:], in1=T2[:], op=ALU.add), reads=[T1, T2], writes=[BR])
                            c.op("dve", lambda e: e.tensor_tensor(out=T3[:], in0=pi_, in1=ctab[:], op=ALU.mult), reads=[PSi, ctab], writes=[T3])
                            c.op("dve", lambda e: e.tensor_tensor(out=T4[:], in0=pr, in1=stab[:], op=ALU.mult), reads=[PSr, stab], writes=[T4])
                            c.op("pool", lambda e: e.tensor_tensor(out=BI[:], in0=T3[:], in1=T4[:], op=ALU.subtract), reads=[T3, T4], writes=[BI])
                            for (B_, S_, ci) in ((BR, SR, 0), (BI, SI, 1)):
                                for st_ in range(4):
                                    c.op("dve", lambda e, B_=B_, S_=S_, st_=st_, ci=ci: e.tensor_tensor_scan(
                                        out=S_[:, st_, :], data0=prm[:, d_, P_RHO8, ct * 4 + st_:ct * 4 + st_ + 1].to_broadcast([128, 128]), data1=B_[:, st_, :],
                                        initial=carr[:, ci, st_:st_ + 1], op0=ALU.mult, op1=ALU.add),
                                        reads=[B_, prm, carr], writes=[S_])
                            lr_, li_ = SR[:, :, 127], SI[:, :, 127]
                            c7, s7 = ck[:, d_, 10, cs], sk[:, d_, 10, cs]
                            c.op("dve", lambda e: e.tensor_tensor(out=ctmp[:, 0, :], in0=lr_, in1=c7, op=ALU.mult), reads=[SR, ck], writes=[ctmp])
                            c.op("dve", lambda e: e.tensor_tensor(out=ctmp[:, 1, :], in0=li_, in1=s7, op=ALU.mult), reads=[SI, sk], writes=[ctmp])
                            c.op("dve", lambda e: e.tensor_tensor(out=ctmp[:, 2, :], in0=lr_, in1=s7, op=ALU.mult), reads=[SR, sk], writes=[ctmp])
                            c.op("dve", lambda e: e.tensor_tensor(out=ctmp[:, 3, :], in0=li_, in1=c7, op=ALU.mult), reads=[SI, ck], writes=[ctmp])
                            c.op("dve", lambda e: e.tensor_tensor(out=carr[:, 0, :], in0=ctmp[:, 0, :], in1=ctmp[:, 1, :], op=ALU.subtract),
                                 reads=[ctmp], writes=[carr])
                            c.op("dve", lambda e: e.tensor_tensor(out=carr[:, 1, :], in0=ctmp[:, 2, :], in1=ctmp[:, 3, :], op=ALU.add),
                                 reads=[ctmp], writes=[carr])
                            R1, R2, R3, R4 = T1, T2, T3, T4
                            c.op("pool", lambda e: e.tensor_tensor(out=R1[:], in0=SR[:], in1=ctab[:], op=ALU.mult), reads=[SR, ctab], writes=[R1])
                            c.op("pool", lambda e: e.tensor_tensor(out=R2[:], in0=SI[:], in1=stab[:], op=ALU.mult), reads=[SI, stab], writes=[R2])
                            c.op("dve", lambda e: e.tensor_tensor(out=SBr[:, :, 1 + m0:1 + m0 + 128], in0=R1[:], in1=R2[:], op=ALU.subtract),
                                 reads=[R1, R2], writes=[SBr])
                            c.op("pool", lambda e: e.tensor_tensor(out=R3[:], in0=SR[:], in1=stab[:], op=ALU.mult), reads=[SR, stab], writes=[R3])
                            c.op("dve", lambda e: e.tensor_tensor(out=R4[:], in0=SI[:], in1=ctab[:], op=ALU.mult), reads=[SI, ctab], writes=[R4])
                            c.op("pool", lambda e: e.tensor_tensor(out=SBi[:, :, 1 + m0:1 + m0 + 128], in0=R3[:], in1=R4[:], op=ALU.add),
                                 reads=[R3, R4], writes=[SBi])
                        if bstage is not None and bstage < 4:
                            continue
                        yv = YACC[:] if d_ == 0 else YACC[:, ::-1]
                        for j in range(8):
                            PY = bank()
                            tot = (j + 1) + 8
                            i_ = 0
                            for i in range(j + 1):
                                c.op("pe", lambda e, i=i, j=j, i_=i_, PY=PY: e.matmul(PY[:, 0:512], lhsT=KT[:, j - i, :], rhs=X_[:, i, :],
                                                                                   start=(i_ == 0), stop=(i_ == tot - 1)),
                                     reads=[KT, X_], writes=[PY])
                                i_ += 1
                            for st_ in range(4):
                                for ri, SB_ in ((0, SBr), (1, SBi)):
                                    c.op("pe", lambda e, st_=st_, ri=ri, SB_=SB_, j=j, i_=i_, PY=PY: e.matmul(
                                        PY[:, 0:512], lhsT=G[:, j + 1, st_, ri, :], rhs=SB_[:, st_, 0:512],
                                        start=(i_ == 0), stop=(i_ == tot - 1)), reads=[G, SB_], writes=[PY])
                                    i_ += 1
                            c.op("dve", lambda e, j=j, PY=PY: e.tensor_tensor(out=yv[:, j::8], in0=PY[:, 0:512], in1=yv[:, j::8], op=ALU.add),
                                 reads=[PY, YACC], writes=[YACC])
                    for cc in range(8):
                        ysl = YACC[:, cc * 512:(cc + 1) * 512]
                        g_ = TM[cc % 4][:].rearrange("p a b -> p (a b)")
                        gb_ = TM[cc % 4]
                        c.op("act", lambda e, ysl=ysl, g_=g_: e.activation(out=g_, in_=ysl, func=AF.Square), reads=[YACC], writes=[gb_])
                        c.op("dve", lambda e, g_=g_: e.tensor_scalar(out=g_, in0=g_, scalar1=0.044715, scalar2=1.0, op0=ALU.mult, op1=ALU.add),
                             reads=[gb_], writes=[gb_])
                        c.op("pool", lambda e, ysl=ysl, g_=g_: e.tensor_tensor(out=g_, in0=g_, in1=ysl, op=ALU.mult), reads=[gb_, YACC], writes=[gb_])
                        c.op("act", lambda e, g_=g_: e.activation(out=g_, in_=g_, func=AF.Sigmoid, scale=1.5957691216057308), reads=[gb_], writes=[gb_])
                        c.op("dve", lambda e, ysl=ysl, g_=g_: e.tensor_tensor(out=ysl, in0=ysl, in1=g_, op=ALU.mult), reads=[gb_, YACC], writes=[YACC])
                    c.op("pool", lambda e: e.tensor_copy(out=YGB[:], in_=YACC[:]), reads=[YACC], writes=[YGB])
                    c.dma(ygT_d[ct * 128:(ct + 1) * 128, :], YACC[:], reads=[YACC], writes=[dram_misc], sem_buf=YACC, store=True)
                    c.dma(ygb_d[ct * 128:(ct + 1) * 128, :], YGB[:], reads=[YGB], writes=[dram_misc], sem_buf=YGB, store=True)
                c.barrier()
                c.release([wc_, spm, prm, ck, sk, lp, hpi, ident, identb, nlpi, st4, ctab, stab, ttmp, BT, G, WI, KT, YACC,
                           SBr, SBi, carr, ctmp, YGB] + XD + TM + dg)
            if 'C' not in skip:
                with ExitStack() as ph:
                    ygb = c.sbuf("ygb", [128, 4, T], BF16, ph)
                    wgl = c.sbuf("wgl", [128, 4, 128], BF16, ph)
                    wcg = c.sbuf("wcg", [128, 8, 128], BF16, ph)
                    SGL = [c.sbuf("SGL%d" % i, [128, 1024], F32, ph) for i in range(2)]
                    SGC = [c.sbuf("SGC%d" % i, [128, 1024], F32, ph) for i in range(2)]
                    YGS = [c.sbuf("YGS%d" % i, [128, 1024], F32, ph) for i in range(2)]
                    YSC = [c.sbuf("YSC%d" % i, [128, 1024], BF16, ph) for i in range(2)]
                    c.dma(ygb[:], ygb_d.rearrange("(k p) t -> p k t", p=128), reads=[dram_misc], writes=[ygb], sem_buf=ygb)
                    for ct in range(4):
                        load_w(wgl, wgl[:], gluw_d[L, :, ct * 128:(ct + 1) * 128].rearrange("(kt p) n -> p kt n", p=128), 4, 128)
                        load_w(wcg, wcg[:], win_cols(L, 7 * W + ct * 128, 128), 8, 128, fold_g=True)
                        for si in range(4):
                            t0 = si * 1024
                            sgl, sgc, ygs, ysc = SGL[si % 2], SGC[si % 2], YGS[si % 2], YSC[si % 2]
                            c.dma(ygs[:], ygT_d[ct * 128:(ct + 1) * 128, t0:t0 + 1024], reads=[dram_misc], writes=[ygs], sem_buf=ygs)

                            def ev_l(bk, a, n, sgl=sgl, t0=t0):
                                c.op("act", lambda e: e.activation(out=sgl[:, a - t0:a - t0 + n], in_=bk[:, 0:n], func=AF.Sigmoid,
                                                                   bias=chp[:, V_GLUB, ct:ct + 1]), reads=[bk, chp], writes=[sgl])

                            def ev_c(bk, a, n, sgc=sgc, t0=t0):
                                c.op("act", lambda e: e.activation(out=sgc[:, a - t0:a - t0 + n], in_=bk[:, 0:n], func=AF.Silu),
                                     reads=[bk], writes=[sgc])
                            mm_chunks([(wgl, (lambda kt: wgl[:, kt, :]), 4, (lambda kt, a, n: ygb[:, kt, a:a + n]), ygb)], t0, 1024, ev_l)
                            mm_chunks([(wcg, (lambda kt: wcg[:, kt, :]), 8, hn_rhs(0), hn)], t0, 1024, ev_c)
                            c.op("dve", lambda e, ygs=ygs, sgl=sgl: e.tensor_tensor(out=ygs[:], in0=ygs[:], in1=sgl[:], op=ALU.mult),
                                 reads=[ygs, sgl], writes=[ygs])
                            if dbg:
                                c.dma(dbg_d[2, ct * 128:(ct + 1) * 128, t0:t0 + 1024], ygs[:], reads=[ygs], writes=[dram_misc],
                                      sem_buf=ygs, store=True)
                            c.op("pool", lambda e, ygs=ygs, sgc=sgc, ysc=ysc: e.tensor_tensor(out=ysc[:], in0=ygs[:], in1=sgc[:], op=ALU.mult),
                                 reads=[ygs, sgc], writes=[ysc])
                            c.dma(ysT_d[2, ct * 128:(ct + 1) * 128, t0:t0 + 1024], ysc[:], reads=[ysc], writes=[dram_ys],
                                  sem_buf=ysc, store=True)
                    c.barrier()
                    c.release([ygb, wgl, wcg] + SGL + SGC + YGS + YSC)
            if stop_after == 'C':
                break
            with ExitStack() as ph:
                wqs = [c.sbuf("wq%d" % i, [128, 8, 128], BF16, ph) for i in range(2)]
                wks = [c.sbuf("wk%d" % i, [128, 8, 128], BF16, ph) for i in range(2)]
                wvs = [c.sbuf("wv%d" % i, [128, 8, 128], BF16, ph) for i in range(2)]
                wbgs = [c.sbuf("wbg%d" % i, [128, 8, 128], BF16, ph) for i in range(2)]

                def loadB(h_):
                    for lst, col in ((wqs, 2), (wks, 3), (wvs, 4), (wbgs, 5)):
                        load_w(lst[h_ % 2], lst[h_ % 2][:], win_cols(L, col * W + h_ * 128, 128), 8, 128)
                if 'B' not in skip:
                    loadB(0)
                qkg = c.sbuf("qkg", [128, 2], F32, ph)
                eps2 = c.sbuf("eps2", [128, 1], F32, ph)
                QZ = c.sbuf("QZ", [128, 64, 2, 64], BF16, ph)
                KN = c.sbuf("KN", [128, T], BF16, ph)
                Zf = c.sbuf("Zf", [128, 1024], F32, ph)
                SQb = c.sbuf("SQb", [128, 1024], BF16, ph)
                RS = c.sbuf("RS", [128, 1024], F32, ph)
                Vs = [c.sbuf("VA%d" % i, [128, 32, 2, 65], BF16, ph) for i in range(2)]
                identf = c.sbuf("identf", [128, 128], F32, ph)
                Yn = [c.sbuf("Yn%d" % i, [64, 2, 64], F32, ph) for i in range(2)]
                rec = [c.sbuf("rec%d" % i, [64, 2], F32, ph) for i in range(2)]
                bmk = c.sbuf("bmk", [128, 8, 512], F32, ph)
                ES = [c.sbuf("ES%d" % i, [128, 512], F32, ph) for i in range(2)]
                EB = [c.sbuf("EB%d" % i, [128, 512], BF16, ph) for i in range(2)]
                RD = c.sbuf("RD", [128, 512], F32, ph)
                SGF = c.sbuf("SGF", [128, T], F32, ph)
                YB = [c.sbuf("YB%d" % i, [128, 512], F32, ph) for i in range(2)]
                SGb = [c.sbuf("SGb%d" % i, [128, 512], F32, ph) for i in range(2)]
                YSb = [c.sbuf("YSb%d" % i, [128, 512], BF16, ph) for i in range(2)]
                for hp in (range(4) if 'B' not in skip else []):
                    wq, wk, wv, wbg = wqs[hp % 2], wks[hp % 2], wvs[hp % 2], wbgs[hp % 2]
                    if hp + 1 < 4:
                        loadB(hp + 1)
                    c.dma(qkg[:], qkg_d[L], writes=[qkg], sem_buf=qkg)
                    c.dma(bmk[:], bm_d[L, hp].rearrange("p c h x -> p c (h x)"), writes=[bmk], sem_buf=bmk)
                    c.op("pool", lambda e: e.memset(eps2[:], 1e-6), writes=[eps2])
                    c.op("dve", lambda e: e.tensor_scalar(out=qkg[:, 0:1], in0=qkg[:, 0:1], scalar1=0.125, scalar2=None, op0=ALU.mult),
                         reads=[qkg], writes=[qkg])
                    c.op("pool", lambda e: e.memset(QZ[64:128, :, 0, :], 0.0), writes=[QZ])
                    c.op("pool", lambda e: e.memset(QZ[0:64, :, 1, :], 0.0), writes=[QZ])
                    for (w_, gi_) in ((wq, 0), (wk, 1)):
                        for si in range(4):
                            t0 = si * 1024

                            def ev_z(bk, a, n, t0=t0):
                                o = a - t0
                                c.op("act", lambda e: e.activation(out=Zf[:, o:o + n], in_=bk[:, 0:n], func=AF.Identity),
                                     reads=[bk], writes=[Zf])
                                c.op("act", lambda e: e.activation(out=SQb[:, o:o + n], in_=bk[:, 0:n], func=AF.Square),
                                     reads=[bk], writes=[SQb])
                            mm_chunks([(w_, (lambda kt, w_=w_: w_[:, kt, :]), 8, hn_rhs(0), hn)], t0, 1024, ev_z)

                            def ev_r(bk, a, n, t0=t0):
                                o = a - t0
                                c.op("act", lambda e: e.activation(out=RS[:, o:o + n], in_=bk[:, 0:n], func=AF.Ln, scale=1.0 / 64,
                                                                   bias=eps2[:, 0:1]), reads=[bk, eps2], writes=[RS])
                            mm_chunks([(blk1, (lambda kt: blk1[:]), 1, (lambda kt, a, n, t0=t0: SQb[:, a - t0:a - t0 + n]), SQb)],
                                      t0, 1024, ev_r)
                            c.op("act", lambda e: e.activation(out=RS[:], in_=RS[:], func=AF.Exp, scale=-0.5), reads=[RS], writes=[RS])
                            if gi_ == 1:
                                c.op("dve", lambda e, t0=t0: e.scalar_tensor_tensor(
                                    out=KN[:, t0:t0 + 1024], in0=Zf[:], scalar=qkg[:, 1:2], in1=RS[:], op0=ALU.mult, op1=ALU.mult),
                                    reads=[Zf, qkg, RS], writes=[KN])
                            else:
                                for h2 in range(2):
                                    ps_ = slice(h2 * 64, (h2 + 1) * 64)
                                    c.op("dve", lambda e, t0=t0, h2=h2, ps_=ps_: e.scalar_tensor_tensor(
                                        out=QZ[ps_, t0 // 64:t0 // 64 + 16, h2, :], in0=Zf[ps_, :].rearrange("p (r q) -> p r q", q=64),
                                        scalar=qkg[ps_, 0:1], in1=RS[ps_, :].rearrange("p (r q) -> p r q", q=64),
                                        op0=ALU.mult, op1=ALU.mult), reads=[Zf, qkg, RS], writes=[QZ])
                    if hp == 0:
                        c.dma(identf[:], ident_d, writes=[identf], sem_buf=identf)
                        for par in range(2):
                            c.op("pool", lambda e, par=par: e.memset(Vs[par][:, :, :, 64:65], 1.0), writes=[Vs[par]])
                    for par in range(2):
                        ntile = 32 - par
                        for tg in range(0, ntile, 4):
                            bk = bank()
                            nt_ = min(4, ntile - tg)
                            for j in range(nt_):
                                tt_ = tg + j
                                for kt in range(8):
                                    c.op("pe", lambda e, j=j, tt_=tt_, kt=kt, bk=bk: e.matmul(
                                        bk[:, j * 128:(j + 1) * 128], lhsT=hn[:, kt, 2 + 64 * par + tt_ * 128:2 + 64 * par + (tt_ + 1) * 128],
                                        rhs=wv[:, kt, :], start=(kt == 0), stop=(kt == 7)), reads=[hn, wv], writes=[bk])
                            c.op("act", lambda e, bk=bk, tg=tg, nt_=nt_: e.activation(
                                out=Vs[par][:, tg:tg + nt_, :, 0:64], in_=bk[:, 0:nt_ * 128].rearrange("p (a h d) -> p a h d", a=nt_, h=2),
                                func=AF.Identity), reads=[bk], writes=[Vs[par]])
                    def ev_bg(bk, a, n):
                        c.op("act", lambda e: e.activation(out=SGF[:, a:a + n], in_=bk[:, 0:n], func=AF.Silu), reads=[bk], writes=[SGF])
                    mm_chunks([(wbg, (lambda kt: wbg[:, kt, :]), 8, hn_rhs(0), hn)], 0, T, ev_bg)
                    state["nb"] = 6
                    nrows = 64 if bstage is None else 8 * bstage

                    def emit_qk(r):
                        rs_ = min(max(r - 4, 0), 56)
                        cfg = r - rs_
                        par = rs_ % 2
                        tt0 = (rs_ - par) // 2
                        k0 = 64 * rs_
                        bS = bank()
                        for kt in range(4):
                            c.op("pe", lambda e, kt=kt: e.matmul(
                                bS[:, kt * 128:(kt + 1) * 128], lhsT=KN[:, k0 + kt * 128:k0 + (kt + 1) * 128],
                                rhs=QZ[:, r, :, :].rearrange("p h q -> p (h q)"), start=True, stop=True),
                                reads=[KN, QZ], writes=[bS])
                        es_, eb_ = ES[r % 2], EB[r % 2]
                        c.op("dve", lambda e: e.tensor_tensor(out=es_[:], in0=bS[:], in1=bmk[:, cfg, :], op=ALU.add),
                             reads=[bS, bmk], writes=[es_])
                        c.op("act", lambda e: e.activation(out=eb_[:], in_=es_[:], func=AF.Exp), reads=[es_], writes=[eb_])
                        return eb_, par, tt0

                    def emit_tr(r):
                        rg, rl = r // 8, r % 8
                        bT = pb[6 + rg % 2]
                        yn_ = Yn[r % 2]
                        c.op("pe", lambda e: e.transpose(bT[:, rl * 64:(rl + 1) * 64], yn_[:].rearrange("p h d -> p (h d)"), identf[0:64, 0:64]),
                             reads=[yn_, identf], writes=[bT])
                        if rl != 7:
                            return
                        t0 = rg * 512
                        yb_, ys_ = YB[rg % 2], YSb[rg % 2]
                        if dbg:
                            c.op("act", lambda e: e.activation(out=yb_[:], in_=bT[:], func=AF.Identity), reads=[bT], writes=[yb_])
                            c.dma(dbg_d[1, hp * 128:(hp + 1) * 128, t0:t0 + 512], yb_[:], reads=[yb_], writes=[dram_misc],
                                  sem_buf=yb_, store=True)
                        c.op("dve", lambda e: e.tensor_tensor(out=ys_[:], in0=bT[:], in1=SGF[:, t0:t0 + 512], op=ALU.mult),
                             reads=[bT, SGF], writes=[ys_])
                        c.dma(ysT_d[1, hp * 128:(hp + 1) * 128, t0:t0 + 512], ys_[:], reads=[ys_], writes=[dram_ys],
                              sem_buf=ys_, store=True)

                    pend = emit_qk(0) if nrows else None
                    prev = None
                    for r in range(nrows):
                        eb_, par, tt0 = pend
                        if r + 1 < nrows:
                            pend = emit_qk(r + 1)
                        bY = bank()
                        for h2 in range(2):
                            for kt in range(4):
                                c.op("pe", lambda e, h2=h2, kt=kt: e.matmul(
                                    bY[0:64, h2 * 65:(h2 + 1) * 65], lhsT=eb_[:, (kt * 2 + h2) * 64:(kt * 2 + h2 + 1) * 64],
                                    rhs=Vs[par][:, tt0 + kt, h2, :], start=(kt == 0), stop=(kt == 3)),
                                    reads=[Vs[par], eb_], writes=[bY])
                        byv = bY[0:64, 0:130].rearrange("p (h c) -> p h c", c=65)
                        rc_, yn_ = rec[r % 2], Yn[r % 2]
                        c.op("dve", lambda e: e.reciprocal(out=rc_[:], in_=byv[:, :, 64]), reads=[bY], writes=[rc_])
                        c.op("dve", lambda e: e.tensor_tensor(out=yn_[:], in0=byv[:, :, 0:64], in1=rc_[:].unsqueeze(2).to_broadcast([64, 2, 64]),
                                                              op=ALU.mult), reads=[bY, rc_], writes=[yn_])
                        if prev is not None:
                            emit_tr(prev)
                        prev = r
                    if prev is not None:
                        emit_tr(prev)
                    state["nb"] = 8
                c.barrier()
                c.release(wqs + wks + wvs + wbgs + [qkg, eps2, KN, Zf, SQb, RS, bmk, RD, SGF, identf] + [QZ] + Vs + ES + EB + YB + SGb + YSb + Yn + rec)
            if stop_after == 'B':
                break
            if stop_after == 'mix':
                break
            TB = 512
            with ExitStack() as ph:
                wmg = c.sbuf("wmg", [128, 8, 4 * D], BF16, ph)
                wbs = c.sbuf("wbs", [128, 4, 4, D], BF16, ph)
                ysb = c.sbuf("ysb", [128, 16, TB], BF16, ph)
                mgb = [c.sbuf("mgb%d" % i, [128, 8, TB], BF16, ph) for i in range(1)]
                acc = c.sbuf("acc", [128, TB], F32, ph)
                sgm = [c.sbuf("sgm%d" % i, [128, TB], F32, ph) for i in range(2)]
                tmpm = [c.sbuf("tmpm%d" % i, [128, TB], F32, ph) for i in range(2)]
                wmg_t = [Buf("wmg_t%d" % i, wmg.t) for i in range(8)]
                wbs_t = [Buf("wbs_t%d" % i, wbs.t) for i in range(8)]
                ysb_t = [Buf("ysb_t%d" % i, ysb.t) for i in range(4)]
                for dt in range(8):
                    for n in range(4):
                        cb = n * 8 + dt
                        load_w(wmg_t[dt], wmg[:, :, cb * 128:(cb + 1) * 128], win_cols(L, 10 * W + cb * 128, 128), 8, 128)
                        load_w(wbs_t[dt], wbs[:, n, :, dt * 128:(dt + 1) * 128],
                               wbr_d[L, n, :, dt * 128:(dt + 1) * 128].rearrange("(kt p) n -> p kt n", p=128), 4, 128)
                for blk in range(T // TB):
                    t0 = blk * TB
                    mg_ = mgb[0]
                    for n in range(4):
                        c.dma(ysb[:, n * 4:(n + 1) * 4, :], ysT_d[n, :, t0:t0 + TB].rearrange("(k p) t -> p k t", p=128),
                              reads=[dram_ys], writes=[ysb_t[n]], sem_buf=ysb_t[n])
                    for dt in range(8):
                        for n in range(4):
                            b1 = bank()
                            for kt in range(8):
                                c.op("pe", lambda e, kt=kt, b1=b1, n=n, dt=dt: e.matmul(
                                    b1[:, 0:TB], lhsT=wmg[:, kt, n * D + dt * 128:n * D + (dt + 1) * 128],
                                    rhs=hn[:, kt, 2 + t0:2 + t0 + TB], start=(kt == 0), stop=(kt == 7)),
                                    reads=[wmg_t[dt], hn], writes=[b1])
                            b2 = bank()
                            for kt in range(4):
                                c.op("pe", lambda e, kt=kt, b2=b2, n=n, dt=dt: e.matmul(
                                    b2[:, 0:TB], lhsT=wbs[:, n, kt, dt * 128:(dt + 1) * 128], rhs=ysb[:, n * 4 + kt, :],
                                    start=(kt == 0), stop=(kt == 3)), reads=[wbs_t[dt], ysb_t[n]], writes=[b2])
                            sg_ = sgm[n % 2]
                            c.op("act", lambda e, sg_=sg_, b1=b1: e.activation(out=sg_[:], in_=b1[:, 0:TB], func=AF.Sigmoid),
                                 reads=[b1], writes=[sg_])
                            if n == 0:
                                c.op("dve", lambda e, sg_=sg_, b2=b2: e.tensor_tensor(out=acc[:], in0=b2[:, 0:TB], in1=sg_[:], op=ALU.mult),
                                     reads=[b2, sg_], writes=[acc])
                            else:
                                tm_ = tmpm[n % 2]
                                c.op("dve", lambda e, sg_=sg_, b2=b2, tm_=tm_: e.tensor_tensor(out=tm_[:], in0=b2[:, 0:TB], in1=sg_[:], op=ALU.mult),
                                     reads=[b2, sg_], writes=[tm_])
                                if n < 3:
                                    c.op("pool", lambda e, tm_=tm_: e.tensor_tensor(out=acc[:], in0=acc[:], in1=tm_[:], op=ALU.add),
                                         reads=[acc, tm_], writes=[acc])
                                else:
                                    c.op("pool", lambda e, tm_=tm_, dt=dt, mg_=mg_: e.tensor_tensor(out=mg_[:, dt, :], in0=acc[:], in1=tm_[:], op=ALU.add),
                                         reads=[acc, tm_], writes=[mg_])
                    c.dma(mgT_d[:, t0:t0 + TB].rearrange("(k p) t -> p k t", p=128), mg_[:], reads=[mg_], writes=[dram_mg],
                          sem_buf=mg_, store=True)
                c.barrier()
                c.release([wmg, wbs, ysb, acc] + mgb + sgm + tmpm + wmg_t + wbs_t + ysb_t)
            with ExitStack() as ph:
                wo_sb = c.sbuf("wo_sb", [128, 8, D], BF16, ph)
                pg_sb = c.sbuf("pg_sb", [128, 8, D], BF16, ph)
                pp_sb = c.sbuf("pp_sb", [128, 2, D], BF16, ph)
                mgi = [c.sbuf("mgi%d" % i, [128, 8, TB], BF16, ph) for i in range(2)]
                xk8 = [c.sbuf("xk8%d" % i, [128, 8, TB], F32, ph) for i in range(2)]
                x1b = c.sbuf("x1b", [128, 8, TB], BF16, ph)
                ptf = c.sbuf("ptf", [128, 2, TB], F32, ph)
                ptb = c.sbuf("ptb", [128, 2, TB], BF16, ph)
                sg2 = [c.sbuf("sg2%d" % i, [128, TB], F32, ph) for i in range(2)]
                wo_t = [Buf("wo_t%d" % i, wo_sb.t) for i in range(8)]
                pg_t = [Buf("pg_t%d" % i, pg_sb.t) for i in range(8)]
                pp_t = [Buf("pp_t%d" % i, pp_sb.t) for i in range(8)]
                x1bs = [x1b, c.sbuf("x1b2", [128, 8, TB], BF16, ph)]
                ptbs = [ptb, c.sbuf("ptb2", [128, 2, TB], BF16, ph)]
                for db in range(8):
                    sl = slice(db * 128, (db + 1) * 128)
                    load_w(wo_t[db], wo_sb[:, :, sl], wout_d[L, :, sl].rearrange("(kt p) n -> p kt n", p=128), 8, 128)
                for db in range(8):
                    sl = slice(db * 128, (db + 1) * 128)
                    load_w(pg_t[db], pg_sb[:, :, sl], pgate_d[L, :, sl].rearrange("(kt p) n -> p kt n", p=128), 8, 128)
                    load_w(pp_t[db], pp_sb[:, :, sl], pproj_d[L, :, sl].rearrange("(kt p) n -> p kt n", p=128), 2, 128)
                nblk = T // TB

                def m2_stage1(blk):
                    t0 = blk * TB
                    mg_, x1, x1b_, ptb_ = mgi[blk % 2], xk8[blk % 2], x1bs[blk % 2], ptbs[blk % 2]
                    c.dma(mg_[:], mgT_d[:, t0:t0 + TB].rearrange("(k p) t -> p k t", p=128), reads=[dram_mg], writes=[mg_], sem_buf=mg_)
                    c.dma(ptf[:], pT_d[L, :, t0:t0 + TB].rearrange("(k p) t -> p k t", p=128), writes=[ptf], sem_buf=ptf)
                    c.dma(x1[:], xsrc_d[:, t0:t0 + TB].rearrange("(k p) t -> p k t", p=128), reads=([dram_x1] if L > 0 else []), writes=[x1], sem_buf=x1)
                    c.op("pool", lambda e: e.tensor_copy(out=ptb_[:], in_=ptf[:]), reads=[ptf], writes=[ptb_])
                    for dt in range(8):
                        b1 = bank()
                        for kt in range(8):
                            c.op("pe", lambda e, kt=kt, b1=b1, dt=dt: e.matmul(
                                b1[:, 0:TB], lhsT=wo_sb[:, kt, dt * 128:(dt + 1) * 128], rhs=mg_[:, kt, :],
                                start=(kt == 0), stop=(kt == 7)), reads=[wo_t[dt], mg_], writes=[b1])
                        c.op("dve", lambda e, dt=dt, b1=b1: e.tensor_tensor(out=x1[:, dt, :], in0=b1[:, 0:TB], in1=x1[:, dt, :], op=ALU.add),
                             reads=[b1, x1], writes=[x1])
                        c.op("pool", lambda e, dt=dt: e.tensor_copy(out=x1b_[:, dt, :], in_=x1[:, dt, :]), reads=[x1], writes=[x1b_])

                def m2_stage2(blk):
                    t0 = blk * TB
                    x1, x1b_, ptb_ = xk8[blk % 2], x1bs[blk % 2], ptbs[blk % 2]
                    for dt in range(8):
                        b1 = bank()
                        for kt in range(8):
                            c.op("pe", lambda e, kt=kt, b1=b1, dt=dt: e.matmul(
                                b1[:, 0:TB], lhsT=pg_sb[:, kt, dt * 128:(dt + 1) * 128], rhs=x1b_[:, kt, :],
                                start=(kt == 0), stop=(kt == 7)), reads=[pg_t[dt], x1b_], writes=[b1])
                        b2 = bank()
                        for kt in range(2):
                            c.op("pe", lambda e, kt=kt, b2=b2, dt=dt: e.matmul(
                                b2[:, 0:TB], lhsT=pp_sb[:, kt, dt * 128:(dt + 1) * 128], rhs=ptb_[:, kt, :],
                                start=(kt == 0), stop=(kt == 1)), reads=[pp_t[dt], ptb_], writes=[b2])
                        sg_ = sg2[dt % 2]
                        c.op("act", lambda e, sg_=sg_, b1=b1: e.activation(out=sg_[:], in_=b1[:, 0:TB], func=AF.Sigmoid),
                             reads=[b1], writes=[sg_])
                        c.op("dve", lambda e, sg_=sg_, b2=b2: e.tensor_tensor(out=sg_[:], in0=b2[:, 0:TB], in1=sg_[:], op=ALU.mult),
                             reads=[b2, sg_], writes=[sg_])
                        c.op("pool", lambda e, sg_=sg_, dt=dt: e.tensor_tensor(out=x1[:, dt, :], in0=sg_[:], in1=x1[:, dt, :], op=ALU.add),
                             reads=[sg_, x1], writes=[x1])
                    c.dma(xdst_d[:, t0:t0 + TB].rearrange("(k p) t -> p k t", p=128), x1[:], reads=[x1],
                          writes=[dram_out if L == NL - 1 else dram_x1], sem_buf=x1, store=True)

                m2_stage1(0)
                for blk in range(nblk):
                    if blk + 1 < nblk:
                        m2_stage1(blk + 1)
                    m2_stage2(blk)
                c.barrier()
                c.release([wo_sb, pg_sb, pp_sb, ptf] + x1bs + ptbs + mgi + xk8 + sg2 + wo_t + pg_t + pp_t)
            if stop_after == 'L0':
                break

        c.barrier()
    return nc


def _prep_shared(inp):
    f = np.float32
    sh = {}
    sh["gl"] = np.ascontiguousarray(inp["norm_scale"].reshape(NL, 8, 128).transpose(0, 2, 1)).astype(f)
    sh["w_in"] = np.ascontiguousarray(inp["w_in"]).astype(f)
    sh["cwbc"] = np.ascontiguousarray(np.broadcast_to(inp["lru_conv_w"][:, None], (NL, 128, 4, W))).astype(f)

    def pc(v):
        return v.reshape(NL, 4, 128).transpose(0, 2, 1)
    vecs = [inp["lru_conv_b"], inp["lru_b_r"][:, 0], inp["lru_b_r"][:, 1], inp["lru_b_i"][:, 0], inp["lru_b_i"][:, 1],
            inp["lru_lambda"][:, 0], inp["lru_lambda"][:, 1], inp["ssm_d"], inp["ssm_glu_b"], inp["pool_scale"]]
    vecs += [inp["lru_conv_w"][:, k] for k in range(4)]
    sh["chp"] = np.ascontiguousarray(np.stack([pc(v) for v in vecs], axis=2)).astype(f)
    bd = np.zeros((NL, 2, 2, 4, 128, 128), f)
    for gt, key in enumerate(("lru_w_r", "lru_w_i")):
        w = inp[key]
        for ct in range(4):
            for j in range(2):
                bd[:, :, gt, ct, j * 64:(j + 1) * 64, j * 64:(j + 1) * 64] = w[:, :, 2 * ct + j]
    sh["bd"] = bd
    qkg = np.zeros((NL, 128, 2), f)
    qkg[:, :, 0] = np.tile(inp["na_q_gain"], (1, 2))
    qkg[:, :, 1] = np.tile(inp["na_k_gain"], (1, 2))
    sh["qkg"] = qkg
    rpb = inp["na_rel_bias"]
    kk = np.arange(512)
    ki = kk // 64
    kc = kk % 64
    qc = np.arange(64)
    win = np.clip(qc - 8, 0, 48)
    valid = (kc[:, None] >= win[None, :]) & (kc[:, None] < win[None, :] + 16)
    dc = np.clip(kc[:, None] - qc[None, :], -15, 15) + 15
    bm = np.empty((NL, 4, 128, 8, 2, 256), f)
    for cfg in range(8):
        dr = ki - cfg + 7
        ok = valid & (dr[:, None] >= 0) & (dr[:, None] <= 14)
        drc = np.clip(dr, 0, 14)
        g = rpb[:, :, drc[:, None], dc]
        g = np.where(ok[None, None], g, f(-1e30)).astype(f)
        g = g.reshape(NL, 4, 2, 4, 128, 64)
        bm[:, :, :, cfg] = g.transpose(0, 1, 4, 3, 2, 5).reshape(NL, 4, 128, 2, 256)
    sh["bm"] = bm
    def sl(a):
        return a.reshape(NL, 2, 16, 2, 64).transpose(0, 1, 3, 4, 2).reshape(NL, 2, 128, 16)
    ldt = np.broadcast_to(inp["ssm_log_dt"][..., None], (NL, 2, 32, 64))
    sh["ssmp"] = np.ascontiguousarray(np.stack([sl(inp["ssm_a_re"]), sl(inp["ssm_a_im"]), sl(ldt)], axis=3)).astype(f)
    bpad = np.zeros((NL, 2, 16, 2, 128, 128), f)
    cpad = np.zeros((NL, 2, 16, 2, 128, 128), f)
    for ri, (bk, ck) in enumerate((("ssm_b_re", "ssm_c_re"), ("ssm_b_im", "ssm_c_im"))):
        B = inp[bk]
        C = inp[ck]
        for stg in range(16):
            q = stg % 4
            for gl in range(2):
                g_ = 2 * stg + gl
                r0 = 32 * q + gl * 16
                bpad[:, :, stg, ri, r0:r0 + 16, gl * 64:(gl + 1) * 64] = B[:, :, g_].transpose(0, 1, 3, 2)
                cpad[:, :, stg, ri, gl * 64:(gl + 1) * 64, r0:r0 + 16] = C[:, :, g_].transpose(0, 1, 3, 2)
    sh["bpad"] = bpad
    sh["cpad"] = cpad
    sh["bpadT"] = np.ascontiguousarray(bpad.transpose(0, 1, 2, 3, 5, 4))
    sh["ident"] = np.eye(128, dtype=f)
    sh["gluw"] = np.ascontiguousarray(inp["ssm_glu_w"]).astype(f)
    sh["poolw"] = np.ascontiguousarray(inp["pool_w"]).astype(f)
    pe = np.zeros((128, 4, 16), f)
    for gi, w in enumerate((2, 4, 8, 16)):
        t = np.concatenate([np.arange(8), np.arange(T - 8, T)])
        lo = np.clip(t - w // 2, 0, T)
        hi = np.clip(t - w // 2 + w, 0, T)
        pe[:, gi, :] = (1.0 / (hi - lo)).astype(f)[None]
    sh["pedge"] = pe
    sh["wbr"] = np.ascontiguousarray(inp["w_branch"]).astype(f)
    sh["wout"] = np.ascontiguousarray(inp["w_out"]).astype(f)
    sh["pproj"] = np.ascontiguousarray(inp["ple_proj"]).astype(f)
    sh["pgate"] = np.ascontiguousarray(inp["ple_gate"]).astype(f)
    return sh


def _in_maps(inp, cores):
    sh = _prep_shared(inp)
    maps = []
    for b in cores:
        m = dict(sh)
        m["xT"] = np.ascontiguousarray(inp["x"][b].T).astype(np.float32)
        m["pT"] = np.ascontiguousarray(inp["p"][:, b].transpose(0, 2, 1)).astype(np.float32)
        maps.append(m)
    return maps


def kernel(**inputs):
    inp = {k: np.asarray(v) for k, v in inputs.items()}
    nc = build()
    res = run_bass_kernel_spmd(nc, _in_maps(inp, range(8)), core_ids=list(range(8)))
    out = np.stack([np.ascontiguousarray(r["outT"].T) for r in res.results], axis=0)
    return out.astype(np.float32)
```
